# Optimizing a Trainium2 kernel written in Bass

```python
import math
import jax
import jax.numpy as jnp
from jax import lax
import numpy as np

D_MODEL = 2048
BATCH = 2
SEQ = 8192
DEPTH = 4

GRID_W = 64
CTX_LEN = 256
N_MOD = 6
EPS = 1e-6

HEAD_DIM = 128
RET_HEADS = 8
ATT_HEADS = 8
ATT_KV_HEADS = 2
RET_WIDTH = RET_HEADS * HEAD_DIM
ATT_WIDTH = ATT_HEADS * HEAD_DIM
ATT_KV_WIDTH = ATT_KV_HEADS * HEAD_DIM
EVEN_IN = 4 * RET_WIDTH + ATT_WIDTH + 2 * ATT_KV_WIDTH
EVEN_OUT = RET_WIDTH + ATT_WIDTH
RET_CHUNK = 128
Q_BLOCK = 128
ROPE_THETA = 10000.0
AXIS_DIM = HEAD_DIM // 2

SSM_EXPAND = 2
D_INNER = SSM_EXPAND * D_MODEL
SSM_HEAD_DIM = 64
SSM_HEADS = D_INNER // SSM_HEAD_DIM
D_STATE = 128
N_GROUPS = 8
HEADS_PER_GROUP = SSM_HEADS // N_GROUPS
CONV_DIM = D_INNER + 2 * N_GROUPS * D_STATE
D_CONV = 5
ODD_IN = D_INNER + CONV_DIM + 2 * SSM_HEADS
SSD_CHUNK = 128

PEER_HEADS = 8
N_KEYS = 128
N_EXPERTS = N_KEYS * N_KEYS
PEER_TOPK = 16
PEER_QDIM = 256
PEER_HALF = PEER_QDIM // 2
PEER_BLOCK = 128

kernel_name = 'hybrid_retention_gqa_ssd_peer_dit'


def rmsnorm(x, w):
    xf = x.astype(jnp.float32)
    y = xf * lax.rsqrt(jnp.mean(xf * xf, axis=-1, keepdims=True) + EPS)
    return (y * w.astype(jnp.float32)).astype(x.dtype)


def modulate(h, shift, scale):
    return h * (1 + scale[:, None, :]) + shift[:, None, :]


def flip(a):
    return jnp.flip(a, axis=1)


def to_chunks(a, size):
    b, t = a.shape[:2]
    return jnp.moveaxis(a.reshape((b, t // size, size) + a.shape[2:]), 1, 0)


def from_chunks(a):
    a = jnp.moveaxis(a, 0, 1)
    return a.reshape((a.shape[0], a.shape[1] * a.shape[2]) + a.shape[3:])


def rope_tables(n_tokens):
    rows = n_tokens // GRID_W
    row = jnp.repeat(jnp.arange(rows, dtype=jnp.float32), GRID_W)
    col = jnp.tile(jnp.arange(GRID_W, dtype=jnp.float32), rows)
    inv = ROPE_THETA ** (-jnp.arange(0, AXIS_DIM, 2, dtype=jnp.float32) / AXIS_DIM)
    ang_r = row[:, None] * inv[None, :]
    ang_c = col[:, None] * inv[None, :]
    return (jnp.cos(ang_r), jnp.sin(ang_r), jnp.cos(ang_c), jnp.sin(ang_c))


def _rotate(x, cos, sin):
    x1, x2 = jnp.split(x, 2, axis=-1)
    cos = cos[None, :, None, :]
    sin = sin[None, :, None, :]
    return jnp.concatenate([x1 * cos - x2 * sin, x1 * sin + x2 * cos], axis=-1)


def apply_rope2d(x, tabs):
    cos_r, sin_r, cos_c, sin_c = tabs
    xf = x.astype(jnp.float32)
    out = jnp.concatenate([_rotate(xf[..., :AXIS_DIM], cos_r, sin_r),
                           _rotate(xf[..., AXIS_DIM:], cos_c, sin_c)], axis=-1)
    return out.astype(x.dtype)


def retention_scan(q, k, v, log_gamma, state0):
    size = RET_CHUNK
    idx = jnp.arange(size, dtype=jnp.float32)
    diff = idx[:, None] - idx[None, :]
    lower = diff >= 0
    decay_mat = jnp.where(lower[None], jnp.exp(jnp.where(lower, diff, 0.0)[None] * log_gamma[:, None, None]), 0.0)
    q_decay = jnp.exp((idx + 1.0)[:, None] * log_gamma[None, :])
    k_decay = jnp.exp((size - 1.0 - idx)[:, None] * log_gamma[None, :])
    chunk_decay = jnp.exp(size * log_gamma)

    def step(state, inp):
        qc, kc, vc = inp
        scores = jnp.einsum('bihd,bjhd->bhij', qc, kc) * decay_mat[None]
        inner = jnp.einsum('bhij,bjhe->bihe', scores, vc)
        cross = jnp.einsum('bihd,bhde->bihe', qc, state) * q_decay[None, :, :, None]
        state = chunk_decay[None, :, None, None] * state + jnp.einsum(
            'bjhd,bjhe->bhde', kc * k_decay[None, :, :, None], vc)
        return state, inner + cross

    state, out = lax.scan(step, state0, (to_chunks(q, size), to_chunks(k, size), to_chunks(v, size)))
    return from_chunks(out), state


def bidir_retention(q, k, v, log_gamma, s_f, s_b):
    o_f, s_f = retention_scan(q, k, v, log_gamma[0], s_f)
    o_b, s_b = retention_scan(flip(q), flip(k), flip(v), log_gamma[1], s_b)
    return o_f + flip(o_b), s_f, s_b


def retention_readout(o, g):
    mu = jnp.mean(o, axis=-1, keepdims=True)
    var = jnp.mean(jnp.square(o - mu), axis=-1, keepdims=True)
    o = (o - mu) * lax.rsqrt(var + EPS)
    b, t = o.shape[:2]
    return (o.reshape(b, t, RET_WIDTH) * jax.nn.silu(g.astype(jnp.float32))).astype(g.dtype)


def block_attention(q, k, v):
    b, t = q.shape[:2]
    groups = ATT_HEADS // ATT_KV_HEADS
    scale = HEAD_DIM ** -0.5
    kf = k.astype(jnp.float32)
    vf = v.astype(jnp.float32)
    qb = to_chunks(q.reshape(b, t, ATT_KV_HEADS, groups, HEAD_DIM), Q_BLOCK)

    def one(qi):
        s = jnp.einsum('bqkgd,btkd->bkgqt', qi.astype(jnp.float32), kf) * scale
        p = jax.nn.softmax(s, axis=-1)
        return jnp.einsum('bkgqt,btkd->bqkgd', p, vf).astype(q.dtype)

    o = from_chunks(lax.map(one, qb))
    return o.reshape(b, t, ATT_WIDTH)


def retention_attention_mixer(h, hc, w_in, w_out, ret_decay, q_norm_w, k_norm_w, rope, need_ctx):
    splits = [RET_WIDTH, 2 * RET_WIDTH, 3 * RET_WIDTH, 4 * RET_WIDTH,
              4 * RET_WIDTH + ATT_WIDTH, 4 * RET_WIDTH + ATT_WIDTH + ATT_KV_WIDTH]

    def project(u):
        p = u @ w_in
        b, t = p.shape[:2]
        rq, rk, rv, rg, aq, ak, av = jnp.split(p, splits, axis=-1)
        heads = lambda a, n: a.reshape(b, t, n, HEAD_DIM)
        return (heads(rq, RET_HEADS), heads(rk, RET_HEADS) * (HEAD_DIM ** -0.5), heads(rv, RET_HEADS), rg,
                rmsnorm(heads(aq, ATT_HEADS), q_norm_w), rmsnorm(heads(ak, ATT_KV_HEADS), k_norm_w),
                heads(av, ATT_KV_HEADS))

    lq, lk, lv, lg, laq, lak, lav = project(h)
    cq, ck, cv, cg, caq, cak, cav = project(hc)
    lq, lk, laq, lak = [apply_rope2d(a, rope) for a in (lq, lk, laq, lak)]
    log_gamma = -jnp.exp(ret_decay.astype(jnp.float32))
    f = lambda a: a.astype(jnp.float32)
    b = h.shape[0]
    zero = jnp.zeros((b, RET_HEADS, HEAD_DIM, HEAD_DIM), jnp.float32)
    c_ret, s_f, s_b = bidir_retention(f(cq), f(ck), f(cv), log_gamma, zero, zero)
    l_ret, _, _ = bidir_retention(f(lq), f(lk), f(lv), log_gamma, s_f, s_b)
    l_att = block_attention(laq, jnp.concatenate([cak, lak], axis=1), jnp.concatenate([cav, lav], axis=1))
    y = jnp.concatenate([retention_readout(l_ret, lg), l_att], axis=-1) @ w_out
    if not need_ctx:
        return y, None
    c_att = block_attention(caq, cak, cav)
    yc = jnp.concatenate([retention_readout(c_ret, cg), c_att], axis=-1) @ w_out
    return y, yc


def depthwise_conv(u, w, bias):
    out = lax.conv_general_dilated(u, w[:, None, :], window_strides=(1,),
                                   padding=[(D_CONV // 2, D_CONV // 2)],
                                   dimension_numbers=('NWC', 'WIO', 'NWC'),
                                   feature_group_count=CONV_DIM)
    return out + bias


def ssd_scan(xs, dt, a, bm, cm, state0):
    size = SSD_CHUNK
    idx = jnp.arange(size)
    lower = idx[:, None] >= idx[None, :]
    a_g = a.reshape(N_GROUPS, HEADS_PER_GROUP)

    def step(state, inp):
        xc, dtc, bc, cc = inp
        bsz = xc.shape[0]
        xg = xc.reshape(bsz, size, N_GROUPS, HEADS_PER_GROUP, SSM_HEAD_DIM)
        dtg = dtc.reshape(bsz, size, N_GROUPS, HEADS_PER_GROUP)
        cs = jnp.cumsum(dtg * a_g, axis=1)
        seg = cs[:, :, None] - cs[:, None, :]
        m = lower[None, :, :, None, None]
        lmat = jnp.where(m, jnp.exp(jnp.where(m, seg, 0.0)), 0.0)
        cb = jnp.einsum('bign,bjgn->bgij', cc, bc)
        w = jnp.einsum('bgij,bijgh->bghij', cb, lmat) * jnp.moveaxis(dtg, 1, -1)[:, :, :, None, :]
        y_diag = jnp.einsum('bghij,bjghp->bighp', w, xg)
        y_off = jnp.einsum('bign,bghpn->bighp', cc, state) * jnp.exp(cs)[..., None]
        to_end = jnp.exp(cs[:, -1:] - cs) * dtg
        state = jnp.exp(cs[:, -1])[..., None, None] * state + jnp.einsum(
            'bjgn,bjgh,bjghp->bghpn', bc, to_end, xg)
        return state, (y_diag + y_off).reshape(bsz, size, SSM_HEADS, SSM_HEAD_DIM)

    state, y = lax.scan(step, state0, tuple(to_chunks(t, size) for t in (xs, dt, bm, cm)))
    return from_chunks(y), state


def bidir_ssd_mixer(h, hc, w_in, conv_w, conv_b, dt_bias, a_log, d_skip, norm_w, w_out, need_ctx):
    a = -jnp.exp(a_log.astype(jnp.float32))

    def project(u):
        p = u @ w_in
        b, t = p.shape[:2]
        z, xbc, dt = jnp.split(p, [D_INNER, D_INNER + CONV_DIM], axis=-1)
        xbc = jax.nn.silu(depthwise_conv(xbc, conv_w, conv_b))
        xs, bm, cm = jnp.split(xbc, [D_INNER, D_INNER + N_GROUPS * D_STATE], axis=-1)
        xs = xs.reshape(b, t, SSM_HEADS, SSM_HEAD_DIM).astype(jnp.float32)
        bm = bm.reshape(b, t, N_GROUPS, D_STATE).astype(jnp.float32)
        cm = cm.reshape(b, t, N_GROUPS, D_STATE).astype(jnp.float32)
        dt = jax.nn.softplus(dt.reshape(b, t, 2, SSM_HEADS).astype(jnp.float32) + dt_bias.astype(jnp.float32))
        return z, xs, bm, cm, dt

    def run(xs, bm, cm, dt, s_f, s_b):
        y_f, s_f = ssd_scan(xs, dt[:, :, 0], a[0], bm, cm, s_f)
        y_b, s_b = ssd_scan(flip(xs), flip(dt[:, :, 1]), a[1], flip(bm), flip(cm), s_b)
        y = y_f + flip(y_b) + d_skip.astype(jnp.float32)[:, None] * xs
        return y, s_f, s_b

    def readout(y, z):
        b, t = z.shape[:2]
        y = y.reshape(b, t, D_INNER) * jax.nn.silu(z.astype(jnp.float32))
        yg = y.reshape(b, t, N_GROUPS, D_INNER // N_GROUPS)
        yg = yg * lax.rsqrt(jnp.mean(yg * yg, axis=-1, keepdims=True) + EPS)
        return (yg.reshape(b, t, D_INNER) * norm_w.astype(jnp.float32)).astype(z.dtype) @ w_out

    cz, cx, cb, cc, cdt = project(hc)
    zero = jnp.zeros((h.shape[0], N_GROUPS, HEADS_PER_GROUP, SSM_HEAD_DIM, D_STATE), jnp.float32)
    yc, s_f, s_b = run(cx, cb, cc, cdt, zero, zero)
    lz, lx, lb, lc, ldt = project(h)
    yl, _, _ = run(lx, lb, lc, ldt, s_f, s_b)
    y = readout(yl, lz)
    if not need_ctx:
        return y, None
    return y, readout(yc, cz)


def peer_ffn(h, w_q, sub_keys, u, v):
    b, t, d = h.shape
    n = b * t
    hf = h.reshape(n, d)
    q = (hf @ w_q).reshape(n, PEER_HEADS, 2, PEER_HALF).astype(jnp.float32)
    s = jnp.einsum('nhsd,hskd->nhsk', q, sub_keys.astype(jnp.float32))
    s1, i1 = lax.top_k(s[:, :, 0], PEER_TOPK)
    s2, i2 = lax.top_k(s[:, :, 1], PEER_TOPK)
    cand = (s1[..., :, None] + s2[..., None, :]).reshape(n, PEER_HEADS, PEER_TOPK * PEER_TOPK)
    cand_idx = (i1[..., :, None] * N_KEYS + i2[..., None, :]).reshape(n, PEER_HEADS, PEER_TOPK * PEER_TOPK)
    best, pos = lax.top_k(cand, PEER_TOPK)
    expert = jnp.take_along_axis(cand_idx, pos, axis=-1)
    gate = jax.nn.softmax(best, axis=-1)
    nblk = n // PEER_BLOCK

    def one(args):
        hx, ex, gx = args
        act = jnp.einsum('nhed,nd->nhe', u[ex], hx)
        wgt = jax.nn.gelu(act.astype(jnp.float32), approximate=False) * gx
        return jnp.einsum('nhe,nhed->nd', wgt.astype(hx.dtype), v[ex])

    out = lax.map(one, (hf.reshape(nblk, PEER_BLOCK, d),
                        expert.reshape(nblk, PEER_BLOCK, PEER_HEADS, PEER_TOPK),
                        gate.reshape(nblk, PEER_BLOCK, PEER_HEADS, PEER_TOPK)))
    return out.reshape(b, t, d)


def setup_inputs(seed: int = 0) -> dict:
    key = jax.random.key(seed)
    keys = jax.random.split(key, 32)
    n_even = (DEPTH + 1) // 2
    n_odd = DEPTH // 2
    f32 = jnp.float32

    def nrm(i, shape, scale):
        return jax.random.normal(keys[i], shape, f32) * scale

    def gain(i, shape):
        return 1.0 + nrm(i, shape, 0.02)

    base_ret = jnp.log(-jnp.log1p(-(2.0 ** (-5.0 - jnp.arange(RET_HEADS, dtype=f32)))))
    dt0 = jnp.exp(jax.random.uniform(keys[17], (n_odd, 2, SSM_HEADS), f32, math.log(1e-3), math.log(1e-1)))
    return {
        'x': nrm(0, (BATCH, SEQ, D_MODEL), 1.0),
        'c': nrm(1, (BATCH, D_MODEL), 1.0),
        'ctx': nrm(2, (BATCH, CTX_LEN, D_MODEL), 1.0),
        'c_ctx': nrm(3, (D_MODEL,), 1.0),
        'ada_w': nrm(4, (DEPTH, D_MODEL, N_MOD * D_MODEL), 0.5 * D_MODEL ** -0.5),
        'ada_b': nrm(5, (DEPTH, N_MOD * D_MODEL), 0.02),
        'norm1_w': gain(6, (DEPTH, D_MODEL)),
        'norm2_w': gain(7, (DEPTH, D_MODEL)),
        'ev_w_in': nrm(8, (n_even, D_MODEL, EVEN_IN), D_MODEL ** -0.5),
        'ev_w_out': nrm(9, (n_even, EVEN_OUT, D_MODEL), EVEN_OUT ** -0.5),
        'ev_ret_decay': base_ret[None, None, :] + nrm(10, (n_even, 2, RET_HEADS), 0.05),
        'ev_q_norm': gain(11, (n_even, HEAD_DIM)),
        'ev_k_norm': gain(12, (n_even, HEAD_DIM)),
        'od_w_in': nrm(13, (n_odd, D_MODEL, ODD_IN), D_MODEL ** -0.5),
        'od_conv_w': nrm(14, (n_odd, D_CONV, CONV_DIM), D_CONV ** -0.5),
        'od_conv_b': nrm(15, (n_odd, CONV_DIM), 0.02),
        'od_dt_bias': dt0 + jnp.log(-jnp.expm1(-dt0)),
        'od_a_log': jnp.log(jax.random.uniform(keys[16], (n_odd, 2, SSM_HEADS), f32, 1.0, 16.0)),
        'od_d': gain(18, (n_odd, SSM_HEADS)),
        'od_norm_w': gain(19, (n_odd, D_INNER)),
        'od_w_out': nrm(20, (n_odd, D_INNER, D_MODEL), D_INNER ** -0.5),
        'peer_wq': nrm(21, (DEPTH, D_MODEL, PEER_HEADS * PEER_QDIM), D_MODEL ** -0.5),
        'peer_keys': nrm(22, (DEPTH, PEER_HEADS, 2, N_KEYS, PEER_HALF), PEER_HALF ** -0.5),
        'peer_u': nrm(23, (DEPTH, N_EXPERTS, D_MODEL), D_MODEL ** -0.5),
        'peer_v': nrm(24, (DEPTH, N_EXPERTS, D_MODEL), 0.5 / math.sqrt(PEER_HEADS)),
        'final_norm_w': gain(25, (D_MODEL,)),
    }


def reference(x, c, ctx, c_ctx, ada_w, ada_b, norm1_w, norm2_w, ev_w_in, ev_w_out, ev_ret_decay,
              ev_q_norm, ev_k_norm, od_w_in, od_conv_w, od_conv_b, od_dt_bias, od_a_log, od_d,
              od_norm_w, od_w_out, peer_wq, peer_keys, peer_u, peer_v, final_norm_w):
    rope = rope_tables(x.shape[1])
    xc = ctx
    sc = jax.nn.silu(c)
    scc = jax.nn.silu(c_ctx)[None]
    for layer in range(DEPTH):
        last = layer == DEPTH - 1
        mod = jnp.split(sc @ ada_w[layer] + ada_b[layer], N_MOD, axis=-1)
        modc = jnp.split(scc @ ada_w[layer] + ada_b[layer], N_MOD, axis=-1)
        h = modulate(rmsnorm(x, norm1_w[layer]), mod[0], mod[1])
        hc = modulate(rmsnorm(xc, norm1_w[layer]), modc[0], modc[1])
        j = layer // 2
        if layer % 2 == 0:
            y, yc = retention_attention_mixer(h, hc, ev_w_in[j], ev_w_out[j], ev_ret_decay[j],
                                              ev_q_norm[j], ev_k_norm[j], rope, not last)
        else:
            y, yc = bidir_ssd_mixer(h, hc, od_w_in[j], od_conv_w[j], od_conv_b[j], od_dt_bias[j],
                                    od_a_log[j], od_d[j], od_norm_w[j], od_w_out[j], not last)
        x = x + mod[2][:, None, :] * y
        h2 = modulate(rmsnorm(x, norm2_w[layer]), mod[3], mod[4])
        x = x + mod[5][:, None, :] * peer_ffn(h2, peer_wq[layer], peer_keys[layer], peer_u[layer], peer_v[layer])
        if not last:
            xc = xc + modc[2][:, None, :] * yc
            hc2 = modulate(rmsnorm(xc, norm2_w[layer]), modc[3], modc[4])
            xc = xc + modc[5][:, None, :] * peer_ffn(hc2, peer_wq[layer], peer_keys[layer], peer_u[layer], peer_v[layer])
    return rmsnorm(x, final_norm_w)
```

```python
import numpy as np
from contextlib import ExitStack
import concourse.bass as bass
import concourse.mybir as mybir
from concourse.bass_utils import run_bass_kernel_spmd

F32 = mybir.dt.float32
BF16 = mybir.dt.bfloat16
AF = mybir.ActivationFunctionType
ALU = mybir.AluOpType
AX = mybir.AxisListType


class Res:
    __slots__ = ("name", "w", "rd", "dsem", "dcnt")

    def __init__(self, name):
        self.name = name
        self.w = None
        self.rd = []
        self.dsem = None
        self.dcnt = 0


class Op:
    __slots__ = ("eng", "fn", "deps", "isdma", "dres", "dval", "inc", "incval")

    def __init__(self, eng, fn, isdma=False):
        self.eng = eng
        self.fn = fn
        self.deps = []
        self.isdma = isdma
        self.dres = None
        self.dval = 0
        self.inc = False
        self.incval = 0


ENGS = ("pe", "act", "dve", "pool", "sp")


class KB:
    def __init__(self, nc):
        self.nc = nc
        self.ops = {e: [] for e in ENGS}
        self.es = ExitStack()
        self.dma_res = []
        self.nres = 0

    def sb(self, name, shape, dt=F32):
        return self.es.enter_context(self.nc.sbuf_tensor(name, list(shape), dt))

    def ps(self, name, shape=(128, 512), dt=F32):
        return self.es.enter_context(self.nc.psum_tensor(name, list(shape), dt))

    def res(self, name=None):
        self.nres += 1
        return Res(name or f"r{self.nres}")

    def handoff(self, frm, to):
        acc = []
        for f in frm:
            if f.w is not None:
                acc.append(f.w)
            acc.extend(f.rd)
        for t in to:
            t.rd = list(t.rd) + acc

    def _track(self, op, reads, writes, nowaw=False):
        deps = op.deps
        for r in reads:
            if r.w is not None:
                deps.append((r.w, "raw"))
            if not op.isdma:
                r.rd = [o for o in r.rd if o.isdma or o.eng != op.eng]
            r.rd.append(op)
        for w in writes:
            if w.w is not None and not nowaw:
                deps.append((w.w, "waw"))
            for o in w.rd:
                if o is not op:
                    deps.append((o, "war"))
            w.w = op
            w.rd = []

    def op(self, eng, fn, reads=(), writes=()):
        o = Op(eng, fn)
        self._track(o, reads, writes)
        self.ops[eng].append(o)
        return o

    def dma(self, eng, out, in_, reads=(), writes=(), dres=None, nowaw=False, **kw):
        def fn(e):
            return e.dma_start(out=out, in_=in_, **kw)
        o = Op(eng, fn, isdma=True)
        d = dres if dres is not None else writes[0]
        if d.dsem is None:
            d.dsem = self.es.enter_context(self.nc.semaphore(f"d_{d.name}_{len(self.dma_res)}"))
            self.dma_res.append(d)
        d.dcnt += 1
        o.dres = d
        o.dval = 16 * d.dcnt
        self._track(o, reads, writes, nowaw)
        self.ops[eng].append(o)
        return o

    def emit(self, final_wait_eng="sp"):
        nc = self.nc
        sems = {e: self.es.enter_context(nc.semaphore(f"s_{e}")) for e in ENGS}
        for e in ENGS:
            for o in self.ops[e]:
                for (d, kind) in o.deps:
                    if d.isdma:
                        continue
                    if d.eng == o.eng and kind != "raw":
                        continue
                    d.inc = True
        for e in ENGS:
            c = 0
            for o in self.ops[e]:
                if o.inc and not o.isdma:
                    c += 1
                    o.incval = c
        fin = [(d.dsem, 16 * d.dcnt) for d in self.dma_res]
        ops = self.ops

        def run(engname, e):
            waited = {}
            for o in ops[engname]:
                need = {}
                for (d, kind) in o.deps:
                    if d.isdma:
                        key = d.dres.dsem
                        val = d.dval
                    else:
                        if d.eng == o.eng and kind != "raw":
                            continue
                        key = sems[d.eng]
                        val = d.incval
                    if need.get(key, 0) < val:
                        need[key] = val
                for key, val in need.items():
                    if waited.get(key, 0) < val:
                        e.wait_ge(key, val)
                        waited[key] = val
                ins = o.fn(e)
                if o.isdma:
                    ins.then_inc(o.dres.dsem, 16)
                elif o.inc:
                    ins.then_inc(sems[engname], 1)
            if engname == final_wait_eng:
                for s, v in fin:
                    if waited.get(s, 0) < v:
                        e.wait_ge(s, v)

        with nc.Block() as block:
            @block.tensor
            def _(e):
                run("pe", e)

            @block.scalar
            def _(e):
                run("act", e)

            @block.vector
            def _(e):
                run("dve", e)

            @block.gpsimd
            def _(e):
                run("pool", e)

            @block.sync
            def _(e):
                run("sp", e)
        self.es.close()
        return {e: len(self.ops[e]) for e in ENGS}


def build_A(nlayer=4, ncols=1536):
    nc = bass.Bass("TRN2", target_bir_lowering=False)
    cT_h = nc.dram_tensor("cT", [128, 48], F32, kind="ExternalInput").ap()
    w_h = nc.dram_tensor("w", [nlayer, 2048, ncols], F32, kind="ExternalInput").ap()
    b_h = nc.dram_tensor("b", [1, nlayer * ncols], F32, kind="ExternalInput").ap()
    out_h = nc.dram_tensor("mod", [3, nlayer * ncols], F32, kind="ExternalOutput").ap()
    k = KB(nc)
    R = k.res
    CT = k.sb("CT", [128, 48]); rCT = R("CT")
    SCT = k.sb("SCT", [128, 48]); rSCT = R("SCT")
    BI = k.sb("BI", [1, nlayer * ncols]); rBI = R("BI")
    ON = k.sb("ON", [1, 4]); rON = R("ON")
    OUT = k.sb("OUT", [3, nlayer * ncols]); rO = R("O")
    WB = [k.sb(f"WB{i}", [128, 16, 512]) for i in range(2)]; rWB = [R(f"WB{i}") for i in range(2)]
    PS = [k.ps(f"PS{i}") for i in range(2)]; rPS = [R(f"PS{i}") for i in range(2)]
    rOUT = R("OUT")
    k.dma("sp", CT[:], cT_h, writes=[rCT])
    k.dma("sp", BI[:], b_h, writes=[rBI])
    k.op("dve", lambda e: e.memset(ON[:], 1.0), [], [rON])
    k.op("act", lambda e: e.activation(SCT[:], CT[:], AF.Silu), [rCT], [rSCT])
    i = 0
    for l in range(nlayer):
        for cc in range(ncols // 512):
            b = i % 2; i += 1
            k.dma("sp", WB[b][:], w_h[l, :, cc * 512:(cc + 1) * 512].rearrange("(kc p) j -> p kc j", p=128), writes=[rWB[b]])
            o0 = l * ncols + cc * 512
            for kc in range(16):
                k.op("pe", lambda e, b=b, kc=kc: e.matmul(PS[b][0:3, :], lhsT=SCT[:, kc * 3:kc * 3 + 3], rhs=WB[b][:, kc, :], start=(kc == 0), stop=False),
                     [rSCT, rWB[b]], [rPS[b]])
            k.op("pe", lambda e, b=b, o0=o0: e.matmul(PS[b][0:3, :], lhsT=ON[0:1, 0:3], rhs=BI[0:1, o0:o0 + 512], start=False, stop=True), [rON, rBI], [rPS[b]])
            k.op("act", lambda e, b=b, o0=o0: e.copy(OUT[0:3, o0:o0 + 512], PS[b][0:3, :]), [rPS[b]], [rO])
    k.dma("sp", out_h, OUT[:], reads=[rO], writes=[rOUT])
    counts = k.emit()
    return nc, counts


EPS = 1e-6
NE = 16384
BLK = 1024
CPB = BLK // 512
I1B = BLK // 128


def build_D(KOc, tok_groups, final, n_chunks=32):
    nc = bass.Bass("TRN2", target_bir_lowering=False)
    T = max(t0 + gn for t0, gn, _ in tok_groups)
    KO = KOc * 128
    xT_h = nc.dram_tensor("xT", [2048, T], F32, kind="ExternalInput").ap()
    oT_h = nc.dram_tensor("oT", [KO, T], F32, kind="ExternalInput").ap()
    wo_h = nc.dram_tensor("w_out", [KO, 2048], F32, kind="ExternalInput").ap()
    mods_h = nc.dram_tensor("mods", [128, 2 * 6 * 16], F32, kind="ExternalInput").ap()
    nw_h = nc.dram_tensor("nw", [128, 32], F32, kind="ExternalInput").ap()
    wq_h = nc.dram_tensor("wq", [2048, 2048], F32, kind="ExternalInput").ap()
    kT_h = nc.dram_tensor("keysT", [128, 2048], F32, kind="ExternalInput").ap()
    uT_h = nc.dram_tensor("uT", [2048, NE], F32, kind="ExternalInput").ap()
    v_h = nc.dram_tensor("v", [NE, 2048], F32, kind="ExternalInput").ap()
    id_h = nc.dram_tensor("ident", [128, 128], F32, kind="ExternalInput").ap()
    out_h = nc.dram_tensor("outT", [2048, T], F32, kind="ExternalOutput").ap()

    k = KB(nc)
    R = k.res
    MODS = k.sb("MODS", [128, 2, 6, 16]); rMODS = R("MODS")
    NW = k.sb("NW", [128, 32]); rNW = R("NW")
    G2 = k.sb("G2", [128, 2, 16]); rG2 = R("G2")
    ID = k.sb("ID", [128, 128]); rID = R("ID")
    IDb = k.sb("IDb", [128, 128], BF16); rIDb = R("IDb")
    ONES = k.sb("ONES", [128, 128]); rONES = R("ONES")
    KT = k.sb("KT", [128, 2048], BF16); rKT = R("KT")
    k.dma("sp", MODS[:].rearrange("p a b c -> p (a b c)"), mods_h, writes=[rMODS])
    k.dma("sp", NW[:], nw_h, writes=[rNW])
    k.dma("sp", ID[:], id_h, writes=[rID])
    k.dma("pool", IDb[:], id_h, writes=[rIDb])
    k.dma("pool", KT[:], kT_h, writes=[rKT])
    k.op("dve", lambda e: e.memset(ONES[:], 1.0), [], [rONES])
    for ms in range(2):
        k.op("dve", lambda e, ms=ms: e.scalar_tensor_tensor(G2[:, ms, :], MODS[:, ms, 4, :], 1.0, NW[:, 0:16],
                                                          op0=ALU.add, op1=ALU.mult), [rMODS, rNW], [rG2])
    M1 = k.sb("M1", [128, 8192])
    MA = k.sb("MA", [128, 8192])
    MB = k.sb("MB", [128, 8192])
    MAb = MA.bitcast(BF16)
    MBb = MB.bitcast(BF16)
    X = M1[:].rearrange("p (kc t) -> p kc t", kc=16); rX = R("X")
    PFv = M1[:].rearrange("p (t d) -> p t d", t=4); rPF = [[R(f"PF{t}_{q}") for q in range(4)] for t in range(4)]
    rPFall = [r for rr in rPF for r in rr]
    OBv = MAb[:].rearrange("p (kc t) -> p kc t", t=512); rOB = R("OB")
    QT = MAb[:, 0:8192].rearrange("p (hs t) -> p hs t", hs=16); rQT = R("QT")
    UT = [MAb[:, i * 8192:(i + 1) * 8192].rearrange("p (kc e) -> p kc e", kc=16) for i in range(2)]; rUT = [R(f"UT{i}") for i in range(2)]
    X1R = MA[:].rearrange("p (kc t) -> p kc t", kc=16); rX1R = R("X1R")
    WO = [MBb[:, i * 4096:i * 4096 + KOc * 128].rearrange("p (kc j) -> p kc j", kc=KOc) for i in range(2)]; rWO = [R(f"WO{i}") for i in range(2)]
    WQ = [MBb[:, 8192 + i * 2048:8192 + (i + 1) * 2048].rearrange("p (kc j) -> p kc j", kc=16) for i in range(2)]; rWQ = [R(f"WQ{i}") for i in range(2)]
    VV = [MBb[:, i * 8192:(i + 1) * 8192].rearrange("p (j d) -> p j d", j=4) for i in range(2)]; rVV = [R(f"VV{i}") for i in range(2)]
    SQ = [k.sb(f"SQ{i}", [128, 512]) for i in range(2)]; rSQ = [R(f"SQ{i}") for i in range(2)]
    RS = k.sb("RS", [128, 512]); rRS = R("RS")
    TMP = [k.sb(f"TMP{i}", [128, 512]) for i in range(2)]; rTMP = [R(f"TMP{i}") for i in range(2)]
    H2 = k.sb("H2", [128, 16, 512], BF16); rH2 = R("H2")
    S = [k.sb(f"S{i}", [128, 2048]) for i in range(4)]; rS = [R(f"S{i}") for i in range(4)]
    MR = k.sb("MR", [128, 256]); rMR = R("MR")
    MR2 = k.sb("MR2", [128, 256]); rMR2 = R("MR2")
    TS = k.sb("TS", [128, 2, 16]); rTS = R("TS")
    C256 = k.sb("C256", [128, 256]); rC256 = R("C256")
    VAL = [k.sb(f"VAL{i}", [128, 8, 24]) for i in range(4)]; rVAL = [R(f"VAL{i}") for i in range(4)]
    D16 = k.sb("D16", [128, 8, 16]); rD16 = R("D16")
    Z = k.sb("Z", [128, 8]); rZ = R("Z")
    BIAS = [k.sb(f"BIAS{i}", [128, 8]) for i in range(4)]; rBIAS = [R(f"BIAS{i}") for i in range(4)]
    THR = [k.sb(f"THR{i}", [128, 8]) for i in range(4)]; rTHR = [R(f"THR{i}") for i in range(4)]
    CAND = [k.sb(f"CAND{i}", [128, BLK]) for i in range(2)]; rCAND = [R(f"CAND{i}") for i in range(2)]
    EE = [k.sb(f"EE{i}", [128, BLK], BF16) for i in range(2)]; rEE = [R(f"EE{i}") for i in range(2)]
    MH = [k.sb(f"MH{i}", [128, BLK], BF16) for i in range(2)]; rMH = [R(f"MH{i}") for i in range(2)]
    GG = [k.sb(f"GG{i}", [128, BLK], BF16) for i in range(4)]; rGG = [R(f"GG{i}") for i in range(4)]
    GA = [k.sb(f"GA{i}", [128, 512], BF16) for i in range(2)]; rGA = [R(f"GA{i}") for i in range(2)]
    WT_ = [k.sb(f"Wt{i}", [128, 512], BF16) for i in range(2)]; rWt = [R(f"Wt{i}") for i in range(2)]
    WTT = [k.sb(f"WTT{i}", [128, 4, 128], BF16) for i in range(2)]; rWTT = [R(f"WTT{i}") for i in range(2)]
    PO = [k.ps(f"PO{i}") for i in range(4)]; rPO = [R(f"PO{i}") for i in range(4)]
    PA = [k.ps(f"PA{i}") for i in range(2)]; rPA = [R(f"PA{i}") for i in range(2)]
    PT = k.ps("PT", (128, 512), BF16); rPT = R("PT")
    PM = k.ps("PM"); rPM = R("PM")
    rOUT = R("OUT")
    cnt = dict(wo=0, sq=0, tmp=0, wq=0, cand=0, uv=0, ga=0, po=0)
    NB = NE // BLK

    def do_group(t0, gn, ms):
        nt = gn // 128
        out_g = out_h[:, t0:t0 + gn].rearrange("(kc p) t -> p kc t", p=128)
        k.dma("sp", X[:, :, 0:gn], xT_h[:, t0:t0 + gn].rearrange("(kc p) t -> p kc t", p=128), writes=[rX])
        k.dma("pool", OBv[:, 0:KOc, 0:gn], oT_h[:, t0:t0 + gn].rearrange("(kc p) t -> p kc t", p=128), writes=[rOB])
        for dc in range(16):
            b = cnt["wo"] % 2; cnt["wo"] += 1
            k.dma("pool", WO[b], wo_h[:, dc * 128:(dc + 1) * 128].rearrange("(kc p) j -> p kc j", p=128), writes=[rWO[b]])
            for kc in range(KOc):
                k.op("pe", lambda e, b=b, kc=kc: e.matmul(PM[:, 0:gn], lhsT=WO[b][:, kc, :], rhs=OBv[:, kc, 0:gn],
                                                          start=(kc == 0), stop=(kc == KOc - 1)), [rWO[b], rOB], [rPM])
            k.op("dve", lambda e, dc=dc: e.scalar_tensor_tensor(X[:, dc, 0:gn], PM[:, 0:gn], MODS[:, ms, 2, dc:dc + 1], X[:, dc, 0:gn],
                                                               op0=ALU.mult, op1=ALU.add), [rPM, rMODS, rX], [rX])
            sb_ = cnt["sq"] % 2; cnt["sq"] += 1
            k.op("act", lambda e, dc=dc, sb_=sb_: e.activation(SQ[sb_][:, 0:gn], X[:, dc, 0:gn], AF.Square), [rX], [rSQ[sb_]])
            k.op("pe", lambda e, dc=dc, sb_=sb_: e.matmul(PA[0][:, 0:gn], lhsT=ONES[:], rhs=SQ[sb_][:, 0:gn],
                                                          start=(dc == 0), stop=(dc == 15)), [rONES, rSQ[sb_]], [rPA[0]])
        k.dma("sp", out_g, X[:, :, 0:gn], reads=[rX], writes=[rOUT])
        k.op("dve", lambda e: e.tensor_scalar(RS[:, 0:gn], PA[0][:, 0:gn], 1.0 / 2048, EPS, op0=ALU.mult, op1=ALU.add), [rPA[0]], [rRS])
        k.op("act", lambda e: e.activation(RS[:, 0:gn], RS[:, 0:gn], AF.Ln), [rRS], [rRS])
        k.op("act", lambda e: e.activation(RS[:, 0:gn], RS[:, 0:gn], AF.Exp, scale=-0.5), [rRS], [rRS])
        for kc in range(16):
            tb = cnt["tmp"] % 2; cnt["tmp"] += 1
            k.op("dve", lambda e, kc=kc, tb=tb: e.scalar_tensor_tensor(TMP[tb][:, 0:gn], X[:, kc, 0:gn], G2[:, ms, kc:kc + 1], RS[:, 0:gn],
                                                                      op0=ALU.mult, op1=ALU.mult), [rX, rG2, rRS], [rTMP[tb]])
            k.op("act", lambda e, kc=kc, tb=tb: e.activation(H2[:, kc, 0:gn], TMP[tb][:, 0:gn], AF.Identity,
                                                             bias=MODS[:, ms, 3, kc:kc + 1], scale=1.0), [rTMP[tb], rMODS], [rH2])
        k.handoff([rOB], [rQT])
        k.handoff(rWO, rWQ)
        for hs in range(16):
            b = cnt["wq"] % 2; cnt["wq"] += 1
            k.dma("pool", WQ[b], wq_h[:, hs * 128:(hs + 1) * 128].rearrange("(kc p) j -> p kc j", p=128), writes=[rWQ[b]])
            for kc in range(16):
                k.op("pe", lambda e, b=b, kc=kc: e.matmul(PM[:, 0:gn], lhsT=WQ[b][:, kc, :], rhs=H2[:, kc, 0:gn],
                                                          start=(kc == 0), stop=(kc == 15)), [rWQ[b], rH2], [rPM])
            k.op("act", lambda e, hs=hs: e.copy(QT[:, hs, 0:gn], PM[:, 0:gn]), [rPM], [rQT])
        for t in range(nt):
            for q in range(4):
                for j in range(4):
                    hs = q * 4 + j
                    k.op("pe", lambda e, t=t, q=q, j=j, hs=hs: e.matmul(PO[q][:, j * 128:(j + 1) * 128], lhsT=QT[:, hs, t * 128:(t + 1) * 128],
                                                                        rhs=KT[:, hs * 128:(hs + 1) * 128], start=True, stop=True),
                         [rQT, rKT], [rPO[q]])
                k.op("act", lambda e, t=t, q=q: e.copy(S[t][:, q * 512:(q + 1) * 512], PO[q][:]), [rPO[q]], [rS[t]])
            for h in range(8):
                for sd in range(2):
                    o0 = (h * 2 + sd) * 128
                    k.op("dve", lambda e, t=t, o0=o0, sd=sd: e.max(TS[:, sd, 0:8], S[t][:, o0:o0 + 128]), [rS[t]], [rTS])
                    k.op("dve", lambda e, t=t, o0=o0, sd=sd: e.match_replace(MR[:, 0:128], TS[:, sd, 0:8], S[t][:, o0:o0 + 128], -1e30), [rS[t], rTS], [rMR])
                    k.op("dve", lambda e, sd=sd: e.max(TS[:, sd, 8:16], MR[:, 0:128]), [rMR], [rTS])
                k.op("dve", lambda e: e.tensor_tensor(C256[:].rearrange("p (a b) -> p a b", a=16),
                                                      TS[:, 0, :].unsqueeze(2).to_broadcast([128, 16, 16]),
                                                      TS[:, 1, :].unsqueeze(1).to_broadcast([128, 16, 16]), op=ALU.add), [rTS], [rC256])
                k.op("dve", lambda e, t=t, h=h: e.max(VAL[t][:, h, 0:8], C256[:]), [rC256], [rVAL[t]])
                k.op("dve", lambda e, t=t, h=h: e.match_replace(MR[:], VAL[t][:, h, 0:8], C256[:], -1e30), [rC256, rVAL[t]], [rMR])
                k.op("dve", lambda e, t=t, h=h: e.max(VAL[t][:, h, 8:16], MR[:]), [rMR], [rVAL[t]])
                k.op("dve", lambda e, t=t, h=h: e.match_replace(MR2[:], VAL[t][:, h, 8:16], MR[:], -1e30), [rMR, rVAL[t]], [rMR2])
                k.op("dve", lambda e, t=t, h=h: e.max(VAL[t][:, h, 16:24], MR2[:]), [rMR2], [rVAL[t]])
            k.op("dve", lambda e, t=t: e.tensor_tensor(D16[:], VAL[t][:, :, 0:16], VAL[t][:, :, 0:1].to_broadcast([128, 8, 16]), op=ALU.subtract), [rVAL[t]], [rD16])
            k.op("act", lambda e: e.activation(D16[:], D16[:], AF.Exp), [rD16], [rD16])
            k.op("dve", lambda e: e.tensor_reduce(Z[:], D16[:], axis=AX.X, op=ALU.add), [rD16], [rZ])
            k.op("act", lambda e: e.activation(Z[:], Z[:], AF.Ln), [rZ], [rZ])
            k.op("dve", lambda e, t=t: e.scalar_tensor_tensor(BIAS[t][:], VAL[t][:, :, 0], -1.0, Z[:], op0=ALU.mult, op1=ALU.subtract), [rVAL[t], rZ], [rBIAS[t]])
            k.op("dve", lambda e, t=t: e.tensor_tensor(THR[t][:], VAL[t][:, :, 15], VAL[t][:, :, 16], op=ALU.add), [rVAL[t]], [rTHR[t]])
            k.op("dve", lambda e, t=t: e.tensor_scalar(THR[t][:], THR[t][:], 0.5, None, op0=ALU.mult), [rTHR[t]], [rTHR[t]])
        k.handoff([rQT], rUT)
        k.handoff(rWQ + rWO, rVV)
        k.handoff([rX], rPFall)
        for blk in range(NB):
            if blk * CPB >= n_chunks:
                break
            for t in range(nt):
                for h in range(8):
                    cb = cnt["cand"] % 2; cnt["cand"] += 1
                    s1 = S[t][:, (h * 2) * 128 + blk * I1B:(h * 2) * 128 + (blk + 1) * I1B]
                    s2 = S[t][:, (h * 2 + 1) * 128:(h * 2 + 2) * 128]
                    k.op("dve", lambda e, cb=cb, s1=s1, s2=s2: e.tensor_tensor(CAND[cb][:].rearrange("p (a b) -> p a b", a=I1B),
                                                                               s1.unsqueeze(2).to_broadcast([128, I1B, 128]),
                                                                               s2.unsqueeze(1).to_broadcast([128, I1B, 128]), op=ALU.add), [rS[t]], [rCAND[cb]])
                    k.op("act", lambda e, cb=cb, t=t, h=h: e.activation(EE[cb][:], CAND[cb][:], AF.Exp, bias=BIAS[t][:, h:h + 1], scale=1.0),
                         [rCAND[cb], rBIAS[t]], [rEE[cb]])
                    if h == 0:
                        k.op("dve", lambda e, cb=cb, t=t, h=h: e.scalar_tensor_tensor(GG[t][:], CAND[cb][:], THR[t][:, h:h + 1], EE[cb][:], op0=ALU.is_ge, op1=ALU.mult),
                             [rCAND[cb], rTHR[t], rEE[cb]], [rGG[t]])
                    else:
                        k.op("dve", lambda e, cb=cb, t=t, h=h: e.scalar_tensor_tensor(MH[cb][:], CAND[cb][:], THR[t][:, h:h + 1], EE[cb][:], op0=ALU.is_ge, op1=ALU.mult),
                             [rCAND[cb], rTHR[t], rEE[cb]], [rMH[cb]])
                        k.op("dve", lambda e, cb=cb, t=t: e.tensor_tensor(GG[t][:], GG[t][:], MH[cb][:], op=ALU.add), [rGG[t], rMH[cb]], [rGG[t]])
            for ec in range(CPB):
                c = blk * CPB + ec
                if c >= n_chunks:
                    break
                ub = cnt["uv"] % 2; cnt["uv"] += 1
                k.dma("pool", UT[ub], uT_h[:, c * 512:(c + 1) * 512].rearrange("(kc p) e -> p kc e", p=128), writes=[rUT[ub]])
                k.dma("pool", VV[ub], v_h[c * 512:(c + 1) * 512, :].rearrange("(j p) d -> p j d", p=128), writes=[rVV[ub]])
                for t in range(nt):
                    gb = cnt["ga"] % 2; cnt["ga"] += 1
                    for kc in range(16):
                        k.op("pe", lambda e, gb=gb, ub=ub, kc=kc, t=t: e.matmul(PA[gb][:], lhsT=H2[:, kc, t * 128:(t + 1) * 128], rhs=UT[ub][:, kc, :],
                                                                                start=(kc == 0), stop=(kc == 15)), [rH2, rUT[ub]], [rPA[gb]])
                    k.op("act", lambda e, gb=gb: e.activation(GA[gb][:], PA[gb][:], AF.Gelu), [rPA[gb]], [rGA[gb]])
                    k.op("dve", lambda e, gb=gb, t=t, ec=ec: e.tensor_tensor(WT_[gb][:], GA[gb][:], GG[t][:, ec * 512:(ec + 1) * 512], op=ALU.mult),
                         [rGA[gb], rGG[t]], [rWt[gb]])
                    for j in range(4):
                        k.op("pe", lambda e, gb=gb, j=j: e.transpose(PT[:, j * 128:(j + 1) * 128], WT_[gb][:, j * 128:(j + 1) * 128], IDb[:]), [rWt[gb], rIDb], [rPT])
                    k.op("act", lambda e, gb=gb: e.copy(WTT[gb][:].rearrange("p a b -> p (a b)"), PT[:]), [rPT], [rWTT[gb]])
                    for dq in range(4):
                        pb = cnt["po"] % 4; cnt["po"] += 1
                        for j in range(4):
                            k.op("pe", lambda e, gb=gb, ub=ub, j=j, dq=dq, pb=pb: e.matmul(PO[pb][:], lhsT=WTT[gb][:, j, :], rhs=VV[ub][:, j, dq * 512:(dq + 1) * 512],
                                                                                         start=(j == 0), stop=(j == 3)), [rWTT[gb], rVV[ub]], [rPO[pb]])
                        if c == 0:
                            k.op("dve", lambda e, t=t, dq=dq, pb=pb: e.tensor_copy(PFv[:, t, dq * 512:(dq + 1) * 512], PO[pb][:]), [rPO[pb]], [rPF[t][dq]])
                        else:
                            k.op("dve", lambda e, t=t, dq=dq, pb=pb: e.tensor_tensor(PFv[:, t, dq * 512:(dq + 1) * 512], PO[pb][:], PFv[:, t, dq * 512:(dq + 1) * 512], op=ALU.add),
                                 [rPO[pb], rPF[t][dq]], [rPF[t][dq]])
        k.handoff(rUT, [rX1R])
        k.dma("sp", X1R[:, :, 0:gn], out_g, reads=[rOUT], writes=[rX1R])
        for t in range(nt):
            for dq in range(4):
                for j in range(4):
                    k.op("pe", lambda e, t=t, dq=dq, j=j: e.transpose(PM[:, j * 128:(j + 1) * 128], PFv[:, t, dq * 512 + j * 128:dq * 512 + (j + 1) * 128], ID[:]),
                         [rPF[t][dq], rID], [rPM])
                for j in range(4):
                    dc = dq * 4 + j
                    k.op("dve", lambda e, t=t, dc=dc, j=j: e.scalar_tensor_tensor(X1R[:, dc, t * 128:(t + 1) * 128], PM[:, j * 128:(j + 1) * 128], MODS[:, ms, 5, dc:dc + 1],
                                                                                 X1R[:, dc, t * 128:(t + 1) * 128], op0=ALU.mult, op1=ALU.add), [rPM, rMODS, rX1R], [rX1R])
        if final:
            for dc in range(16):
                sb_ = cnt["sq"] % 2; cnt["sq"] += 1
                k.op("act", lambda e, dc=dc, sb_=sb_: e.activation(SQ[sb_][:, 0:gn], X1R[:, dc, 0:gn], AF.Square), [rX1R], [rSQ[sb_]])
                k.op("pe", lambda e, dc=dc, sb_=sb_: e.matmul(PA[0][:, 0:gn], lhsT=ONES[:], rhs=SQ[sb_][:, 0:gn],
                                                              start=(dc == 0), stop=(dc == 15)), [rONES, rSQ[sb_]], [rPA[0]])
            k.op("dve", lambda e: e.tensor_scalar(RS[:, 0:gn], PA[0][:, 0:gn], 1.0 / 2048, EPS, op0=ALU.mult, op1=ALU.add), [rPA[0]], [rRS])
            k.op("act", lambda e: e.activation(RS[:, 0:gn], RS[:, 0:gn], AF.Ln), [rRS], [rRS])
            k.op("act", lambda e: e.activation(RS[:, 0:gn], RS[:, 0:gn], AF.Exp, scale=-0.5), [rRS], [rRS])
            for kc in range(16):
                k.op("dve", lambda e, kc=kc: e.scalar_tensor_tensor(X1R[:, kc, 0:gn], X1R[:, kc, 0:gn], NW[:, 16 + kc:17 + kc], RS[:, 0:gn],
                                                                   op0=ALU.mult, op1=ALU.mult), [rX1R, rNW, rRS], [rX1R])
        k.dma("sp", out_g, X1R[:, :, 0:gn], reads=[rX1R], writes=[rOUT])
        k.handoff([rX1R], [rOB])
        k.handoff(rVV, rWO)
        k.handoff(rPFall, [rX])

    for (t0_, gn_, ms_) in tok_groups:
        do_group(t0_, gn_, ms_)
    counts = k.emit()
    return nc, counts


EPS = 1e-6
NCTX = 256
NLOC = 8192
TB = NCTX + NLOC
NKT = TB // 128
GA_ = 256
NBLK = 11
C_RQ, C_RQP, C_RK, C_RKP, C_RG, C_AQ, C_AQP, C_AK, C_AKP, C_RV, C_AV = range(11)


def build_E(nbatch=2, tb=TB, nctx=NCTX):
    nc = bass.Bass("TRN2", target_bir_lowering=False)
    nkt = tb // 128
    nloc = tb - nctx
    xT_h = nc.dram_tensor("xT", [2048, nbatch * tb], F32, kind="ExternalInput").ap()
    w_h = nc.dram_tensor("w", [2048, NBLK * 128], F32, kind="ExternalInput").ap()
    mods_h = nc.dram_tensor("mods", [128, 3 * 2 * 16], F32, kind="ExternalInput").ap()
    nw_h = nc.dram_tensor("nw", [128, 16], F32, kind="ExternalInput").ap()
    sm_h = nc.dram_tensor("small", [128, 8], F32, kind="ExternalInput").ap()
    cos_h = nc.dram_tensor("cosT", [128, nloc], F32, kind="ExternalInput").ap()
    sin_h = nc.dram_tensor("sinT", [128, nloc], F32, kind="ExternalInput").ap()
    out_h = nc.dram_tensor("oT", [nbatch, 256, tb], F32, kind="ExternalOutput").ap()

    k = KB(nc)
    R = k.res
    MODS = k.sb("MODS", [128, 3, 2, 16]); rMODS = R("MODS")
    NW = k.sb("NW", [128, 16]); rNW = R("NW")
    SM = k.sb("SM", [128, 8]); rSM = R("SM")
    G1 = k.sb("G1", [128, 3, 16]); rG1 = R("G1")
    ONES = k.sb("ONES", [128, 128]); rONES = R("ONES")
    ONESb = k.sb("ONESb", [128, 128], BF16); rONESb = R("ONESb")
    LG = k.sb("LG", [128, 4]); rLG = R("LG")
    XR = k.sb("XR", [128, 4096])
    XRb = XR.bitcast(BF16)
    R2 = k.sb("R2", [128, 1536])
    R2b = R2.bitcast(BF16)
    IO1 = XR[:, 0:512]; rIO1 = R("IO1")
    IOM = k.sb("IOM", [128, 80]); rIOM = R("IOM")
    CF = k.sb("CF", [128, 80]); rCF = R("CF")
    CB = k.sb("CB", [128, 80]); rCB = R("CB")
    BF_ = k.sb("BF", [128, 512], BF16); rBF = R("BF")
    BB_ = k.sb("BB", [128, 512], BF16); rBB = R("BB")
    DD = [k.sb(f"DD{m}", [128, 512], BF16) for m in range(4)]; rDD = [R(f"DD{m}") for m in range(4)]
    DDf = XR[:, 1536:2048]; rDDf = R("DDf")
    T1 = XR[:, 512:1024]; rT1 = R("T1c")
    T2 = XR[:, 1024:1536]; rT2 = R("T2c")
    W = k.sb("W", [128, 16, NBLK * 128], BF16); rW = R("W")
    k.dma("sp", MODS[:].rearrange("p a b c -> p (a b c)"), mods_h, writes=[rMODS])
    k.dma("sp", NW[:], nw_h, writes=[rNW])
    k.dma("sp", SM[:], sm_h, writes=[rSM])
    for kc in range(16):
        k.dma("pool", W[:, kc, :], w_h[kc * 128:(kc + 1) * 128, :], writes=[rW])
    k.op("dve", lambda e: e.memset(ONES[:], 1.0), [], [rONES])
    k.op("dve", lambda e: e.memset(ONESb[:], 1.0), [], [rONESb])
    for s_ in range(3):
        k.op("dve", lambda e, s_=s_: e.scalar_tensor_tensor(G1[:, s_, :], MODS[:, s_, 1, :], 1.0, NW[:], op0=ALU.add, op1=ALU.mult), [rMODS, rNW], [rG1])
    k.op("act", lambda e: e.activation(LG[:, 2:4], SM[:, 0:2], AF.Exp), [rSM], [rLG])
    k.op("dve", lambda e: e.tensor_scalar(LG[:, 0:2], LG[:, 2:4], -1.0, None, op0=ALU.mult), [rLG], [rLG])
    k.op("pool", lambda e: e.iota(IO1, pattern=[[1, 512]], base=0, channel_multiplier=-1, allow_small_or_imprecise_dtypes=True), [], [rIO1])
    k.op("pool", lambda e: e.iota(IOM[:], pattern=[[128, 80]], base=0, channel_multiplier=0, allow_small_or_imprecise_dtypes=True), [], [rIOM])
    SC = 128.0 ** -0.5
    k.op("act", lambda e: e.activation(CF[:], IOM[:], AF.Exp, scale=LG[:, 0:1]), [rIOM, rLG], [rCF])
    k.op("act", lambda e: e.activation(CB[:], IOM[:], AF.Exp, scale=LG[:, 1:2]), [rIOM, rLG], [rCB])
    k.op("dve", lambda e: e.tensor_scalar(CF[:], CF[:], SC, None, op0=ALU.mult), [rCF], [rCF])
    k.op("dve", lambda e: e.tensor_scalar(CB[:], CB[:], SC, None, op0=ALU.mult), [rCB], [rCB])
    k.op("act", lambda e: e.activation(BF_[:], IO1, AF.Exp, scale=LG[:, 0:1]), [rIO1, rLG], [rBF])
    k.op("act", lambda e: e.activation(BB_[:], IO1, AF.Exp, scale=LG[:, 3:4]), [rIO1, rLG], [rBB])
    for m in range(4):
        k.op("dve", lambda e, m=m: e.tensor_scalar(T1, IO1, -128.0 * m, 0.0, op0=ALU.add, op1=ALU.max), [rIO1], [rT1])
        k.op("dve", lambda e, m=m: e.tensor_scalar(T2, IO1, -1.0, 128.0 * m, op0=ALU.mult, op1=ALU.add), [rIO1], [rT2])
        k.op("dve", lambda e: e.tensor_scalar(T2, T2, 0.0, LG[:, 1:2], op0=ALU.max, op1=ALU.mult), [rT2, rLG], [rT2])
        k.op("dve", lambda e: e.scalar_tensor_tensor(T1, T1, LG[:, 0:1], T2, op0=ALU.mult, op1=ALU.add), [rT1, rLG, rT2], [rT1])
        k.op("act", lambda e, m=m: e.activation(DDf, T1, AF.Exp), [rT1], [rDDf])
        k.op("dve", lambda e, m=m: e.tensor_scalar(T2, IO1, 128.0 * m, None, op0=ALU.is_equal), [rIO1], [rT2])
        k.op("dve", lambda e, m=m: e.tensor_tensor(DDf, DDf, T2, op=ALU.add), [rDDf, rT2], [rDDf])
        k.op("dve", lambda e, m=m: e.tensor_scalar(DD[m][:], DDf, SC, None, op0=ALU.mult), [rDDf], [rDD[m]])
    RQ = k.sb("RQ", [128, tb], BF16); rRQ = R("RQ")
    RK = k.sb("RK", [128, tb], BF16); rRK = R("RK")
    RG = k.sb("RG", [128, tb], BF16); rRG = R("RG")
    AQ = k.sb("AQ", [128, tb], BF16); rAQ = R("AQ")
    AK = k.sb("AK", [128, tb], BF16); rAK = R("AK")
    RV = k.sb("RV", [128, nkt, 128], BF16); rRV = R("RV")
    AV = k.sb("AV", [128, nkt, 128], BF16); rAV = R("AV")
    X = XR[:].rearrange("p (kc t) -> p kc t", kc=16)
    H = XRb[:].rearrange("p (kc t) -> p kc t", kc=16)[:, :, 0:GA_]
    rXk = [R(f"X{i}") for i in range(16)]
    SQ = [k.sb(f"SQ{i}", [128, GA_]) for i in range(2)]; rSQ = [R(f"SQ{i}") for i in range(2)]
    RS = k.sb("RS", [128, GA_]); rRS = R("RS")
    TMP = [k.sb(f"TMP{i}", [128, GA_]) for i in range(2)]; rTMP = [R(f"TMP{i}") for i in range(2)]
    COS = [k.sb(f"COS{i}", [128, GA_]) for i in range(1)]; rCOS = [R(f"COS{i}") for i in range(1)]
    SIN = [k.sb(f"SIN{i}", [128, GA_]) for i in range(1)]; rSIN = [R(f"SIN{i}") for i in range(1)]
    XN = R2[:, 0:512].rearrange("p (a t) -> p a t", a=2); rXN = R("XN")
    RA = R2[:, 512:768]; rRA = R("RA")
    RB = R2[:, 768:1024]; rRB = R("RB")
    RSQ = R2[:, 1024:1280]; rRSQ = R("RSQ")
    resA = rXk + [rXN, rRA, rRB, rRSQ]
    OS = [XR[:, 0:512], XR[:, 512:1024]]; rOS = [R(f"OS{i}") for i in range(2)]
    ORS = XR[:, 1024:1536]; rORS = R("ORS")
    CEN = XR[:, 1536:2048]; rCEN = R("CEN")
    SQB = XR[:, 2048:2560]; rSQB = R("SQB")
    RSB = XR[:, 2560:3072]; rRSB = R("RSB")
    RI = XR[:, 3072:3584]; rRI = R("RI")
    PTa = [XRb[:, 7168:7680], XRb[:, 7680:8192]]; rPTa = [R(f"PTa{i}") for i in range(2)]
    DC = [R2[:, 0:512], R2[:, 512:1024]]; rDC = [R(f"DC{i}") for i in range(2)]
    PTr = [R2b[:, 2048:2560], R2b[:, 2560:3072]]; rPTr = [R(f"PTr{i}") for i in range(2)]
    resB = rOS + [rORS, rCEN, rSQB, rRSB, rRI] + rPTa + rDC + rPTr
    k.handoff([rIO1, rT1, rT2, rDDf], resA)
    PP = [k.ps(f"PP{i}") for i in range(4)]; rPP = [R(f"PP{i}") for i in range(4)]
    PO = k.ps("PO"); rPO = R("PO")
    PR = k.ps("PR"); rPR = R("PR")
    PQ = k.ps("PQ"); rPQ = R("PQ")
    PM = k.ps("PM"); rPM = R("PM")
    rOUT = R("OUT")
    cnt = dict(sq=0, tmp=0, cs=0, pp=0, pta=0, ptr=0, dc=0, os=0)

    def stats_rstd(src_of_kc, nk, N, div, out_rs):
        for kc in range(nk):
            sb_ = cnt["sq"] % 2; cnt["sq"] += 1
            ap, rr = src_of_kc(kc)
            k.op("act", lambda e, ap=ap, sb_=sb_: e.activation(SQ[sb_][:, 0:N], ap, AF.Square), [rr], [rSQ[sb_]])
            k.op("pe", lambda e, kc=kc, sb_=sb_: e.matmul(PM[:, 0:N], lhsT=ONES[:], rhs=SQ[sb_][:, 0:N], start=(kc == 0), stop=(kc == nk - 1)),
                 [rONES, rSQ[sb_]], [rPM])
        ors, rr = out_rs
        k.op("dve", lambda e: e.tensor_scalar(ors, PM[:, 0:N], 1.0 / div, EPS, op0=ALU.mult, op1=ALU.add), [rPM], [rr])
        k.op("act", lambda e: e.activation(ors, ors, AF.Ln), [rr], [rr])
        k.op("act", lambda e: e.activation(ors, ors, AF.Exp, scale=-0.5), [rr], [rr])

    def phase_a_group(b, g0, N, ms, is_ctx, loc0):
        c0 = b * tb + g0
        k.dma("sp", X[:, :, 0:N], xT_h[:, c0:c0 + N].rearrange("(kc p) t -> p kc t", p=128), writes=rXk, dres=rXk[0])
        if not is_ctx:
            cb = 0
            k.dma("sp", COS[cb][:, 0:N], cos_h[:, loc0:loc0 + N], writes=[rCOS[cb]])
            k.dma("sp", SIN[cb][:, 0:N], sin_h[:, loc0:loc0 + N], writes=[rSIN[cb]])
        stats_rstd(lambda kc: (X[:, kc, 0:N], rXk[kc]), 16, N, 2048.0, (RS[:, 0:N], rRS))
        for kc in range(16):
            tb_ = cnt["tmp"] % 2; cnt["tmp"] += 1
            k.op("dve", lambda e, kc=kc, tb_=tb_: e.scalar_tensor_tensor(TMP[tb_][:, 0:N], X[:, kc, 0:N], G1[:, ms, kc:kc + 1], RS[:, 0:N],
                                                                        op0=ALU.mult, op1=ALU.mult), [rXk[kc], rG1, rRS], [rTMP[tb_]])
            k.op("act", lambda e, kc=kc, tb_=tb_: e.activation(H[:, kc, 0:N], TMP[tb_][:, 0:N], AF.Identity,
                                                               bias=MODS[:, ms, 0, kc:kc + 1], scale=1.0), [rTMP[tb_], rMODS], [rXk[kc]])

        def proj(cblk, pp, off):
            for kc in range(16):
                k.op("pe", lambda e, kc=kc: e.matmul(PP[pp][:, off:off + N], lhsT=W[:, kc, cblk * 128:(cblk + 1) * 128], rhs=H[:, kc, 0:N],
                                                     start=(kc == 0), stop=(kc == 15)), [rW, rXk[kc]], [rPP[pp]])

        def rope_store(pp, dst, rdst, srcA, srcB, rsrc):
            if is_ctx:
                k.op("act", lambda e: e.copy(dst[:, g0:g0 + N], srcA), rsrc, [rdst])
            else:
                k.op("dve", lambda e: e.tensor_tensor(RA[:, 0:N], srcA, COS[cb][:, 0:N], op=ALU.mult), rsrc + [rCOS[cb]], [rRA])
                k.op("dve", lambda e: e.tensor_tensor(RB[:, 0:N], srcB, SIN[cb][:, 0:N], op=ALU.mult), rsrc + [rSIN[cb]], [rRB])
                k.op("dve", lambda e: e.tensor_tensor(dst[:, g0:g0 + N], RA[:, 0:N], RB[:, 0:N], op=ALU.add), [rRA, rRB], [rdst])

        for (ca, cbk, dst, rdst) in ((C_RQ, C_RQP, RQ, rRQ), (C_RK, C_RKP, RK, rRK)):
            pp = cnt["pp"] % 4; cnt["pp"] += 1
            proj(ca, pp, 0)
            if not is_ctx:
                proj(cbk, pp, 256)
            rope_store(pp, dst, rdst, PP[pp][:, 0:N], PP[pp][:, 256:256 + N], [rPP[pp]])
        pp = cnt["pp"] % 4; cnt["pp"] += 1
        proj(C_RG, pp, 0)
        k.op("act", lambda e, pp=pp: e.activation(RG[:, g0:g0 + N], PP[pp][:, 0:N], AF.Silu), [rPP[pp]], [rRG])
        for (ca, cbk, dst, rdst, wc) in ((C_AQ, C_AQP, AQ, rAQ, 2), (C_AK, C_AKP, AK, rAK, 4)):
            pp = cnt["pp"] % 4; cnt["pp"] += 1
            proj(ca, pp, 0)
            if not is_ctx:
                proj(cbk, pp, 256)
            stats_rstd(lambda kc, pp=pp: (PP[pp][:, 0:N], rPP[pp]), 1, N, 128.0, (RSQ[:, 0:N], rRSQ))
            k.op("dve", lambda e, pp=pp, wc=wc: e.scalar_tensor_tensor(XN[:, 0, 0:N], PP[pp][:, 0:N], SM[:, wc:wc + 1], RSQ[:, 0:N], op0=ALU.mult, op1=ALU.mult),
                 [rPP[pp], rSM, rRSQ], [rXN])
            if not is_ctx:
                k.op("dve", lambda e, pp=pp, wc=wc: e.scalar_tensor_tensor(XN[:, 1, 0:N], PP[pp][:, 256:256 + N], SM[:, wc + 1:wc + 2], RSQ[:, 0:N], op0=ALU.mult, op1=ALU.mult),
                     [rPP[pp], rSM, rRSQ], [rXN])
            rope_store(pp, dst, rdst, XN[:, 0, 0:N], XN[:, 1, 0:N], [rXN])
        for (cv, dst, rdst) in ((C_RV, RV, rRV), (C_AV, AV, rAV)):
            pp = cnt["pp"] % 4; cnt["pp"] += 1
            for tt in range(N // 128):
                for kc in range(16):
                    k.op("pe", lambda e, kc=kc, tt=tt, pp=pp, cv=cv: e.matmul(PP[pp][:, tt * 128:(tt + 1) * 128], lhsT=H[:, kc, tt * 128:(tt + 1) * 128],
                                                                              rhs=W[:, kc, cv * 128:(cv + 1) * 128], start=(kc == 0), stop=(kc == 15)), [rW, rXk[kc]], [rPP[pp]])
            kt0 = g0 // 128
            k.op("act", lambda e, pp=pp, dst=dst, kt0=kt0: e.copy(dst[:, kt0:kt0 + N // 128, :].rearrange("p a b -> p (a b)"), PP[pp][:, 0:N]), [rPP[pp]], [rdst])

    def phase_b_group(b, q0, N, is_ctx, jloc):
        kts = list(range(nctx // 128)) if is_ctx else list(range(nkt))
        nk = len(kts)
        for i, kt in enumerate(kts):
            pa = cnt["pta"] % 2; cnt["pta"] += 1
            k.op("pe", lambda e, kt=kt, pa=pa: e.matmul(PP[pa][:, 0:N], lhsT=AK[:, kt * 128:(kt + 1) * 128], rhs=AQ[:, q0:q0 + N], start=True, stop=True),
                 [rAK, rAQ], [rPP[pa]])
            k.op("act", lambda e, pa=pa: e.activation(PTa[pa][:, 0:N], PP[pa][:, 0:N], AF.Exp, scale=SC), [rPP[pa]], [rPTa[pa]])
            k.op("pe", lambda e, kt=kt, pa=pa, i=i: e.matmul(PO[:, 0:N], lhsT=AV[:, kt, :], rhs=PTa[pa][:, 0:N], start=(i == 0), stop=(i == nk - 1)),
                 [rAV, rPTa[pa]], [rPO])
            k.op("pe", lambda e, pa=pa, i=i: e.matmul(PR[:, 0:N], lhsT=ONESb[:], rhs=PTa[pa][:, 0:N], start=(i == 0), stop=(i == nk - 1)),
                 [rONESb, rPTa[pa]], [rPR])
            pr = cnt["ptr"] % 2; cnt["ptr"] += 1
            k.op("pe", lambda e, kt=kt, pr=pr: e.matmul(PP[2 + pr][:, 0:N], lhsT=RK[:, kt * 128:(kt + 1) * 128], rhs=RQ[:, q0:q0 + N], start=True, stop=True),
                 [rRK, rRQ], [rPP[2 + pr]])
            q0t = q0 // 128
            if is_ctx:
                m = kt
                k.op("dve", lambda e, pr=pr, m=m: e.tensor_tensor(PTr[pr][:, 0:N], PP[2 + pr][:, 0:N], DD[m][:, 0:N], op=ALU.mult), [rPP[2 + pr], rDD[m]], [rPTr[pr]])
            elif kt < nctx // 128:
                dcb = cnt["dc"] % 2; cnt["dc"] += 1
                mf = q0t - kt
                mb = nkt - q0t + kt
                k.op("dve", lambda e, dcb=dcb, mf=mf: e.tensor_scalar(DC[dcb][:, 0:N], BF_[:, 0:N], CF[:, mf:mf + 1], None, op0=ALU.mult), [rBF, rCF], [rDC[dcb]])
                k.op("dve", lambda e, dcb=dcb, mb=mb: e.scalar_tensor_tensor(DC[dcb][:, 0:N], BB_[:, 0:N], CB[:, mb:mb + 1], DC[dcb][:, 0:N], op0=ALU.mult, op1=ALU.add),
                     [rBB, rCB, rDC[dcb]], [rDC[dcb]])
                k.op("dve", lambda e, pr=pr, dcb=dcb: e.tensor_tensor(PTr[pr][:, 0:N], PP[2 + pr][:, 0:N], DC[dcb][:, 0:N], op=ALU.mult), [rPP[2 + pr], rDC[dcb]], [rPTr[pr]])
            elif kt < q0t:
                mf = q0t - kt
                k.op("dve", lambda e, pr=pr, mf=mf: e.scalar_tensor_tensor(PTr[pr][:, 0:N], PP[2 + pr][:, 0:N], CF[:, mf:mf + 1], BF_[:, 0:N], op0=ALU.mult, op1=ALU.mult),
                     [rPP[2 + pr], rCF, rBF], [rPTr[pr]])
            elif kt < q0t + N // 128:
                m = kt - q0t
                k.op("dve", lambda e, pr=pr, m=m: e.tensor_tensor(PTr[pr][:, 0:N], PP[2 + pr][:, 0:N], DD[m][:, 0:N], op=ALU.mult), [rPP[2 + pr], rDD[m]], [rPTr[pr]])
            else:
                mb = kt - q0t
                k.op("dve", lambda e, pr=pr, mb=mb: e.scalar_tensor_tensor(PTr[pr][:, 0:N], PP[2 + pr][:, 0:N], CB[:, mb:mb + 1], BB_[:, 0:N], op0=ALU.mult, op1=ALU.mult),
                     [rPP[2 + pr], rCB, rBB], [rPTr[pr]])
            k.op("pe", lambda e, kt=kt, pr=pr, i=i: e.matmul(PQ[:, 0:N], lhsT=RV[:, kt, :], rhs=PTr[pr][:, 0:N], start=(i == 0), stop=(i == nk - 1)),
                 [rRV, rPTr[pr]], [rPQ])
        ob = cnt["os"] % 2; cnt["os"] += 1
        k.op("dve", lambda e: e.reciprocal(RI[:, 0:N], PR[:, 0:N]), [rPR], [rRI])
        k.op("dve", lambda e, ob=ob: e.tensor_tensor(OS[ob][:, 0:N], PO[:, 0:N], RI[:, 0:N], op=ALU.mult), [rPO, rRI], [rOS[ob]])
        k.dma("sp", out_h[b, 128:256, q0:q0 + N], OS[ob][:, 0:N], reads=[rOS[ob]], writes=[rOUT])
        k.op("act", lambda e: e.copy(ORS[:, 0:N], PQ[:, 0:N]), [rPQ], [rORS])
        k.op("pe", lambda e: e.matmul(PM[:, 0:N], lhsT=ONES[:], rhs=ORS[:, 0:N], start=True, stop=True), [rONES, rORS], [rPM])
        k.op("dve", lambda e: e.scalar_tensor_tensor(CEN[:, 0:N], PM[:, 0:N], -1.0 / 128, ORS[:, 0:N], op0=ALU.mult, op1=ALU.add), [rPM, rORS], [rCEN])
        k.op("act", lambda e: e.activation(SQB[:, 0:N], CEN[:, 0:N], AF.Square), [rCEN], [rSQB])
        k.op("pe", lambda e: e.matmul(PM[:, 0:N], lhsT=ONES[:], rhs=SQB[:, 0:N], start=True, stop=True), [rONES, rSQB], [rPM])
        k.op("dve", lambda e: e.tensor_scalar(RSB[:, 0:N], PM[:, 0:N], 1.0 / 128, EPS, op0=ALU.mult, op1=ALU.add), [rPM], [rRSB])
        k.op("act", lambda e: e.activation(RSB[:, 0:N], RSB[:, 0:N], AF.Ln), [rRSB], [rRSB])
        k.op("act", lambda e: e.activation(RSB[:, 0:N], RSB[:, 0:N], AF.Exp, scale=-0.5), [rRSB], [rRSB])
        k.op("dve", lambda e: e.tensor_tensor(CEN[:, 0:N], CEN[:, 0:N], RSB[:, 0:N], op=ALU.mult), [rCEN, rRSB], [rCEN])
        ob2 = cnt["os"] % 2; cnt["os"] += 1
        k.op("dve", lambda e, ob2=ob2: e.tensor_tensor(OS[ob2][:, 0:N], CEN[:, 0:N], RG[:, q0:q0 + N], op=ALU.mult), [rCEN, rRG], [rOS[ob2]])
        k.dma("sp", out_h[b, 0:128, q0:q0 + N], OS[ob2][:, 0:N], reads=[rOS[ob2]], writes=[rOUT])

    for b in range(nbatch):
        g0 = 0
        while g0 < tb:
            is_ctx = g0 < nctx
            N = min(GA_, (nctx - g0) if is_ctx else (tb - g0))
            phase_a_group(b, g0, N, 2 if is_ctx else b, is_ctx, g0 - nctx)
            g0 += N
        k.handoff(resA, resB)
        phase_b_group(b, 0, nctx, True, 0)
        q0 = nctx
        while q0 < tb:
            N = min(512, tb - q0)
            phase_b_group(b, q0, N, False, 0)
            q0 += N
        k.handoff(resB, resA)
    counts = k.emit()
    return nc, counts


EPS = 1e-6
NCTX = 256
NLOC = 8192
TB = NCTX + NLOC
GN = 256
HL = 2
NW_ = 1296


def build_O(nbatch=2, tb=TB, nctx=NCTX):
    nc = bass.Bass("TRN2", target_bir_lowering=False)
    nkt = tb // 128
    xT_h = nc.dram_tensor("xT", [2048, nbatch * tb], F32, kind="ExternalInput").ap()
    w_h = nc.dram_tensor("w", [2048, NW_], F32, kind="ExternalInput").ap()
    mods_h = nc.dram_tensor("mods", [128, 96], F32, kind="ExternalInput").ap()
    nw_h = nc.dram_tensor("nw", [128, 16], F32, kind="ExternalInput").ap()
    cw_h = nc.dram_tensor("cw", [128, 36], F32, kind="ExternalInput").ap()
    sm_h = nc.dram_tensor("small", [128, 32], F32, kind="ExternalInput").ap()
    dn_h = nc.dram_tensor("dn", [128, 1024], F32, kind="ExternalInput").ap()
    tri_h = nc.dram_tensor("tri", [128, 640], F32, kind="ExternalInput").ap()
    out_h = nc.dram_tensor("o", [nbatch, tb, 512], F32, kind="ExternalOutput").ap()
    sX = nc.dram_tensor("sX", [nbatch * nkt, 128, 512], F32, kind="ExternalOutput").ap()
    sY = nc.dram_tensor("sY", [nbatch * nkt, 128, 512], F32, kind="ExternalOutput").ap()
    sZ = nc.dram_tensor("sZ", [nbatch * nkt, 128, 512], F32, kind="ExternalOutput").ap()
    sB = nc.dram_tensor("sB", [nbatch * nkt, 128, 384], F32, kind="ExternalOutput").ap()
    sD = nc.dram_tensor("sD", [nbatch * nkt, 128, 16], F32, kind="ExternalOutput").ap()

    k = KB(nc)
    R = k.res
    MODS = k.sb("MODS", [128, 3, 2, 16]); rMODS = R("MODS")
    NW = k.sb("NW", [128, 16]); rNW = R("NW")
    G1 = k.sb("G1", [128, 3, 16]); rG1 = R("G1")
    CW = k.sb("CW", [128, 36]); rCW = R("CW")
    SM = k.sb("SM", [128, 32]); rSM = R("SM")
    AN = k.sb("AN", [128, 16]); rAN = R("AN")
    DN = k.sb("DN", [128, 1024]); rDN = R("DN")
    TRI = k.sb("TRI", [128, 640]); rTRI = R("TRI")
    ONES = k.sb("ONES", [128, 128]); rONES = R("ONES")
    W = k.sb("W", [128, 16, NW_], BF16); rW = R("W")
    k.dma("sp", MODS[:].rearrange("p a b c -> p (a b c)"), mods_h, writes=[rMODS])
    k.dma("sp", NW[:], nw_h, writes=[rNW])
    k.dma("sp", CW[:], cw_h, writes=[rCW])
    k.dma("sp", SM[:], sm_h, writes=[rSM])
    k.dma("sp", DN[:], dn_h, writes=[rDN])
    k.dma("sp", TRI[:], tri_h, writes=[rTRI])
    for kc in range(16):
        k.dma("pool", W[:, kc, :], w_h[kc * 128:(kc + 1) * 128, :], writes=[rW])
    k.op("dve", lambda e: e.memset(ONES[:], 1.0), [], [rONES])
    for s_ in range(3):
        k.op("dve", lambda e, s_=s_: e.scalar_tensor_tensor(G1[:, s_, :], MODS[:, s_, 1, :], 1.0, NW[:], op0=ALU.add, op1=ALU.mult), [rMODS, rNW], [rG1])
    k.op("act", lambda e: e.activation(AN[:], SM[:, 16:32], AF.Exp), [rSM], [rAN])
    k.op("dve", lambda e: e.tensor_scalar(AN[:], AN[:], -1.0, None, op0=ALU.mult), [rAN], [rAN])
    TRIv = [TRI[:, 0:128], TRI[:, 128:256]]
    STRv = [TRI[:, 256:384], TRI[:, 384:512]]
    ID = TRI[:, 512:640]
    DSK = DN[:, 0:512]
    NWO = DN[:, 512:1024]
    DTB = SM[:, 0:16]
    XW = GN + 2 * HL
    XR = k.sb("XR", [128, 16 * XW])
    XRb = XR.bitcast(BF16)
    X = XR[:].rearrange("p (kc t) -> p kc t", kc=16)
    H = XRb[:].rearrange("p (kc t) -> p kc t", kc=16)[:, :, 0:XW]
    rXk = [R(f"X{i}") for i in range(16)]
    SQ = [k.sb(f"SQ{i}", [128, XW]) for i in range(2)]; rSQ = [R(f"SQ{i}") for i in range(2)]
    RS = k.sb("RS", [128, XW]); rRS = R("RS")
    TMP = [k.sb(f"TMP{i}", [128, XW]) for i in range(2)]; rTMP = [R(f"TMP{i}") for i in range(2)]
    ACC = [k.sb(f"ACC{i}", [128, GN]) for i in range(2)]; rACC = [R(f"ACC{i}") for i in range(2)]
    XC = k.sb("XC", [128, 6, GN]); rXC = [R(f"XC{i}") for i in range(6)]
    NS = 2
    XT = [k.sb(f"XT{i}", [128, 512]) for i in range(NS)]; rXT = [R(f"XT{i}") for i in range(NS)]
    YF = [k.sb(f"YF{i}", [128, 512]) for i in range(NS)]; rYF = [R(f"YF{i}") for i in range(NS)]
    ZS = [k.sb(f"ZS{i}", [128, 512]) for i in range(NS)]; rZS = [R(f"ZS{i}") for i in range(NS)]
    BC = [k.sb(f"BC{i}", [128, 384]) for i in range(NS)]; rBC = [R(f"BC{i}") for i in range(NS)]
    DT = [k.sb(f"DT{i}", [128, 16]) for i in range(NS)]; rDT = [R(f"DT{i}") for i in range(NS)]
    ST = [k.sb(f"ST{d}", [128, 512]) for d in range(2)]; rST = [R(f"ST{d}") for d in range(2)]
    DTA = k.sb("DTA", [128, 8]); rDTA = R("DTA")
    CSS = k.sb("CSS", [128, 16]); rCSS = R("CSS")
    ECS = k.sb("ECS", [128, 8]); rECS = R("ECS")
    TE = k.sb("TE", [128, 8]); rTE = R("TE")
    DEC = k.sb("DEC", [128, 8]); rDEC = R("DEC")
    CBM = k.sb("CBM", [128, 128]); rCBM = R("CBM")
    LH = [k.sb(f"LH{i}", [128, 128]) for i in range(2)]; rLH = [R(f"LH{i}") for i in range(2)]
    EX = [k.sb(f"EX{i}", [128, 128]) for i in range(2)]; rEX = [R(f"EX{i}") for i in range(2)]
    WH = [k.sb(f"WH{i}", [128, 128]) for i in range(2)]; rWH = [R(f"WH{i}") for i in range(2)]
    T1 = k.sb("T1", [128, 512]); rT1 = R("T1")
    XS = k.sb("XS", [128, 512]); rXS = R("XS")
    GB = k.sb("GB", [128, 512]); rGB = R("GB")
    GSQ = k.sb("GSQ", [128, 512]); rGSQ = R("GSQ")
    SS = k.sb("SS", [128, 1]); rSS = R("SS")
    OB = [k.sb(f"OB{i}", [128, 512]) for i in range(2)]; rOB = [R(f"OB{i}") for i in range(2)]
    PP = [k.ps(f"PP{i}") for i in range(2)]; rPP = [R(f"PP{i}") for i in range(2)]
    PTr = k.ps("PTr"); rPTr = R("PTr")
    PMs = k.ps("PMs"); rPMs = R("PMs")
    PSm = k.ps("PSm"); rPScs = rPScb = rPSdt = rPSbk = R("PSm")
    PSg = k.ps("PSg"); rPSg = R("PSg")
    PY = k.ps("PY"); rPY = R("PY")
    PYO = k.ps("PYO"); rPYO = R("PYO")
    rOUT = R("OUT")
    rsX = [R(f"sX{i}") for i in range(NS)]; rsY = [R(f"sY{i}") for i in range(NS)]; rsZ = [R(f"sZ{i}") for i in range(NS)]; rsB = [R(f"sB{i}") for i in range(NS)]; rsD = [R(f"sD{i}") for i in range(NS)]
    cnt = dict(sq=0, tmp=0, pp=0, acc=0, set=0, lh=0, ob=0)

    def stats_rstd(srcs, N, div, ors, rr):
        nk = len(srcs)
        for i, (ap, rs_) in enumerate(srcs):
            sb_ = cnt["sq"] % 2; cnt["sq"] += 1
            k.op("act", lambda e, ap=ap, sb_=sb_: e.activation(SQ[sb_][:, 0:N], ap, AF.Square), [rs_], [rSQ[sb_]])
            k.op("pe", lambda e, i=i, sb_=sb_: e.matmul(PMs[:, 0:N], lhsT=ONES[:], rhs=SQ[sb_][:, 0:N], start=(i == 0), stop=(i == nk - 1)),
                 [rONES, rSQ[sb_]], [rPMs])
        k.op("dve", lambda e: e.tensor_scalar(ors, PMs[:, 0:N], 1.0 / div, EPS, op0=ALU.mult, op1=ALU.add), [rPMs], [rr])
        k.op("act", lambda e: e.activation(ors, ors, AF.Ln), [rr], [rr])
        k.op("act", lambda e: e.activation(ors, ors, AF.Exp, scale=-0.5), [rr], [rr])

    def ssd_dir(d, s, first_chunk):
        BT = BC[s][:, 0:128]; CT = BC[s][:, 128:256]; BK = BC[s][:, 256:384]
        dts = DT[s][:, 8 * d:8 * d + 8]
        k.op("dve", lambda e: e.tensor_tensor(DTA[:], dts, AN[:, 8 * d:8 * d + 8], op=ALU.mult), [rDT[s], rAN], [rDTA])
        k.op("pe", lambda e: e.matmul(PSm[:, 0:8], lhsT=TRIv[d], rhs=DTA[:], start=True, stop=True), [rTRI, rDTA], [rPScs])
        k.op("pe", lambda e: e.matmul(PSm[:, 8:16], lhsT=ONES[:], rhs=DTA[:], start=True, stop=True), [rONES, rDTA], [rPScs])
        k.op("act", lambda e: e.copy(CSS[:], PSm[:, 0:16]), [rPScs], [rCSS])
        k.op("act", lambda e: e.activation(ECS[:], CSS[:, 0:8], AF.Exp), [rCSS], [rECS])
        k.op("act", lambda e: e.activation(DEC[:], CSS[:, 8:16], AF.Exp), [rCSS], [rDEC])
        k.op("dve", lambda e: e.tensor_tensor(TE[:], CSS[:, 8:16], CSS[:, 0:8], op=ALU.subtract), [rCSS], [rTE])
        k.op("act", lambda e: e.activation(TE[:], TE[:], AF.Exp), [rTE], [rTE])
        k.op("dve", lambda e: e.tensor_tensor(TE[:], TE[:], dts, op=ALU.mult), [rTE, rDT[s]], [rTE])
        k.op("pe", lambda e: e.matmul(PSm[:, 128:256], lhsT=BT, rhs=CT, start=True, stop=True), [rBC[s]], [rPScb])
        k.op("dve", lambda e: e.tensor_tensor(CBM[:], PSm[:, 128:256], TRIv[d], op=ALU.mult), [rPScb, rTRI], [rCBM])
        for h in range(8):
            lb = cnt["lh"] % 2; cnt["lh"] += 1
            k.op("dve", lambda e, h=h, lb=lb: e.tensor_scalar(LH[lb][:], STRv[d], DTA[:, h:h + 1], None, op0=ALU.mult), [rTRI, rDTA], [rLH[lb]])
            k.op("pe", lambda e, lb=lb: e.matmul(PSg[:, 0:128], lhsT=LH[lb][:], rhs=TRIv[d], start=True, stop=True), [rLH[lb], rTRI], [rPSg])
            k.op("act", lambda e, lb=lb: e.activation(EX[lb][:], PSg[:, 0:128], AF.Exp), [rPSg], [rEX[lb]])
            k.op("dve", lambda e, h=h, lb=lb: e.scalar_tensor_tensor(WH[lb][:], EX[lb][:], DT[s][:, 8 * d + h:8 * d + h + 1], CBM[:], op0=ALU.mult, op1=ALU.mult),
                 [rEX[lb], rDT[s], rCBM], [rWH[lb]])
            k.op("pe", lambda e, h=h, lb=lb: e.matmul(PY[:, h * 64:(h + 1) * 64], lhsT=WH[lb][:], rhs=XT[s][:, h * 64:(h + 1) * 64], start=True, stop=True),
                 [rWH[lb], rXT[s]], [rPY])
        if not first_chunk:
            k.op("pe", lambda e: e.matmul(PYO[:], lhsT=CT, rhs=ST[d][:], start=True, stop=True), [rBC[s], rST[d]], [rPYO])
            k.op("dve", lambda e: e.tensor_tensor(T1[:].rearrange("p (h q) -> p h q", h=8), PYO[:].rearrange("p (h q) -> p h q", h=8),
                                                  ECS[:].unsqueeze(2).to_broadcast([128, 8, 64]), op=ALU.mult), [rPYO, rECS], [rT1])
        k.op("dve", lambda e: e.tensor_tensor(XS[:].rearrange("p (h q) -> p h q", h=8), XT[s][:].rearrange("p (h q) -> p h q", h=8),
                                              TE[:].unsqueeze(2).to_broadcast([128, 8, 64]), op=ALU.mult), [rXT[s], rTE], [rXS])
        k.op("pe", lambda e: e.matmul(PYO[:], lhsT=BK, rhs=XS[:], start=True, stop=True), [rBC[s], rXS], [rPYO])
        if first_chunk:
            k.op("act", lambda e: e.copy(ST[d][:], PYO[:]), [rPYO], [rST[d]])
        else:
            k.op("dve", lambda e: e.tensor_tensor(ST[d][:].rearrange("p (h q) -> p h q", h=8), ST[d][:].rearrange("p (h q) -> p h q", h=8),
                                                  DEC[:].unsqueeze(2).to_broadcast([128, 8, 64]), op=ALU.mult), [rST[d], rDEC], [rST[d]])
            k.op("dve", lambda e: e.tensor_tensor(ST[d][:], ST[d][:], PYO[:], op=ALU.add), [rST[d], rPYO], [rST[d]])

    def group_pass1(b, seg0, seg1, g0, ms, first_group):
        N = GN
        lo = max(g0 - HL, seg0); hi = min(g0 + N + HL, seg1)
        c_lo = lo - (g0 - HL); c_hi = hi - (g0 - HL); NV = c_hi - c_lo
        col0 = b * tb
        k.dma("sp", X[:, :, c_lo:c_hi], xT_h[:, col0 + lo:col0 + hi].rearrange("(kc p) t -> p kc t", p=128), writes=rXk, dres=rXk[0])
        stats_rstd([(X[:, kc, c_lo:c_hi], rXk[kc]) for kc in range(16)], NV, 2048.0, RS[:, 0:NV], rRS)
        for kc in range(16):
            tb_ = cnt["tmp"] % 2; cnt["tmp"] += 1
            k.op("dve", lambda e, kc=kc, tb_=tb_: e.scalar_tensor_tensor(TMP[tb_][:, 0:NV], X[:, kc, c_lo:c_hi], G1[:, ms, kc:kc + 1], RS[:, 0:NV],
                                                                        op0=ALU.mult, op1=ALU.mult), [rXk[kc], rG1, rRS], [rTMP[tb_]])
            k.op("act", lambda e, kc=kc, tb_=tb_: e.activation(H[:, kc, c_lo:c_hi], TMP[tb_][:, 0:NV], AF.Identity,
                                                               bias=MODS[:, ms, 0, kc:kc + 1], scale=1.0), [rTMP[tb_], rMODS], [rXk[kc]])
        for blk in range(6):
            pp = cnt["pp"] % 2; cnt["pp"] += 1
            for kc in range(16):
                k.op("pe", lambda e, kc=kc, pp=pp, blk=blk: e.matmul(PP[pp][:, c_lo:c_hi], lhsT=W[:, kc, blk * 128:(blk + 1) * 128], rhs=H[:, kc, c_lo:c_hi],
                                                                     start=(kc == 0), stop=(kc == 15)), [rW, rXk[kc]], [rPP[pp]])
            ab = cnt["acc"] % 2; cnt["acc"] += 1
            k.op("dve", lambda e, pp=pp, blk=blk, ab=ab: e.tensor_scalar(ACC[ab][:, 0:N], PP[pp][:, HL:HL + N], CW[:, blk * 5 + 2:blk * 5 + 3], None, op0=ALU.mult),
                 [rPP[pp], rCW], [rACC[ab]])
            for tap in (0, 1, 3, 4):
                off = tap - 2
                t_lo = max(g0, lo - off); t_hi = min(g0 + N, hi - off)
                o_lo = t_lo - g0; o_hi = t_hi - g0
                i_lo = o_lo + HL + off; i_hi = o_hi + HL + off
                k.op("dve", lambda e, pp=pp, blk=blk, ab=ab, tap=tap, o_lo=o_lo, o_hi=o_hi, i_lo=i_lo, i_hi=i_hi: e.scalar_tensor_tensor(
                    ACC[ab][:, o_lo:o_hi], PP[pp][:, i_lo:i_hi], CW[:, blk * 5 + tap:blk * 5 + tap + 1], ACC[ab][:, o_lo:o_hi], op0=ALU.mult, op1=ALU.add),
                    [rPP[pp], rCW, rACC[ab]], [rACC[ab]])
            k.op("act", lambda e, blk=blk, ab=ab: e.activation(XC[:, blk, :], ACC[ab][:, 0:N], AF.Silu, bias=CW[:, 30 + blk:31 + blk], scale=1.0),
                 [rACC[ab], rCW], [rXC[blk]])
        for ch in range(N // 128):
            c = (g0 + ch * 128) // 128
            sc = b * nkt + c
            s = cnt["set"] % NS; cnt["set"] += 1
            hc0 = HL + ch * 128
            pp = cnt["pp"] % 2; cnt["pp"] += 1
            for kc in range(16):
                k.op("pe", lambda e, kc=kc, pp=pp, hc0=hc0: e.matmul(PP[pp][:], lhsT=H[:, kc, hc0:hc0 + 128], rhs=W[:, kc, 768:1280], start=(kc == 0), stop=(kc == 15)),
                     [rW, rXk[kc]], [rPP[pp]])
            k.op("act", lambda e, pp=pp, s=s: e.activation(ZS[s][:], PP[pp][:], AF.Silu), [rPP[pp]], [rZS[s]])
            k.dma("sp", sZ[sc], ZS[s][:], reads=[rZS[s]], writes=[rsZ[s]], nowaw=True)
            for kc in range(16):
                k.op("pe", lambda e, kc=kc, hc0=hc0: e.matmul(PSm[:, 256:272], lhsT=H[:, kc, hc0:hc0 + 128], rhs=W[:, kc, 1280:1296], start=(kc == 0), stop=(kc == 15)),
                     [rW, rXk[kc]], [rPSdt])
            k.op("dve", lambda e, s=s: e.tensor_tensor(DT[s][:], PSm[:, 256:272], DTB, op=ALU.add), [rPSdt, rSM], [rDT[s]])
            k.op("act", lambda e, s=s: e.activation(DT[s][:], DT[s][:], AF.Exp), [rDT[s]], [rDT[s]])
            k.op("dve", lambda e, s=s: e.tensor_scalar(DT[s][:], DT[s][:], 1.0, None, op0=ALU.add), [rDT[s]], [rDT[s]])
            k.op("act", lambda e, s=s: e.activation(DT[s][:], DT[s][:], AF.Ln), [rDT[s]], [rDT[s]])
            k.dma("sp", sD[sc], DT[s][:], reads=[rDT[s]], writes=[rsD[s]], nowaw=True)
            for q in range(4):
                k.op("pe", lambda e, q=q, ch=ch: e.transpose(PTr[:, q * 128:(q + 1) * 128], XC[:, q, ch * 128:(ch + 1) * 128], ID), [rXC[q], rTRI], [rPTr])
            k.op("act", lambda e, s=s: e.copy(XT[s][:], PTr[:]), [rPTr], [rXT[s]])
            k.dma("sp", sX[sc], XT[s][:], reads=[rXT[s]], writes=[rsX[s]], nowaw=True)
            k.op("dve", lambda e, s=s, ch=ch: e.tensor_copy(BC[s][:, 0:256].rearrange("p (a t) -> p a t", a=2), XC[:, 4:6, ch * 128:(ch + 1) * 128]), [rXC[4], rXC[5]], [rBC[s]])
            k.op("pe", lambda e, ch=ch: e.transpose(PSm[:, 384:512], XC[:, 4, ch * 128:(ch + 1) * 128], ID), [rXC[4], rTRI], [rPSbk])
            k.op("act", lambda e, s=s: e.copy(BC[s][:, 256:384], PSm[:, 384:512]), [rPSbk], [rBC[s]])
            k.dma("sp", sB[sc], BC[s][:], reads=[rBC[s]], writes=[rsB[s]], nowaw=True)
            fc = first_group and ch == 0
            ssd_dir(0, s, fc)
            k.op("dve", lambda e, s=s: e.tensor_tensor(YF[s][:], XT[s][:], DSK, op=ALU.mult), [rXT[s], rDN], [rYF[s]])
            if not fc:
                k.op("dve", lambda e, s=s: e.tensor_tensor(YF[s][:], YF[s][:], T1[:], op=ALU.add), [rYF[s], rT1], [rYF[s]])
            k.op("dve", lambda e, s=s: e.tensor_tensor(YF[s][:], PY[:], YF[s][:], op=ALU.add), [rPY, rYF[s]], [rYF[s]])
            k.dma("sp", sY[sc], YF[s][:], reads=[rYF[s]], writes=[rsY[s]], nowaw=True)

    def chunk_pass2(b, c, first_chunk, tok0):
        sc = b * nkt + c
        s = cnt["set"] % NS; cnt["set"] += 1
        k.dma("sp", XT[s][:], sX[sc], reads=rsX, writes=[rXT[s]])
        k.dma("sp", BC[s][:], sB[sc], reads=rsB, writes=[rBC[s]])
        k.dma("sp", DT[s][:], sD[sc], reads=rsD, writes=[rDT[s]])
        k.dma("sp", ZS[s][:], sZ[sc], reads=rsZ, writes=[rZS[s]])
        k.dma("sp", YF[s][:], sY[sc], reads=rsY, writes=[rYF[s]])
        ssd_dir(1, s, first_chunk)
        if not first_chunk:
            k.op("dve", lambda e, s=s: e.tensor_tensor(YF[s][:], YF[s][:], T1[:], op=ALU.add), [rYF[s], rT1], [rYF[s]])
        k.op("dve", lambda e, s=s: e.tensor_tensor(YF[s][:], PY[:], YF[s][:], op=ALU.add), [rPY, rYF[s]], [rYF[s]])
        k.op("dve", lambda e, s=s: e.tensor_tensor(GB[:], YF[s][:], ZS[s][:], op=ALU.mult), [rYF[s], rZS[s]], [rGB])
        k.op("act", lambda e: e.activation(GSQ[:], GB[:], AF.Square, accum_out=SS[:]), [rGB], [rGSQ, rSS])
        k.op("dve", lambda e: e.tensor_scalar(SS[:], SS[:], 1.0 / 512, EPS, op0=ALU.mult, op1=ALU.add), [rSS], [rSS])
        k.op("act", lambda e: e.activation(SS[:], SS[:], AF.Ln), [rSS], [rSS])
        k.op("act", lambda e: e.activation(SS[:], SS[:], AF.Exp, scale=-0.5), [rSS], [rSS])
        ob = cnt["ob"] % 2; cnt["ob"] += 1
        k.op("dve", lambda e, ob=ob: e.scalar_tensor_tensor(OB[ob][:], GB[:], SS[:, 0:1], NWO, op0=ALU.mult, op1=ALU.mult), [rGB, rSS, rDN], [rOB[ob]])
        k.dma("sp", out_h[b, tok0:tok0 + 128, :], OB[ob][:], reads=[rOB[ob]], writes=[rOUT])

    for b in range(nbatch):
        first = True
        for (seg0, seg1, ms) in ((0, nctx, 2), (nctx, tb, b)):
            g0 = seg0
            while g0 < seg1:
                group_pass1(b, seg0, seg1, g0, ms, first)
                first = False
                g0 += GN
        order = list(range(nctx // 128 - 1, -1, -1)) + list(range(nkt - 1, nctx // 128 - 1, -1))
        for i, c in enumerate(order):
            chunk_pass2(b, c, i == 0, c * 128)
    counts = k.emit()
    return nc, counts


_CACHE = {}


def _prog(key, fn):
    if key not in _CACHE:
        _CACHE[key] = fn()[0]
    return _CACHE[key]


def _pk(v):
    return np.ascontiguousarray(np.asarray(v, np.float32).reshape(16, 128).T)


def _rope_tabs(nloc, grid_w=64, theta=10000.0):
    rows = nloc // grid_w
    row = np.repeat(np.arange(rows, dtype=np.float32), grid_w)
    col = np.tile(np.arange(grid_w, dtype=np.float32), rows)
    inv = (np.float32(theta) ** (-np.arange(0, 64, 2, dtype=np.float32) / np.float32(64))).astype(np.float32)
    ar = row[:, None] * inv[None]
    ac = col[:, None] * inv[None]
    cosT = np.zeros((128, nloc), np.float32)
    sinT = np.zeros((128, nloc), np.float32)
    for blk, ang in ((0, ar), (1, ac)):
        c = np.cos(ang).T.astype(np.float32)
        s = np.sin(ang).T.astype(np.float32)
        cosT[blk * 64:blk * 64 + 32] = c
        cosT[blk * 64 + 32:blk * 64 + 64] = c
        sinT[blk * 64:blk * 64 + 32] = -s
        sinT[blk * 64 + 32:blk * 64 + 64] = s
    return cosT, sinT


_PERM = np.array([d + 32 if (d % 64) < 32 else d - 32 for d in range(128)])


def _tri_consts():
    j = np.arange(128)[:, None]
    i = np.arange(128)[None, :]
    return np.concatenate([(j <= i), (j >= i), (j > i), (j < i), np.eye(128, dtype=bool)], 1).astype(np.float32)


def _e_inputs(h, xT, w_in, rd, qw, kw, mods, nw, cosT, sinT):
    def blk(c0):
        return w_in[:, c0:c0 + 128]
    rq = blk(h * 128); rk = blk(1024 + h * 128); rv = blk(2048 + h * 128); rg = blk(3072 + h * 128)
    aq = blk(4096 + h * 128); ak = blk(5120 + (h // 4) * 128); av = blk(5376 + (h // 4) * 128)
    w = np.concatenate([rq, rq[:, _PERM], rk, rk[:, _PERM], rg, aq, aq[:, _PERM], ak, ak[:, _PERM], rv, av], 1)
    small = np.zeros((128, 8), np.float32)
    small[:, 0] = rd[0, h]; small[:, 1] = rd[1, h]
    small[:, 2] = qw; small[:, 3] = qw[_PERM]; small[:, 4] = kw; small[:, 5] = kw[_PERM]
    return dict(xT=xT, w=np.ascontiguousarray(w), mods=mods, nw=nw, small=small, cosT=cosT, sinT=sinT)


def _o_inputs(g, xT, w_in, conv_w, conv_b, dt_bias, a_log, d_skip, norm_w, mods, nw, tri):
    xs_c = 4096 + 512 * g; b_c = 8192 + 128 * g; c_c = 8192 + 1024 + 128 * g; z_c = 512 * g
    dt_cols = [10240 + d * 64 + 8 * g + j for d in range(2) for j in range(8)]
    w = np.concatenate([w_in[:, xs_c:xs_c + 512], w_in[:, b_c:b_c + 128], w_in[:, c_c:c_c + 128], w_in[:, z_c:z_c + 512], w_in[:, dt_cols]], 1)
    ch = np.concatenate([512 * g + np.arange(512), 4096 + 128 * g + np.arange(128), 5120 + 128 * g + np.arange(128)])
    cw = np.zeros((128, 36), np.float32)
    for blk in range(6):
        cc = ch[blk * 128:(blk + 1) * 128]
        cw[:, blk * 5:blk * 5 + 5] = conv_w[:, cc].T
        cw[:, 30 + blk] = conv_b[cc]
    small = np.zeros((128, 32), np.float32)
    small[:, 0:16] = np.concatenate([dt_bias[0, 8 * g:8 * g + 8], dt_bias[1, 8 * g:8 * g + 8]])[None]
    small[:, 16:32] = np.concatenate([a_log[0, 8 * g:8 * g + 8], a_log[1, 8 * g:8 * g + 8]])[None]
    dn = np.zeros((128, 1024), np.float32)
    dn[:, 0:512] = np.repeat(d_skip[8 * g:8 * g + 8], 64)[None]
    dn[:, 512:1024] = norm_w[512 * g:512 * g + 512][None]
    return dict(xT=xT, w=np.ascontiguousarray(w), mods=mods, nw=nw, cw=cw, small=small, dn=dn, tri=tri)


def kernel(x, c, ctx, c_ctx, ada_w, ada_b, norm1_w, norm2_w, ev_w_in, ev_w_out, ev_ret_decay,
           ev_q_norm, ev_k_norm, od_w_in, od_conv_w, od_conv_b, od_dt_bias, od_a_log, od_d,
           od_norm_w, od_w_out, peer_wq, peer_keys, peer_u, peer_v, final_norm_w):
    f32 = lambda a: np.asarray(a, dtype=np.float32)
    x = f32(x); ctx = f32(ctx); c = f32(c); c_ctx = f32(c_ctx)
    NCORE = 8
    cores = list(range(NCORE))
    DEPTH = 4
    cs = np.stack([c[0], c[1], c_ctx])
    cT = np.ascontiguousarray(cs.reshape(3, 16, 128).transpose(2, 1, 0)).reshape(128, 48)
    ncA = _prog("A", lambda: build_A(DEPTH, 1536))
    ada_w = f32(ada_w); ada_b = f32(ada_b)
    in_maps = [dict(cT=cT, w=np.ascontiguousarray(ada_w[:, :, i * 1536:(i + 1) * 1536]),
                    b=np.ascontiguousarray(ada_b[:, i * 1536:(i + 1) * 1536]).reshape(1, -1)) for i in cores]
    res = run_bass_kernel_spmd(ncA, in_maps, core_ids=cores)
    mods_all = np.zeros((DEPTH, 3, 12288), np.float32)
    for i in cores:
        m = res.results[i]["mod"].reshape(3, DEPTH, 1536)
        for l in range(DEPTH):
            mods_all[l, :, i * 1536:(i + 1) * 1536] = m[:, l]
    cosT, sinT = _rope_tabs(8192)
    tri = _tri_consts()
    ident = np.eye(128, dtype=np.float32)
    xc = ctx
    TBK = 256 + 8192
    for layer in range(DEPTH):
        last = layer == DEPTH - 1
        j = layer // 2
        mv = mods_all[layer].reshape(3, 6, 2048)
        mods_m = np.stack([np.stack([_pk(mv[r, 0]), _pk(mv[r, 1])]) for r in range(3)])
        mods_m = np.ascontiguousarray(mods_m.transpose(2, 0, 1, 3)).reshape(128, 96)
        nw1 = _pk(f32(norm1_w)[layer])
        xcat = np.concatenate([np.concatenate([xc[b], x[b]], 0) for b in range(2)], 0)
        xT_all = np.ascontiguousarray(xcat.T)
        if layer % 2 == 0:
            ncE = _prog("E", lambda: build_E(2))
            in_maps = [_e_inputs(h, xT_all, f32(ev_w_in)[j], f32(ev_ret_decay)[j], f32(ev_q_norm)[j], f32(ev_k_norm)[j], mods_m, nw1, cosT, sinT)
                       for h in cores]
            res = run_bass_kernel_spmd(ncE, in_maps, core_ids=cores)
            KO = 2048
            o_all = np.zeros((2, TBK, KO), np.float32)
            for h in cores:
                r = res.results[h]["oT"]
                for b in range(2):
                    o_all[b, :, h * 128:(h + 1) * 128] = r[b, 0:128].T
                    o_all[b, :, 1024 + h * 128:1024 + (h + 1) * 128] = r[b, 128:256].T
            w_out = f32(ev_w_out)[j]
        else:
            ncO = _prog("O", lambda: build_O(2))
            in_maps = [_o_inputs(g, xT_all, f32(od_w_in)[j], f32(od_conv_w)[j], f32(od_conv_b)[j], f32(od_dt_bias)[j], f32(od_a_log)[j],
                                 f32(od_d)[j], f32(od_norm_w)[j], mods_m, nw1, tri) for g in cores]
            res = run_bass_kernel_spmd(ncO, in_maps, core_ids=cores)
            KO = 4096
            o_all = np.zeros((2, TBK, KO), np.float32)
            for g in cores:
                r = res.results[g]["o"]
                for b in range(2):
                    o_all[b, :, g * 512:(g + 1) * 512] = r[b]
            w_out = f32(od_w_out)[j]
        del res, in_maps, xT_all, xcat
        if last:
            groups = [(0, 512, 0), (512, 512, 0), (1024, 512, 0), (1536, 512, 0)]
        else:
            groups = [(0, 512, 0), (512, 512, 0), (1024, 512, 0), (1536, 512, 0), (2048, 128, 1)]
        T = groups[-1][0] + groups[-1][1]
        KOc = KO // 128
        ncD = _prog(("D", KOc, last), lambda: build_D(KOc, groups, last))
        nw2 = np.ascontiguousarray(np.concatenate([_pk(f32(norm2_w)[layer]), _pk(f32(final_norm_w))], 1))
        keysT = np.ascontiguousarray(f32(peer_keys)[layer].transpose(3, 0, 1, 2)).reshape(128, 2048)
        uT = np.ascontiguousarray(f32(peer_u)[layer].T)
        vv = np.ascontiguousarray(f32(peer_v)[layer])
        wq = np.ascontiguousarray(f32(peer_wq)[layer])
        in_maps = []
        for ci in cores:
            b = ci // 4; q = ci % 4
            parts_x = [x[b, q * 2048:(q + 1) * 2048]]
            parts_o = [o_all[b, 256 + q * 2048:256 + (q + 1) * 2048]]
            if not last:
                parts_x += [xc[b, q * 64:(q + 1) * 64], np.zeros((64, 2048), np.float32)]
                parts_o += [o_all[b, q * 64:(q + 1) * 64], np.zeros((64, KO), np.float32)]
            xT = np.ascontiguousarray(np.concatenate(parts_x, 0).T)
            oT = np.ascontiguousarray(np.concatenate(parts_o, 0).T)
            md = np.stack([np.stack([_pk(mv[b, i]) for i in range(6)]), np.stack([_pk(mv[2, i]) for i in range(6)])])
            md = np.ascontiguousarray(md.transpose(2, 0, 1, 3)).reshape(128, 192)
            in_maps.append(dict(xT=xT, oT=oT, w_out=w_out, mods=md, nw=nw2, wq=wq, keysT=keysT, uT=uT, v=vv, ident=ident))
        res = run_bass_kernel_spmd(ncD, in_maps, core_ids=cores)
        x_new = np.zeros_like(x)
        xc_new = np.zeros_like(xc)
        for ci in cores:
            b = ci // 4; q = ci % 4
            r = res.results[ci]["outT"]
            x_new[b, q * 2048:(q + 1) * 2048] = r[:, 0:2048].T
            if not last:
                xc_new[b, q * 64:(q + 1) * 64] = r[:, 2048:2112].T
        x = x_new
        xc = xc_new
        del res, in_maps, o_all, uT, vv
    return x
```

```python
import numpy as np
from contextlib import ExitStack
import concourse.bass as bass
import concourse.mybir as mybir
from concourse.bass_utils import run_bass_kernel_spmd

F32 = mybir.dt.float32
BF16 = mybir.dt.bfloat16
AF = mybir.ActivationFunctionType
ALU = mybir.AluOpType
AX = mybir.AxisListType


class Res:
    __slots__ = ("name", "w", "rd", "dsem", "dcnt")

    def __init__(self, name):
        self.name = name
        self.w = None
        self.rd = []
        self.dsem = None
        self.dcnt = 0


class Op:
    __slots__ = ("eng", "fn", "deps", "isdma", "dres", "dval", "inc", "incval")

    def __init__(self, eng, fn, isdma=False):
        self.eng = eng
        self.fn = fn
        self.deps = []
        self.isdma = isdma
        self.dres = None
        self.dval = 0
        self.inc = False
        self.incval = 0


ENGS = ("pe", "act", "dve", "pool", "sp")


class KB:
    def __init__(self, nc):
        self.nc = nc
        self.ops = {e: [] for e in ENGS}
        self.es = ExitStack()
        self.dma_res = []
        self.nres = 0

    def sb(self, name, shape, dt=F32):
        return self.es.enter_context(self.nc.sbuf_tensor(name, list(shape), dt))

    def ps(self, name, shape=(128, 512), dt=F32):
        return self.es.enter_context(self.nc.psum_tensor(name, list(shape), dt))

    def res(self, name=None):
        self.nres += 1
        return Res(name or f"r{self.nres}")

    def handoff(self, frm, to):
        acc = []
        for f in frm:
            if f.w is not None:
                acc.append(f.w)
            acc.extend(f.rd)
        for t in to:
            t.rd = list(t.rd) + acc

    def _track(self, op, reads, writes, nowaw=False):
        deps = op.deps
        for r in reads:
            if r.w is not None:
                deps.append((r.w, "raw"))
            if not op.isdma:
                r.rd = [o for o in r.rd if o.isdma or o.eng != op.eng]
            r.rd.append(op)
        for w in writes:
            if w.w is not None and not nowaw:
                deps.append((w.w, "waw"))
            for o in w.rd:
                if o is not op:
                    deps.append((o, "war"))
            w.w = op
            w.rd = []

    def op(self, eng, fn, reads=(), writes=()):
        o = Op(eng, fn)
        self._track(o, reads, writes)
        self.ops[eng].append(o)
        return o

    def dma(self, eng, out, in_, reads=(), writes=(), dres=None, nowaw=False, **kw):
        def fn(e):
            return e.dma_start(out=out, in_=in_, **kw)
        o = Op(eng, fn, isdma=True)
        d = dres if dres is not None else writes[0]
        if d.dsem is None:
            d.dsem = self.es.enter_context(self.nc.semaphore(f"d_{d.name}_{len(self.dma_res)}"))
            self.dma_res.append(d)
        d.dcnt += 1
        o.dres = d
        o.dval = 16 * d.dcnt
        self._track(o, reads, writes, nowaw)
        self.ops[eng].append(o)
        return o

    def emit(self, final_wait_eng="sp"):
        nc = self.nc
        sems = {e: self.es.enter_context(nc.semaphore(f"s_{e}")) for e in ENGS}
        for e in ENGS:
            for o in self.ops[e]:
                for (d, kind) in o.deps:
                    if d.isdma:
                        continue
                    if d.eng == o.eng and kind != "raw":
                        continue
                    d.inc = True
        for e in ENGS:
            c = 0
            for o in self.ops[e]:
                if o.inc and not o.isdma:
                    c += 1
                    o.incval = c
        fin = [(d.dsem, 16 * d.dcnt) for d in self.dma_res]
        ops = self.ops

        def run(engname, e):
            waited = {}
            for o in ops[engname]:
                need = {}
                for (d, kind) in o.deps:
                    if d.isdma:
                        key = d.dres.dsem
                        val = d.dval
                    else:
                        if d.eng == o.eng and kind != "raw":
                            continue
                        key = sems[d.eng]
                        val = d.incval
                    if need.get(key, 0) < val:
                        need[key] = val
                for key, val in need.items():
                    if waited.get(key, 0) < val:
                        e.wait_ge(key, val)
                        waited[key] = val
                ins = o.fn(e)
                if o.isdma:
                    ins.then_inc(o.dres.dsem, 16)
                elif o.inc:
                    ins.then_inc(sems[engname], 1)
            if engname == final_wait_eng:
                for s, v in fin:
                    if waited.get(s, 0) < v:
                        e.wait_ge(s, v)

        with nc.Block() as block:
            @block.tensor
            def _(e):
                run("pe", e)

            @block.scalar
            def _(e):
                run("act", e)

            @block.vector
            def _(e):
                run("dve", e)

            @block.gpsimd
            def _(e):
                run("pool", e)

            @block.sync
            def _(e):
                run("sp", e)
        self.es.close()
        return {e: len(self.ops[e]) for e in ENGS}


def build_A(nlayer=4, ncols=1536):
    nc = bass.Bass("TRN2", target_bir_lowering=False)
    cT_h = nc.dram_tensor("cT", [128, 48], F32, kind="ExternalInput").ap()
    w_h = nc.dram_tensor("w", [nlayer, 2048, ncols], F32, kind="ExternalInput").ap()
    b_h = nc.dram_tensor("b", [1, nlayer * ncols], F32, kind="ExternalInput").ap()
    out_h = nc.dram_tensor("mod", [3, nlayer * ncols], F32, kind="ExternalOutput").ap()
    k = KB(nc)
    R = k.res
    CT = k.sb("CT", [128, 48]); rCT = R("CT")
    SCT = k.sb("SCT", [128, 48]); rSCT = R("SCT")
    BI = k.sb("BI", [1, nlayer * ncols]); rBI = R("BI")
    ON = k.sb("ON", [1, 4]); rON = R("ON")
    OUT = k.sb("OUT", [3, nlayer * ncols]); rO = R("O")
    WB = [k.sb(f"WB{i}", [128, 16, 512]) for i in range(2)]; rWB = [R(f"WB{i}") for i in range(2)]
    PS = [k.ps(f"PS{i}") for i in range(2)]; rPS = [R(f"PS{i}") for i in range(2)]
    rOUT = R("OUT")
    k.dma("sp", CT[:], cT_h, writes=[rCT])
    k.dma("sp", BI[:], b_h, writes=[rBI])
    k.op("dve", lambda e: e.memset(ON[:], 1.0), [], [rON])
    k.op("act", lambda e: e.activation(SCT[:], CT[:], AF.Silu), [rCT], [rSCT])
    i = 0
    for l in range(nlayer):
        for cc in range(ncols // 512):
            b = i % 2; i += 1
            k.dma("sp", WB[b][:], w_h[l, :, cc * 512:(cc + 1) * 512].rearrange("(kc p) j -> p kc j", p=128), writes=[rWB[b]])
            o0 = l * ncols + cc * 512
            for kc in range(16):
                k.op("pe", lambda e, b=b, kc=kc: e.matmul(PS[b][0:3, :], lhsT=SCT[:, kc * 3:kc * 3 + 3], rhs=WB[b][:, kc, :], start=(kc == 0), stop=False),
                     [rSCT, rWB[b]], [rPS[b]])
            k.op("pe", lambda e, b=b, o0=o0: e.matmul(PS[b][0:3, :], lhsT=ON[0:1, 0:3], rhs=BI[0:1, o0:o0 + 512], start=False, stop=True), [rON, rBI], [rPS[b]])
            k.op("act", lambda e, b=b, o0=o0: e.copy(OUT[0:3, o0:o0 + 512], PS[b][0:3, :]), [rPS[b]], [rO])
    k.dma("sp", out_h, OUT[:], reads=[rO], writes=[rOUT])
    counts = k.emit()
    return nc, counts


EPS = 1e-6
NE = 16384
BLK = 1024
CPB = BLK // 512
I1B = BLK // 128


def build_D(KOc, tok_groups, final, n_chunks=32):
    nc = bass.Bass("TRN2", target_bir_lowering=False)
    T = max(t0 + gn for t0, gn, _ in tok_groups)
    KO = KOc * 128
    xT_h = nc.dram_tensor("xT", [2048, T], F32, kind="ExternalInput").ap()
    oT_h = nc.dram_tensor("oT", [KO, T], F32, kind="ExternalInput").ap()
    wo_h = nc.dram_tensor("w_out", [KO, 2048], F32, kind="ExternalInput").ap()
    mods_h = nc.dram_tensor("mods", [128, 2 * 6 * 16], F32, kind="ExternalInput").ap()
    nw_h = nc.dram_tensor("nw", [128, 32], F32, kind="ExternalInput").ap()
    wq_h = nc.dram_tensor("wq", [2048, 2048], F32, kind="ExternalInput").ap()
    kT_h = nc.dram_tensor("keysT", [128, 2048], F32, kind="ExternalInput").ap()
    uT_h = nc.dram_tensor("uT", [2048, NE], F32, kind="ExternalInput").ap()
    v_h = nc.dram_tensor("v", [NE, 2048], F32, kind="ExternalInput").ap()
    id_h = nc.dram_tensor("ident", [128, 128], F32, kind="ExternalInput").ap()
    out_h = nc.dram_tensor("outT", [2048, T], F32, kind="ExternalOutput").ap()

    k = KB(nc)
    R = k.res
    MODS = k.sb("MODS", [128, 2, 6, 16]); rMODS = R("MODS")
    NW = k.sb("NW", [128, 32]); rNW = R("NW")
    G2 = k.sb("G2", [128, 2, 16]); rG2 = R("G2")
    ID = k.sb("ID", [128, 128]); rID = R("ID")
    IDb = k.sb("IDb", [128, 128], BF16); rIDb = R("IDb")
    ONES = k.sb("ONES", [128, 128]); rONES = R("ONES")
    KT = k.sb("KT", [128, 2048], BF16); rKT = R("KT")
    k.dma("sp", MODS[:].rearrange("p a b c -> p (a b c)"), mods_h, writes=[rMODS])
    k.dma("sp", NW[:], nw_h, writes=[rNW])
    k.dma("sp", ID[:], id_h, writes=[rID])
    k.dma("pool", IDb[:], id_h, writes=[rIDb])
    k.dma("pool", KT[:], kT_h, writes=[rKT])
    k.op("dve", lambda e: e.memset(ONES[:], 1.0), [], [rONES])
    for ms in range(2):
        k.op("dve", lambda e, ms=ms: e.scalar_tensor_tensor(G2[:, ms, :], MODS[:, ms, 4, :], 1.0, NW[:, 0:16],
                                                          op0=ALU.add, op1=ALU.mult), [rMODS, rNW], [rG2])
    M1 = k.sb("M1", [128, 8192])
    MA = k.sb("MA", [128, 8192])
    MB = k.sb("MB", [128, 8192])
    MAb = MA.bitcast(BF16)
    MBb = MB.bitcast(BF16)
    X = M1[:].rearrange("p (kc t) -> p kc t", kc=16); rX = R("X")
    PFv = M1[:].rearrange("p (t d) -> p t d", t=4); rPF = [[R(f"PF{t}_{q}") for q in range(4)] for t in range(4)]
    rPFall = [r for rr in rPF for r in rr]
    OBv = MAb[:].rearrange("p (kc t) -> p kc t", t=512); rOB = R("OB")
    QT = MAb[:, 0:8192].rearrange("p (hs t) -> p hs t", hs=16); rQT = R("QT")
    UT = [MAb[:, i * 8192:(i + 1) * 8192].rearrange("p (kc e) -> p kc e", kc=16) for i in range(2)]; rUT = [R(f"UT{i}") for i in range(2)]
    X1R = MA[:].rearrange("p (kc t) -> p kc t", kc=16); rX1R = R("X1R")
    WO = [MBb[:, i * 4096:i * 4096 + KOc * 128].rearrange("p (kc j) -> p kc j", kc=KOc) for i in range(2)]; rWO = [R(f"WO{i}") for i in range(2)]
    WQ = [MBb[:, 8192 + i * 2048:8192 + (i + 1) * 2048].rearrange("p (kc j) -> p kc j", kc=16) for i in range(2)]; rWQ = [R(f"WQ{i}") for i in range(2)]
    VV = [MBb[:, i * 8192:(i + 1) * 8192].rearrange("p (j d) -> p j d", j=4) for i in range(2)]; rVV = [R(f"VV{i}") for i in range(2)]
    CAND = [k.sb(f"CAND{i}", [128, BLK]) for i in range(2)]; rCAND = [R(f"CAND{i}") for i in range(2)]
    EE = [k.sb(f"EE{i}", [128, BLK], BF16) for i in range(2)]; rEE = [R(f"EE{i}") for i in range(2)]
    SQ = [CAND[i] for i in range(2)]; rSQ = rCAND
    TMP = [EE[i].bitcast(F32) for i in range(2)]; rTMP = rEE
    RS = k.sb("RS", [128, 512]); rRS = R("RS")
    H2 = k.sb("H2", [128, 16, 512], BF16); rH2 = R("H2")
    S = [k.sb(f"S{i}", [128, 2048]) for i in range(4)]; rS = [R(f"S{i}") for i in range(4)]
    MR = k.sb("MR", [128, 256]); rMR = R("MR")
    MR2 = k.sb("MR2", [128, 256]); rMR2 = R("MR2")
    TS = k.sb("TS", [128, 2, 16]); rTS = R("TS")
    C256 = k.sb("C256", [128, 256]); rC256 = R("C256")
    VAL = [k.sb(f"VAL{i}", [128, 8, 24]) for i in range(4)]; rVAL = [R(f"VAL{i}") for i in range(4)]
    D16 = k.sb("D16", [128, 8, 16]); rD16 = R("D16")
    Z = k.sb("Z", [128, 8]); rZ = R("Z")
    BIAS = [k.sb(f"BIAS{i}", [128, 8]) for i in range(4)]; rBIAS = [R(f"BIAS{i}") for i in range(4)]
    THR = [k.sb(f"THR{i}", [128, 8]) for i in range(4)]; rTHR = [R(f"THR{i}") for i in range(4)]
    MH = [k.sb(f"MH{i}", [128, BLK], BF16) for i in range(2)]; rMH = [R(f"MH{i}") for i in range(2)]
    GG = [[k.sb(f"GG{i}_{j}", [128, BLK], BF16) for j in range(2)] for i in range(4)]; rGG = [[R(f"GG{i}_{j}") for j in range(2)] for i in range(4)]
    GA = [k.sb(f"GA{i}", [128, 512], BF16) for i in range(2)]; rGA = [R(f"GA{i}") for i in range(2)]
    WT_ = [k.sb(f"Wt{i}", [128, 512], BF16) for i in range(2)]; rWt = [R(f"Wt{i}") for i in range(2)]
    WTT = [k.sb(f"WTT{i}", [128, 4, 128], BF16) for i in range(2)]; rWTT = [R(f"WTT{i}") for i in range(2)]
    PO = [k.ps(f"PO{i}") for i in range(4)]; rPO = [R(f"PO{i}") for i in range(4)]
    PA = [k.ps(f"PA{i}") for i in range(2)]; rPA = [R(f"PA{i}") for i in range(2)]
    PT = k.ps("PT", (128, 512), BF16); rPT = R("PT")
    PM = k.ps("PM"); rPM = R("PM")
    rOUT = R("OUT")
    cnt = dict(wo=0, sq=0, tmp=0, wq=0, cand=0, uv=0, ga=0, po=0)
    NB = NE // BLK

    def do_group(t0, gn, ms):
        nt = gn // 128
        out_g = out_h[:, t0:t0 + gn].rearrange("(kc p) t -> p kc t", p=128)
        k.dma("sp", X[:, :, 0:gn], xT_h[:, t0:t0 + gn].rearrange("(kc p) t -> p kc t", p=128), writes=[rX])
        k.dma("pool", OBv[:, 0:KOc, 0:gn], oT_h[:, t0:t0 + gn].rearrange("(kc p) t -> p kc t", p=128), writes=[rOB])
        for dc in range(16):
            b = cnt["wo"] % 2; cnt["wo"] += 1
            k.dma("pool", WO[b], wo_h[:, dc * 128:(dc + 1) * 128].rearrange("(kc p) j -> p kc j", p=128), writes=[rWO[b]])
            for kc in range(KOc):
                k.op("pe", lambda e, b=b, kc=kc: e.matmul(PM[:, 0:gn], lhsT=WO[b][:, kc, :], rhs=OBv[:, kc, 0:gn],
                                                          start=(kc == 0), stop=(kc == KOc - 1)), [rWO[b], rOB], [rPM])
            k.op("dve", lambda e, dc=dc: e.scalar_tensor_tensor(X[:, dc, 0:gn], PM[:, 0:gn], MODS[:, ms, 2, dc:dc + 1], X[:, dc, 0:gn],
                                                               op0=ALU.mult, op1=ALU.add), [rPM, rMODS, rX], [rX])
            sb_ = cnt["sq"] % 2; cnt["sq"] += 1
            k.op("act", lambda e, dc=dc, sb_=sb_: e.activation(SQ[sb_][:, 0:gn], X[:, dc, 0:gn], AF.Square), [rX], [rSQ[sb_]])
            k.op("pe", lambda e, dc=dc, sb_=sb_: e.matmul(PA[0][:, 0:gn], lhsT=ONES[:], rhs=SQ[sb_][:, 0:gn],
                                                          start=(dc == 0), stop=(dc == 15)), [rONES, rSQ[sb_]], [rPA[0]])
        k.dma("sp", out_g, X[:, :, 0:gn], reads=[rX], writes=[rOUT])
        k.op("dve", lambda e: e.tensor_scalar(RS[:, 0:gn], PA[0][:, 0:gn], 1.0 / 2048, EPS, op0=ALU.mult, op1=ALU.add), [rPA[0]], [rRS])
        k.op("act", lambda e: e.activation(RS[:, 0:gn], RS[:, 0:gn], AF.Ln), [rRS], [rRS])
        k.op("act", lambda e: e.activation(RS[:, 0:gn], RS[:, 0:gn], AF.Exp, scale=-0.5), [rRS], [rRS])
        for kc in range(16):
            tb = cnt["tmp"] % 2; cnt["tmp"] += 1
            k.op("dve", lambda e, kc=kc, tb=tb: e.scalar_tensor_tensor(TMP[tb][:, 0:gn], X[:, kc, 0:gn], G2[:, ms, kc:kc + 1], RS[:, 0:gn],
                                                                      op0=ALU.mult, op1=ALU.mult), [rX, rG2, rRS], [rTMP[tb]])
            k.op("act", lambda e, kc=kc, tb=tb: e.activation(H2[:, kc, 0:gn], TMP[tb][:, 0:gn], AF.Identity,
                                                             bias=MODS[:, ms, 3, kc:kc + 1], scale=1.0), [rTMP[tb], rMODS], [rH2])
        k.handoff([rOB], [rQT])
        k.handoff(rWO, rWQ)
        for hs in range(16):
            b = cnt["wq"] % 2; cnt["wq"] += 1
            k.dma("pool", WQ[b], wq_h[:, hs * 128:(hs + 1) * 128].rearrange("(kc p) j -> p kc j", p=128), writes=[rWQ[b]])
            for kc in range(16):
                k.op("pe", lambda e, b=b, kc=kc: e.matmul(PM[:, 0:gn], lhsT=WQ[b][:, kc, :], rhs=H2[:, kc, 0:gn],
                                                          start=(kc == 0), stop=(kc == 15)), [rWQ[b], rH2], [rPM])
            k.op("act", lambda e, hs=hs: e.copy(QT[:, hs, 0:gn], PM[:, 0:gn]), [rPM], [rQT])
        for t in range(nt):
            for q in range(4):
                for j in range(4):
                    hs = q * 4 + j
                    k.op("pe", lambda e, t=t, q=q, j=j, hs=hs: e.matmul(PO[q][:, j * 128:(j + 1) * 128], lhsT=QT[:, hs, t * 128:(t + 1) * 128],
                                                                        rhs=KT[:, hs * 128:(hs + 1) * 128], start=True, stop=True),
                         [rQT, rKT], [rPO[q]])
                k.op("act", lambda e, t=t, q=q: e.copy(S[t][:, q * 512:(q + 1) * 512], PO[q][:]), [rPO[q]], [rS[t]])
            for h in range(8):
                for sd in range(2):
                    o0 = (h * 2 + sd) * 128
                    k.op("dve", lambda e, t=t, o0=o0, sd=sd: e.max(TS[:, sd, 0:8], S[t][:, o0:o0 + 128]), [rS[t]], [rTS])
                    k.op("dve", lambda e, t=t, o0=o0, sd=sd: e.match_replace(MR[:, 0:128], TS[:, sd, 0:8], S[t][:, o0:o0 + 128], -1e30), [rS[t], rTS], [rMR])
                    k.op("dve", lambda e, sd=sd: e.max(TS[:, sd, 8:16], MR[:, 0:128]), [rMR], [rTS])
                k.op("dve", lambda e: e.tensor_tensor(C256[:].rearrange("p (a b) -> p a b", a=16),
                                                      TS[:, 0, :].unsqueeze(2).to_broadcast([128, 16, 16]),
                                                      TS[:, 1, :].unsqueeze(1).to_broadcast([128, 16, 16]), op=ALU.add), [rTS], [rC256])
                k.op("dve", lambda e, t=t, h=h: e.max(VAL[t][:, h, 0:8], C256[:]), [rC256], [rVAL[t]])
                k.op("dve", lambda e, t=t, h=h: e.match_replace(MR[:], VAL[t][:, h, 0:8], C256[:], -1e30), [rC256, rVAL[t]], [rMR])
                k.op("dve", lambda e, t=t, h=h: e.max(VAL[t][:, h, 8:16], MR[:]), [rMR], [rVAL[t]])
                k.op("dve", lambda e, t=t, h=h: e.match_replace(MR2[:], VAL[t][:, h, 8:16], MR[:], -1e30), [rMR, rVAL[t]], [rMR2])
                k.op("dve", lambda e, t=t, h=h: e.max(VAL[t][:, h, 16:24], MR2[:]), [rMR2], [rVAL[t]])
            k.op("dve", lambda e, t=t: e.tensor_tensor(D16[:], VAL[t][:, :, 0:16], VAL[t][:, :, 0:1].to_broadcast([128, 8, 16]), op=ALU.subtract), [rVAL[t]], [rD16])
            k.op("act", lambda e: e.activation(D16[:], D16[:], AF.Exp), [rD16], [rD16])
            k.op("dve", lambda e: e.tensor_reduce(Z[:], D16[:], axis=AX.X, op=ALU.add), [rD16], [rZ])
            k.op("act", lambda e: e.activation(Z[:], Z[:], AF.Ln), [rZ], [rZ])
            k.op("dve", lambda e, t=t: e.scalar_tensor_tensor(BIAS[t][:], VAL[t][:, :, 0], -1.0, Z[:], op0=ALU.mult, op1=ALU.subtract), [rVAL[t], rZ], [rBIAS[t]])
            k.op("dve", lambda e, t=t: e.tensor_tensor(THR[t][:], VAL[t][:, :, 15], VAL[t][:, :, 16], op=ALU.add), [rVAL[t]], [rTHR[t]])
            k.op("dve", lambda e, t=t: e.tensor_scalar(THR[t][:], THR[t][:], 0.5, None, op0=ALU.mult), [rTHR[t]], [rTHR[t]])
        k.handoff([rQT], rUT)
        k.handoff(rWQ + rWO, rVV)
        k.handoff([rX], rPFall)
        blocks = [bb for bb in range(NB) if bb * CPB < n_chunks]

        def g_cand(blk, t, h):
            cb = cnt["cand"] % 2; cnt["cand"] += 1
            s1 = S[t][:, (h * 2) * 128 + blk * I1B:(h * 2) * 128 + (blk + 1) * I1B]
            s2 = S[t][:, (h * 2 + 1) * 128:(h * 2 + 2) * 128]
            k.op("dve", lambda e: e.tensor_tensor(CAND[cb][:].rearrange("p (a b) -> p a b", a=I1B),
                                                  s1.unsqueeze(2).to_broadcast([128, I1B, 128]),
                                                  s2.unsqueeze(1).to_broadcast([128, I1B, 128]), op=ALU.add), [rS[t]], [rCAND[cb]])
            k.op("act", lambda e: e.activation(EE[cb][:], CAND[cb][:], AF.Exp, bias=BIAS[t][:, h:h + 1], scale=1.0),
                 [rCAND[cb], rBIAS[t]], [rEE[cb]])
            return cb

        def g_mask(blk, t, h, cb):
            gj = blk % 2
            if h == 0:
                k.op("dve", lambda e: e.scalar_tensor_tensor(GG[t][gj][:], CAND[cb][:], THR[t][:, h:h + 1], EE[cb][:], op0=ALU.is_ge, op1=ALU.mult),
                     [rCAND[cb], rTHR[t], rEE[cb]], [rGG[t][gj]])
            else:
                k.op("dve", lambda e: e.scalar_tensor_tensor(MH[cb][:], CAND[cb][:], THR[t][:, h:h + 1], EE[cb][:], op0=ALU.is_ge, op1=ALU.mult),
                     [rCAND[cb], rTHR[t], rEE[cb]], [rMH[cb]])
                k.op("pool", lambda e: e.tensor_tensor(GG[t][gj][:], GG[t][gj][:], MH[cb][:], op=ALU.add), [rGG[t][gj], rMH[cb]], [rGG[t][gj]])

        def g_items(blk, items):
            if not items:
                return
            cbs = [None] * len(items)
            cbs[0] = g_cand(blk, *items[0])
            for i_, (t, h) in enumerate(items):
                if i_ + 1 < len(items):
                    cbs[i_ + 1] = g_cand(blk, *items[i_ + 1])
                g_mask(blk, t, h, cbs[i_])

        def chunk_load(c):
            ub = cnt["uv"] % 2; cnt["uv"] += 1
            k.dma("pool", UT[ub], uT_h[:, c * 512:(c + 1) * 512].rearrange("(kc p) e -> p kc e", p=128), writes=[rUT[ub]])
            k.dma("pool", VV[ub], v_h[c * 512:(c + 1) * 512, :].rearrange("(j p) d -> p j d", p=128), writes=[rVV[ub]])
            return ub

        def chunk_act(ub, t):
            gb = cnt["ga"] % 2; cnt["ga"] += 1
            for kc in range(16):
                k.op("pe", lambda e, kc=kc: e.matmul(PA[gb][:], lhsT=H2[:, kc, t * 128:(t + 1) * 128], rhs=UT[ub][:, kc, :],
                                                     start=(kc == 0), stop=(kc == 15)), [rH2, rUT[ub]], [rPA[gb]])
            k.op("act", lambda e: e.activation(GA[gb][:], PA[gb][:], AF.Gelu), [rPA[gb]], [rGA[gb]])
            return gb

        def chunk_prod(c, ec, blk, ub, t, gb):
            gj = blk % 2
            k.op("dve", lambda e: e.tensor_tensor(WT_[gb][:], GA[gb][:], GG[t][gj][:, ec * 512:(ec + 1) * 512], op=ALU.mult),
                 [rGA[gb], rGG[t][gj]], [rWt[gb]])
            for j in range(4):
                k.op("pe", lambda e, j=j: e.transpose(PT[:, j * 128:(j + 1) * 128], WT_[gb][:, j * 128:(j + 1) * 128], IDb[:]), [rWt[gb], rIDb], [rPT])
            k.op("act", lambda e: e.copy(WTT[gb][:].rearrange("p a b -> p (a b)"), PT[:]), [rPT], [rWTT[gb]])
            pbs = []
            for dq in range(4):
                pb = cnt["po"] % 4; cnt["po"] += 1
                pbs.append(pb)
                for j in range(4):
                    k.op("pe", lambda e, j=j, dq=dq, pb=pb: e.matmul(PO[pb][:], lhsT=WTT[gb][:, j, :], rhs=VV[ub][:, j, dq * 512:(dq + 1) * 512],
                                                                     start=(j == 0), stop=(j == 3)), [rWTT[gb], rVV[ub]], [rPO[pb]])
            return pbs

        def chunk_pf(c, t, pbs):
            for dq in range(4):
                pb = pbs[dq]
                if c == 0:
                    k.op("dve", lambda e, dq=dq, pb=pb: e.tensor_copy(PFv[:, t, dq * 512:(dq + 1) * 512], PO[pb][:]), [rPO[pb]], [rPF[t][dq]])
                else:
                    k.op("dve", lambda e, dq=dq, pb=pb: e.tensor_tensor(PFv[:, t, dq * 512:(dq + 1) * 512], PO[pb][:], PFv[:, t, dq * 512:(dq + 1) * 512], op=ALU.add),
                         [rPO[pb], rPF[t][dq]], [rPF[t][dq]])

        all_g = [(t, h) for t in range(nt) for h in range(8)]
        g_items(blocks[0], all_g)
        for bi, blk in enumerate(blocks):
            nxt = blocks[bi + 1] if bi + 1 < len(blocks) else None
            chs = [(blk * CPB + ec, ec) for ec in range(CPB) if blk * CPB + ec < n_chunks]
            ubs = {c: chunk_load(c) for (c, ec) in chs}
            items = [(c, ec, t) for (c, ec) in chs for t in range(nt)]
            per = -(-len(all_g) // len(items))
            gbs = {}
            gbs[0] = chunk_act(ubs[items[0][0]], items[0][2])
            for i_, (c, ec, t) in enumerate(items):
                if i_ + 1 < len(items):
                    gbs[i_ + 1] = chunk_act(ubs[items[i_ + 1][0]], items[i_ + 1][2])
                pbs = chunk_prod(c, ec, blk, ubs[c], t, gbs[i_])
                if nxt is not None:
                    g_items(nxt, all_g[i_ * per:(i_ + 1) * per])
                chunk_pf(c, t, pbs)
        k.handoff(rUT, [rX1R])
        k.dma("sp", X1R[:, :, 0:gn], out_g, reads=[rOUT], writes=[rX1R])
        for t in range(nt):
            for dq in range(4):
                for j in range(4):
                    k.op("pe", lambda e, t=t, dq=dq, j=j: e.transpose(PM[:, j * 128:(j + 1) * 128], PFv[:, t, dq * 512 + j * 128:dq * 512 + (j + 1) * 128], ID[:]),
                         [rPF[t][dq], rID], [rPM])
                for j in range(4):
                    dc = dq * 4 + j
                    k.op("dve", lambda e, t=t, dc=dc, j=j: e.scalar_tensor_tensor(X1R[:, dc, t * 128:(t + 1) * 128], PM[:, j * 128:(j + 1) * 128], MODS[:, ms, 5, dc:dc + 1],
                                                                                 X1R[:, dc, t * 128:(t + 1) * 128], op0=ALU.mult, op1=ALU.add), [rPM, rMODS, rX1R], [rX1R])
        if final:
            for dc in range(16):
                sb_ = cnt["sq"] % 2; cnt["sq"] += 1
                k.op("act", lambda e, dc=dc, sb_=sb_: e.activation(SQ[sb_][:, 0:gn], X1R[:, dc, 0:gn], AF.Square), [rX1R], [rSQ[sb_]])
                k.op("pe", lambda e, dc=dc, sb_=sb_: e.matmul(PA[0][:, 0:gn], lhsT=ONES[:], rhs=SQ[sb_][:, 0:gn],
                                                              start=(dc == 0), stop=(dc == 15)), [rONES, rSQ[sb_]], [rPA[0]])
            k.op("dve", lambda e: e.tensor_scalar(RS[:, 0:gn], PA[0][:, 0:gn], 1.0 / 2048, EPS, op0=ALU.mult, op1=ALU.add), [rPA[0]], [rRS])
            k.op("act", lambda e: e.activation(RS[:, 0:gn], RS[:, 0:gn], AF.Ln), [rRS], [rRS])
            k.op("act", lambda e: e.activation(RS[:, 0:gn], RS[:, 0:gn], AF.Exp, scale=-0.5), [rRS], [rRS])
            for kc in range(16):
                k.op("dve", lambda e, kc=kc: e.scalar_tensor_tensor(X1R[:, kc, 0:gn], X1R[:, kc, 0:gn], NW[:, 16 + kc:17 + kc], RS[:, 0:gn],
                                                                   op0=ALU.mult, op1=ALU.mult), [rX1R, rNW, rRS], [rX1R])
        k.dma("sp", out_g, X1R[:, :, 0:gn], reads=[rX1R], writes=[rOUT])
        k.handoff([rX1R], [rOB])
        k.handoff(rVV, rWO)
        k.handoff(rPFall, [rX])

    for (t0_, gn_, ms_) in tok_groups:
        do_group(t0_, gn_, ms_)
    counts = k.emit()
    return nc, counts


EPS = 1e-6
NCTX = 256
NLOC = 8192
TB = NCTX + NLOC
NKT = TB // 128
GA_ = 256
NBLK = 11
C_RQ, C_RQP, C_RK, C_RKP, C_RG, C_AQ, C_AQP, C_AK, C_AKP, C_RV, C_AV = range(11)


def build_E(nbatch=2, tb=TB, nctx=NCTX):
    nc = bass.Bass("TRN2", target_bir_lowering=False)
    nkt = tb // 128
    nloc = tb - nctx
    xT_h = nc.dram_tensor("xT", [2048, nbatch * tb], F32, kind="ExternalInput").ap()
    w_h = nc.dram_tensor("w", [2048, NBLK * 128], F32, kind="ExternalInput").ap()
    mods_h = nc.dram_tensor("mods", [128, 3 * 2 * 16], F32, kind="ExternalInput").ap()
    nw_h = nc.dram_tensor("nw", [128, 16], F32, kind="ExternalInput").ap()
    sm_h = nc.dram_tensor("small", [128, 8], F32, kind="ExternalInput").ap()
    cos_h = nc.dram_tensor("cosT", [128, nloc], F32, kind="ExternalInput").ap()
    sin_h = nc.dram_tensor("sinT", [128, nloc], F32, kind="ExternalInput").ap()
    out_h = nc.dram_tensor("oT", [nbatch, 256, tb], F32, kind="ExternalOutput").ap()

    k = KB(nc)
    R = k.res
    MODS = k.sb("MODS", [128, 3, 2, 16]); rMODS = R("MODS")
    NW = k.sb("NW", [128, 16]); rNW = R("NW")
    SM = k.sb("SM", [128, 8]); rSM = R("SM")
    G1 = k.sb("G1", [128, 3, 16]); rG1 = R("G1")
    ONES = k.sb("ONES", [128, 128]); rONES = R("ONES")
    ONESb = k.sb("ONESb", [128, 128], BF16); rONESb = R("ONESb")
    LG = k.sb("LG", [128, 4]); rLG = R("LG")
    XR = k.sb("XR", [128, 4096])
    XRb = XR.bitcast(BF16)
    R2 = k.sb("R2", [128, 1536])
    R2b = R2.bitcast(BF16)
    IO1 = XR[:, 0:512]; rIO1 = R("IO1")
    IOM = k.sb("IOM", [128, 80]); rIOM = R("IOM")
    CF = k.sb("CF", [128, 80]); rCF = R("CF")
    CB = k.sb("CB", [128, 80]); rCB = R("CB")
    BF_ = k.sb("BF", [128, 512], BF16); rBF = R("BF")
    BB_ = k.sb("BB", [128, 512], BF16); rBB = R("BB")
    DD = [k.sb(f"DD{m}", [128, 512], BF16) for m in range(4)]; rDD = [R(f"DD{m}") for m in range(4)]
    DDf = XR[:, 1536:2048]; rDDf = R("DDf")
    T1 = XR[:, 512:1024]; rT1 = R("T1c")
    T2 = XR[:, 1024:1536]; rT2 = R("T2c")
    W = k.sb("W", [128, 16, NBLK * 128], BF16); rW = R("W")
    k.dma("sp", MODS[:].rearrange("p a b c -> p (a b c)"), mods_h, writes=[rMODS])
    k.dma("sp", NW[:], nw_h, writes=[rNW])
    k.dma("sp", SM[:], sm_h, writes=[rSM])
    for kc in range(16):
        k.dma("pool", W[:, kc, :], w_h[kc * 128:(kc + 1) * 128, :], writes=[rW])
    k.op("dve", lambda e: e.memset(ONES[:], 1.0), [], [rONES])
    k.op("dve", lambda e: e.memset(ONESb[:], 1.0), [], [rONESb])
    for s_ in range(3):
        k.op("dve", lambda e, s_=s_: e.scalar_tensor_tensor(G1[:, s_, :], MODS[:, s_, 1, :], 1.0, NW[:], op0=ALU.add, op1=ALU.mult), [rMODS, rNW], [rG1])
    k.op("act", lambda e: e.activation(LG[:, 2:4], SM[:, 0:2], AF.Exp), [rSM], [rLG])
    k.op("dve", lambda e: e.tensor_scalar(LG[:, 0:2], LG[:, 2:4], -1.0, None, op0=ALU.mult), [rLG], [rLG])
    k.op("pool", lambda e: e.iota(IO1, pattern=[[1, 512]], base=0, channel_multiplier=-1, allow_small_or_imprecise_dtypes=True), [], [rIO1])
    k.op("pool", lambda e: e.iota(IOM[:], pattern=[[128, 80]], base=0, channel_multiplier=0, allow_small_or_imprecise_dtypes=True), [], [rIOM])
    SC = 128.0 ** -0.5
    k.op("act", lambda e: e.activation(CF[:], IOM[:], AF.Exp, scale=LG[:, 0:1]), [rIOM, rLG], [rCF])
    k.op("act", lambda e: e.activation(CB[:], IOM[:], AF.Exp, scale=LG[:, 1:2]), [rIOM, rLG], [rCB])
    k.op("dve", lambda e: e.tensor_scalar(CF[:], CF[:], SC, None, op0=ALU.mult), [rCF], [rCF])
    k.op("dve", lambda e: e.tensor_scalar(CB[:], CB[:], SC, None, op0=ALU.mult), [rCB], [rCB])
    k.op("act", lambda e: e.activation(BF_[:], IO1, AF.Exp, scale=LG[:, 0:1]), [rIO1, rLG], [rBF])
    k.op("act", lambda e: e.activation(BB_[:], IO1, AF.Exp, scale=LG[:, 3:4]), [rIO1, rLG], [rBB])
    for m in range(4):
        k.op("dve", lambda e, m=m: e.tensor_scalar(T1, IO1, -128.0 * m, 0.0, op0=ALU.add, op1=ALU.max), [rIO1], [rT1])
        k.op("dve", lambda e, m=m: e.tensor_scalar(T2, IO1, -1.0, 128.0 * m, op0=ALU.mult, op1=ALU.add), [rIO1], [rT2])
        k.op("dve", lambda e: e.tensor_scalar(T2, T2, 0.0, LG[:, 1:2], op0=ALU.max, op1=ALU.mult), [rT2, rLG], [rT2])
        k.op("dve", lambda e: e.scalar_tensor_tensor(T1, T1, LG[:, 0:1], T2, op0=ALU.mult, op1=ALU.add), [rT1, rLG, rT2], [rT1])
        k.op("act", lambda e, m=m: e.activation(DDf, T1, AF.Exp), [rT1], [rDDf])
        k.op("dve", lambda e, m=m: e.tensor_scalar(T2, IO1, 128.0 * m, None, op0=ALU.is_equal), [rIO1], [rT2])
        k.op("dve", lambda e, m=m: e.tensor_tensor(DDf, DDf, T2, op=ALU.add), [rDDf, rT2], [rDDf])
        k.op("dve", lambda e, m=m: e.tensor_scalar(DD[m][:], DDf, SC, None, op0=ALU.mult), [rDDf], [rDD[m]])
    RQ = k.sb("RQ", [128, tb], BF16); rRQ = R("RQ")
    RK = k.sb("RK", [128, tb], BF16); rRK = R("RK")
    RG = k.sb("RG", [128, tb], BF16); rRG = R("RG")
    AQ = k.sb("AQ", [128, tb], BF16); rAQ = R("AQ")
    AK = k.sb("AK", [128, tb], BF16); rAK = R("AK")
    RV = k.sb("RV", [128, nkt, 128], BF16); rRV = R("RV")
    AV = k.sb("AV", [128, nkt, 128], BF16); rAV = R("AV")
    X = XR[:].rearrange("p (kc t) -> p kc t", kc=16)
    H = XRb[:].rearrange("p (kc t) -> p kc t", kc=16)[:, :, 0:GA_]
    rXk = [R(f"X{i}") for i in range(16)]
    SQ = [k.sb(f"SQ{i}", [128, GA_]) for i in range(2)]; rSQ = [R(f"SQ{i}") for i in range(2)]
    RS = k.sb("RS", [128, GA_]); rRS = R("RS")
    TMP = [k.sb(f"TMP{i}", [128, GA_]) for i in range(2)]; rTMP = [R(f"TMP{i}") for i in range(2)]
    COS = [k.sb(f"COS{i}", [128, GA_]) for i in range(1)]; rCOS = [R(f"COS{i}") for i in range(1)]
    SIN = [k.sb(f"SIN{i}", [128, GA_]) for i in range(1)]; rSIN = [R(f"SIN{i}") for i in range(1)]
    XN = R2[:, 0:512].rearrange("p (a t) -> p a t", a=2); rXN = R("XN")
    RA = R2[:, 512:768]; rRA = R("RA")
    RB = R2[:, 768:1024]; rRB = R("RB")
    RSQ = R2[:, 1024:1280]; rRSQ = R("RSQ")
    resA = rXk + [rXN, rRA, rRB, rRSQ]
    OS = [XR[:, 0:512], XR[:, 512:1024]]; rOS = [R(f"OS{i}") for i in range(2)]
    ORS = XR[:, 1024:1536]; rORS = R("ORS")
    CEN = XR[:, 1536:2048]; rCEN = R("CEN")
    SQB = XR[:, 2048:2560]; rSQB = R("SQB")
    RSB = XR[:, 2560:3072]; rRSB = R("RSB")
    RI = XR[:, 3072:3584]; rRI = R("RI")
    PTa = [XRb[:, 7168:7680], XRb[:, 7680:8192]]; rPTa = [R(f"PTa{i}") for i in range(2)]
    DC = [R2[:, 0:512], R2[:, 512:1024]]; rDC = [R(f"DC{i}") for i in range(2)]
    PTr = [R2b[:, 2048:2560], R2b[:, 2560:3072]]; rPTr = [R(f"PTr{i}") for i in range(2)]
    resB = rOS + [rORS, rCEN, rSQB, rRSB, rRI] + rPTa + rDC + rPTr
    k.handoff([rIO1, rT1, rT2, rDDf], resA)
    PP = [k.ps(f"PP{i}") for i in range(4)]; rPP = [R(f"PP{i}") for i in range(4)]
    PO = k.ps("PO"); rPO = R("PO")
    PR = k.ps("PR"); rPR = R("PR")
    PQ = k.ps("PQ"); rPQ = R("PQ")
    PM = k.ps("PM"); rPM = R("PM")
    rOUT = R("OUT")
    cnt = dict(sq=0, tmp=0, cs=0, pp=0, pta=0, ptr=0, dc=0, os=0)

    def stats_rstd(src_of_kc, nk, N, div, out_rs):
        for kc in range(nk):
            sb_ = cnt["sq"] % 2; cnt["sq"] += 1
            ap, rr = src_of_kc(kc)
            k.op("act", lambda e, ap=ap, sb_=sb_: e.activation(SQ[sb_][:, 0:N], ap, AF.Square), [rr], [rSQ[sb_]])
            k.op("pe", lambda e, kc=kc, sb_=sb_: e.matmul(PM[:, 0:N], lhsT=ONES[:], rhs=SQ[sb_][:, 0:N], start=(kc == 0), stop=(kc == nk - 1)),
                 [rONES, rSQ[sb_]], [rPM])
        ors, rr = out_rs
        k.op("dve", lambda e: e.tensor_scalar(ors, PM[:, 0:N], 1.0 / div, EPS, op0=ALU.mult, op1=ALU.add), [rPM], [rr])
        k.op("act", lambda e: e.activation(ors, ors, AF.Ln), [rr], [rr])
        k.op("act", lambda e: e.activation(ors, ors, AF.Exp, scale=-0.5), [rr], [rr])

    def phase_a_group(b, g0, N, ms, is_ctx, loc0):
        c0 = b * tb + g0
        k.dma("sp", X[:, :, 0:N], xT_h[:, c0:c0 + N].rearrange("(kc p) t -> p kc t", p=128), writes=rXk, dres=rXk[0])
        if not is_ctx:
            cb = 0
            k.dma("sp", COS[cb][:, 0:N], cos_h[:, loc0:loc0 + N], writes=[rCOS[cb]])
            k.dma("sp", SIN[cb][:, 0:N], sin_h[:, loc0:loc0 + N], writes=[rSIN[cb]])
        stats_rstd(lambda kc: (X[:, kc, 0:N], rXk[kc]), 16, N, 2048.0, (RS[:, 0:N], rRS))
        for kc in range(16):
            tb_ = cnt["tmp"] % 2; cnt["tmp"] += 1
            k.op("dve", lambda e, kc=kc, tb_=tb_: e.scalar_tensor_tensor(TMP[tb_][:, 0:N], X[:, kc, 0:N], G1[:, ms, kc:kc + 1], RS[:, 0:N],
                                                                        op0=ALU.mult, op1=ALU.mult), [rXk[kc], rG1, rRS], [rTMP[tb_]])
            k.op("act", lambda e, kc=kc, tb_=tb_: e.activation(H[:, kc, 0:N], TMP[tb_][:, 0:N], AF.Identity,
                                                               bias=MODS[:, ms, 0, kc:kc + 1], scale=1.0), [rTMP[tb_], rMODS], [rXk[kc]])

        def proj(cblk, pp, off):
            for kc in range(16):
                k.op("pe", lambda e, kc=kc: e.matmul(PP[pp][:, off:off + N], lhsT=W[:, kc, cblk * 128:(cblk + 1) * 128], rhs=H[:, kc, 0:N],
                                                     start=(kc == 0), stop=(kc == 15)), [rW, rXk[kc]], [rPP[pp]])

        def rope_store(pp, dst, rdst, srcA, srcB, rsrc):
            if is_ctx:
                k.op("act", lambda e: e.copy(dst[:, g0:g0 + N], srcA), rsrc, [rdst])
            else:
                k.op("dve", lambda e: e.tensor_tensor(RA[:, 0:N], srcA, COS[cb][:, 0:N], op=ALU.mult), rsrc + [rCOS[cb]], [rRA])
                k.op("dve", lambda e: e.tensor_tensor(RB[:, 0:N], srcB, SIN[cb][:, 0:N], op=ALU.mult), rsrc + [rSIN[cb]], [rRB])
                k.op("dve", lambda e: e.tensor_tensor(dst[:, g0:g0 + N], RA[:, 0:N], RB[:, 0:N], op=ALU.add), [rRA, rRB], [rdst])

        for (ca, cbk, dst, rdst) in ((C_RQ, C_RQP, RQ, rRQ), (C_RK, C_RKP, RK, rRK)):
            pp = cnt["pp"] % 4; cnt["pp"] += 1
            proj(ca, pp, 0)
            if not is_ctx:
                proj(cbk, pp, 256)
            rope_store(pp, dst, rdst, PP[pp][:, 0:N], PP[pp][:, 256:256 + N], [rPP[pp]])
        pp = cnt["pp"] % 4; cnt["pp"] += 1
        proj(C_RG, pp, 0)
        k.op("act", lambda e, pp=pp: e.activation(RG[:, g0:g0 + N], PP[pp][:, 0:N], AF.Silu), [rPP[pp]], [rRG])
        for (ca, cbk, dst, rdst, wc) in ((C_AQ, C_AQP, AQ, rAQ, 2), (C_AK, C_AKP, AK, rAK, 4)):
            pp = cnt["pp"] % 4; cnt["pp"] += 1
            proj(ca, pp, 0)
            if not is_ctx:
                proj(cbk, pp, 256)
            stats_rstd(lambda kc, pp=pp: (PP[pp][:, 0:N], rPP[pp]), 1, N, 128.0, (RSQ[:, 0:N], rRSQ))
            k.op("dve", lambda e, pp=pp, wc=wc: e.scalar_tensor_tensor(XN[:, 0, 0:N], PP[pp][:, 0:N], SM[:, wc:wc + 1], RSQ[:, 0:N], op0=ALU.mult, op1=ALU.mult),
                 [rPP[pp], rSM, rRSQ], [rXN])
            if not is_ctx:
                k.op("dve", lambda e, pp=pp, wc=wc: e.scalar_tensor_tensor(XN[:, 1, 0:N], PP[pp][:, 256:256 + N], SM[:, wc + 1:wc + 2], RSQ[:, 0:N], op0=ALU.mult, op1=ALU.mult),
                     [rPP[pp], rSM, rRSQ], [rXN])
            rope_store(pp, dst, rdst, XN[:, 0, 0:N], XN[:, 1, 0:N], [rXN])
        for (cv, dst, rdst) in ((C_RV, RV, rRV), (C_AV, AV, rAV)):
            pp = cnt["pp"] % 4; cnt["pp"] += 1
            for tt in range(N // 128):
                for kc in range(16):
                    k.op("pe", lambda e, kc=kc, tt=tt, pp=pp, cv=cv: e.matmul(PP[pp][:, tt * 128:(tt + 1) * 128], lhsT=H[:, kc, tt * 128:(tt + 1) * 128],
                                                                              rhs=W[:, kc, cv * 128:(cv + 1) * 128], start=(kc == 0), stop=(kc == 15)), [rW, rXk[kc]], [rPP[pp]])
            kt0 = g0 // 128
            k.op("act", lambda e, pp=pp, dst=dst, kt0=kt0: e.copy(dst[:, kt0:kt0 + N // 128, :].rearrange("p a b -> p (a b)"), PP[pp][:, 0:N]), [rPP[pp]], [rdst])

    def phase_b_group(b, q0, N, is_ctx, jloc):
        kts = list(range(nctx // 128)) if is_ctx else list(range(nkt))
        nk = len(kts)
        q0t = q0 // 128

        def scores(kt):
            pa = cnt["pta"] % 2; cnt["pta"] += 1
            k.op("pe", lambda e: e.matmul(PP[pa][:, 0:N], lhsT=AK[:, kt * 128:(kt + 1) * 128], rhs=AQ[:, q0:q0 + N], start=True, stop=True),
                 [rAK, rAQ], [rPP[pa]])
            k.op("act", lambda e: e.activation(PTa[pa][:, 0:N], PP[pa][:, 0:N], AF.Exp, scale=SC), [rPP[pa]], [rPTa[pa]])
            pr = cnt["ptr"] % 2; cnt["ptr"] += 1
            k.op("pe", lambda e: e.matmul(PP[2 + pr][:, 0:N], lhsT=RK[:, kt * 128:(kt + 1) * 128], rhs=RQ[:, q0:q0 + N], start=True, stop=True),
                 [rRK, rRQ], [rPP[2 + pr]])
            if is_ctx:
                m = kt
                k.op("dve", lambda e: e.tensor_tensor(PTr[pr][:, 0:N], PP[2 + pr][:, 0:N], DD[m][:, 0:N], op=ALU.mult), [rPP[2 + pr], rDD[m]], [rPTr[pr]])
            elif kt < nctx // 128:
                dcb = cnt["dc"] % 2; cnt["dc"] += 1
                mf = q0t - kt
                mb = nkt - q0t + kt
                k.op("dve", lambda e: e.tensor_scalar(DC[dcb][:, 0:N], BF_[:, 0:N], CF[:, mf:mf + 1], None, op0=ALU.mult), [rBF, rCF], [rDC[dcb]])
                k.op("dve", lambda e: e.scalar_tensor_tensor(DC[dcb][:, 0:N], BB_[:, 0:N], CB[:, mb:mb + 1], DC[dcb][:, 0:N], op0=ALU.mult, op1=ALU.add),
                     [rBB, rCB, rDC[dcb]], [rDC[dcb]])
                k.op("dve", lambda e: e.tensor_tensor(PTr[pr][:, 0:N], PP[2 + pr][:, 0:N], DC[dcb][:, 0:N], op=ALU.mult), [rPP[2 + pr], rDC[dcb]], [rPTr[pr]])
            elif kt < q0t:
                mf = q0t - kt
                k.op("dve", lambda e: e.scalar_tensor_tensor(PTr[pr][:, 0:N], PP[2 + pr][:, 0:N], CF[:, mf:mf + 1], BF_[:, 0:N], op0=ALU.mult, op1=ALU.mult),
                     [rPP[2 + pr], rCF, rBF], [rPTr[pr]])
            elif kt < q0t + N // 128:
                m = kt - q0t
                k.op("dve", lambda e: e.tensor_tensor(PTr[pr][:, 0:N], PP[2 + pr][:, 0:N], DD[m][:, 0:N], op=ALU.mult), [rPP[2 + pr], rDD[m]], [rPTr[pr]])
            else:
                mb = kt - q0t
                k.op("dve", lambda e: e.scalar_tensor_tensor(PTr[pr][:, 0:N], PP[2 + pr][:, 0:N], CB[:, mb:mb + 1], BB_[:, 0:N], op0=ALU.mult, op1=ALU.mult),
                     [rPP[2 + pr], rCB, rBB], [rPTr[pr]])
            return pa, pr

        def pv(kt, i, pa, pr):
            k.op("pe", lambda e: e.matmul(PO[:, 0:N], lhsT=AV[:, kt, :], rhs=PTa[pa][:, 0:N], start=(i == 0), stop=(i == nk - 1)),
                 [rAV, rPTa[pa]], [rPO])
            k.op("pe", lambda e: e.matmul(PR[:, 0:N], lhsT=ONESb[:], rhs=PTa[pa][:, 0:N], start=(i == 0), stop=(i == nk - 1)),
                 [rONESb, rPTa[pa]], [rPR])
            k.op("pe", lambda e: e.matmul(PQ[:, 0:N], lhsT=RV[:, kt, :], rhs=PTr[pr][:, 0:N], start=(i == 0), stop=(i == nk - 1)),
                 [rRV, rPTr[pr]], [rPQ])

        bufs = {0: scores(kts[0])}
        for i, kt in enumerate(kts):
            if i + 1 < nk:
                bufs[i + 1] = scores(kts[i + 1])
            pv(kt, i, *bufs[i])
        ob = cnt["os"] % 2; cnt["os"] += 1
        k.op("dve", lambda e: e.reciprocal(RI[:, 0:N], PR[:, 0:N]), [rPR], [rRI])
        k.op("dve", lambda e, ob=ob: e.tensor_tensor(OS[ob][:, 0:N], PO[:, 0:N], RI[:, 0:N], op=ALU.mult), [rPO, rRI], [rOS[ob]])
        k.dma("sp", out_h[b, 128:256, q0:q0 + N], OS[ob][:, 0:N], reads=[rOS[ob]], writes=[rOUT])
        k.op("act", lambda e: e.copy(ORS[:, 0:N], PQ[:, 0:N]), [rPQ], [rORS])
        k.op("pe", lambda e: e.matmul(PM[:, 0:N], lhsT=ONES[:], rhs=ORS[:, 0:N], start=True, stop=True), [rONES, rORS], [rPM])
        k.op("dve", lambda e: e.scalar_tensor_tensor(CEN[:, 0:N], PM[:, 0:N], -1.0 / 128, ORS[:, 0:N], op0=ALU.mult, op1=ALU.add), [rPM, rORS], [rCEN])
        k.op("act", lambda e: e.activation(SQB[:, 0:N], CEN[:, 0:N], AF.Square), [rCEN], [rSQB])
        k.op("pe", lambda e: e.matmul(PM[:, 0:N], lhsT=ONES[:], rhs=SQB[:, 0:N], start=True, stop=True), [rONES, rSQB], [rPM])
        k.op("dve", lambda e: e.tensor_scalar(RSB[:, 0:N], PM[:, 0:N], 1.0 / 128, EPS, op0=ALU.mult, op1=ALU.add), [rPM], [rRSB])
        k.op("act", lambda e: e.activation(RSB[:, 0:N], RSB[:, 0:N], AF.Ln), [rRSB], [rRSB])
        k.op("act", lambda e: e.activation(RSB[:, 0:N], RSB[:, 0:N], AF.Exp, scale=-0.5), [rRSB], [rRSB])
        k.op("dve", lambda e: e.tensor_tensor(CEN[:, 0:N], CEN[:, 0:N], RSB[:, 0:N], op=ALU.mult), [rCEN, rRSB], [rCEN])
        ob2 = cnt["os"] % 2; cnt["os"] += 1
        k.op("dve", lambda e, ob2=ob2: e.tensor_tensor(OS[ob2][:, 0:N], CEN[:, 0:N], RG[:, q0:q0 + N], op=ALU.mult), [rCEN, rRG], [rOS[ob2]])
        k.dma("sp", out_h[b, 0:128, q0:q0 + N], OS[ob2][:, 0:N], reads=[rOS[ob2]], writes=[rOUT])

    for b in range(nbatch):
        g0 = 0
        while g0 < tb:
            is_ctx = g0 < nctx
            N = min(GA_, (nctx - g0) if is_ctx else (tb - g0))
            phase_a_group(b, g0, N, 2 if is_ctx else b, is_ctx, g0 - nctx)
            g0 += N
        k.handoff(resA, resB)
        phase_b_group(b, 0, nctx, True, 0)
        q0 = nctx
        while q0 < tb:
            N = min(512, tb - q0)
            phase_b_group(b, q0, N, False, 0)
            q0 += N
        k.handoff(resB, resA)
    counts = k.emit()
    return nc, counts


EPS = 1e-6
NCTX = 256
NLOC = 8192
TB = NCTX + NLOC
GN = 256
HL = 2
NW_ = 1296


def build_O(nbatch=2, tb=TB, nctx=NCTX):
    nc = bass.Bass("TRN2", target_bir_lowering=False)
    nkt = tb // 128
    xT_h = nc.dram_tensor("xT", [2048, nbatch * tb], F32, kind="ExternalInput").ap()
    w_h = nc.dram_tensor("w", [2048, NW_], F32, kind="ExternalInput").ap()
    mods_h = nc.dram_tensor("mods", [128, 96], F32, kind="ExternalInput").ap()
    nw_h = nc.dram_tensor("nw", [128, 16], F32, kind="ExternalInput").ap()
    cw_h = nc.dram_tensor("cw", [128, 36], F32, kind="ExternalInput").ap()
    sm_h = nc.dram_tensor("small", [128, 32], F32, kind="ExternalInput").ap()
    dn_h = nc.dram_tensor("dn", [128, 1024], F32, kind="ExternalInput").ap()
    tri_h = nc.dram_tensor("tri", [128, 640], F32, kind="ExternalInput").ap()
    out_h = nc.dram_tensor("o", [nbatch, tb, 512], F32, kind="ExternalOutput").ap()
    sX = nc.dram_tensor("sX", [nbatch * nkt, 128, 512], F32, kind="ExternalOutput").ap()
    sY = nc.dram_tensor("sY", [nbatch * nkt, 128, 512], F32, kind="ExternalOutput").ap()
    sZ = nc.dram_tensor("sZ", [nbatch * nkt, 128, 512], F32, kind="ExternalOutput").ap()
    sB = nc.dram_tensor("sB", [nbatch * nkt, 128, 384], F32, kind="ExternalOutput").ap()
    sD = nc.dram_tensor("sD", [nbatch * nkt, 128, 16], F32, kind="ExternalOutput").ap()

    k = KB(nc)
    R = k.res
    MODS = k.sb("MODS", [128, 3, 2, 16]); rMODS = R("MODS")
    NW = k.sb("NW", [128, 16]); rNW = R("NW")
    G1 = k.sb("G1", [128, 3, 16]); rG1 = R("G1")
    CW = k.sb("CW", [128, 36]); rCW = R("CW")
    SM = k.sb("SM", [128, 32]); rSM = R("SM")
    AN = k.sb("AN", [128, 16]); rAN = R("AN")
    DN = k.sb("DN", [128, 1024]); rDN = R("DN")
    TRI = k.sb("TRI", [128, 640]); rTRI = R("TRI")
    ONES = k.sb("ONES", [128, 128]); rONES = R("ONES")
    W = k.sb("W", [128, 16, NW_], BF16); rW = R("W")
    k.dma("sp", MODS[:].rearrange("p a b c -> p (a b c)"), mods_h, writes=[rMODS])
    k.dma("sp", NW[:], nw_h, writes=[rNW])
    k.dma("sp", CW[:], cw_h, writes=[rCW])
    k.dma("sp", SM[:], sm_h, writes=[rSM])
    k.dma("sp", DN[:], dn_h, writes=[rDN])
    k.dma("sp", TRI[:], tri_h, writes=[rTRI])
    for kc in range(16):
        k.dma("pool", W[:, kc, :], w_h[kc * 128:(kc + 1) * 128, :], writes=[rW])
    k.op("dve", lambda e: e.memset(ONES[:], 1.0), [], [rONES])
    for s_ in range(3):
        k.op("dve", lambda e, s_=s_: e.scalar_tensor_tensor(G1[:, s_, :], MODS[:, s_, 1, :], 1.0, NW[:], op0=ALU.add, op1=ALU.mult), [rMODS, rNW], [rG1])
    k.op("act", lambda e: e.activation(AN[:], SM[:, 16:32], AF.Exp), [rSM], [rAN])
    k.op("dve", lambda e: e.tensor_scalar(AN[:], AN[:], -1.0, None, op0=ALU.mult), [rAN], [rAN])
    TRIv = [TRI[:, 0:128], TRI[:, 128:256]]
    STRv = [TRI[:, 256:384], TRI[:, 384:512]]
    ID = TRI[:, 512:640]
    DSK = DN[:, 0:512]
    NWO = DN[:, 512:1024]
    DTB = SM[:, 0:16]
    XW = GN + 2 * HL
    XR = k.sb("XR", [128, 16 * XW])
    XRb = XR.bitcast(BF16)
    X = XR[:].rearrange("p (kc t) -> p kc t", kc=16)
    H = XRb[:].rearrange("p (kc t) -> p kc t", kc=16)[:, :, 0:XW]
    rXk = [R(f"X{i}") for i in range(16)]
    SQ = [k.sb(f"SQ{i}", [128, XW]) for i in range(2)]; rSQ = [R(f"SQ{i}") for i in range(2)]
    RS = k.sb("RS", [128, XW]); rRS = R("RS")
    TMP = [k.sb(f"TMP{i}", [128, XW]) for i in range(2)]; rTMP = [R(f"TMP{i}") for i in range(2)]
    ACC = [k.sb(f"ACC{i}", [128, GN]) for i in range(2)]; rACC = [R(f"ACC{i}") for i in range(2)]
    XC = k.sb("XC", [128, 6, GN]); rXC = [R(f"XC{i}") for i in range(6)]
    NS = 2
    XT = [k.sb(f"XT{i}", [128, 512]) for i in range(NS)]; rXT = [R(f"XT{i}") for i in range(NS)]
    YF = [k.sb(f"YF{i}", [128, 512]) for i in range(NS)]; rYF = [R(f"YF{i}") for i in range(NS)]
    ZS = [k.sb(f"ZS{i}", [128, 512]) for i in range(NS)]; rZS = [R(f"ZS{i}") for i in range(NS)]
    BC = [k.sb(f"BC{i}", [128, 384]) for i in range(NS)]; rBC = [R(f"BC{i}") for i in range(NS)]
    DT = [k.sb(f"DT{i}", [128, 16]) for i in range(NS)]; rDT = [R(f"DT{i}") for i in range(NS)]
    ST = [k.sb(f"ST{d}", [128, 512]) for d in range(2)]; rST = [R(f"ST{d}") for d in range(2)]
    DTA = k.sb("DTA", [128, 8]); rDTA = R("DTA")
    CSS = k.sb("CSS", [128, 16]); rCSS = R("CSS")
    ECS = k.sb("ECS", [128, 8]); rECS = R("ECS")
    TE = k.sb("TE", [128, 8]); rTE = R("TE")
    DEC = k.sb("DEC", [128, 8]); rDEC = R("DEC")
    CBM = k.sb("CBM", [128, 128]); rCBM = R("CBM")
    LH = [k.sb(f"LH{i}", [128, 128]) for i in range(2)]; rLH = [R(f"LH{i}") for i in range(2)]
    EX = [k.sb(f"EX{i}", [128, 128]) for i in range(2)]; rEX = [R(f"EX{i}") for i in range(2)]
    WH = [k.sb(f"WH{i}", [128, 128]) for i in range(2)]; rWH = [R(f"WH{i}") for i in range(2)]
    T1 = k.sb("T1", [128, 512]); rT1 = R("T1")
    XS = k.sb("XS", [128, 512]); rXS = R("XS")
    GB = k.sb("GB", [128, 512]); rGB = R("GB")
    GSQ = k.sb("GSQ", [128, 512]); rGSQ = R("GSQ")
    SS = k.sb("SS", [128, 1]); rSS = R("SS")
    OB = [k.sb(f"OB{i}", [128, 512]) for i in range(2)]; rOB = [R(f"OB{i}") for i in range(2)]
    PP = [k.ps(f"PP{i}") for i in range(2)]; rPP = [R(f"PP{i}") for i in range(2)]
    PMs = k.ps("PMs"); rPMs = R("PMs")
    PSm = k.ps("PSm"); rPScs = rPScb = rPSdt = rPSbk = R("PSm")
    PSg = [k.ps(f"PSg{i}") for i in range(2)]; rPSg = [R(f"PSg{i}") for i in range(2)]
    PY = k.ps("PY"); rPY = R("PY")
    PYO = k.ps("PYO"); rPYO = R("PYO")
    rOUT = R("OUT")
    rsX = [R(f"sX{i}") for i in range(NS)]; rsY = [R(f"sY{i}") for i in range(NS)]; rsZ = [R(f"sZ{i}") for i in range(NS)]; rsB = [R(f"sB{i}") for i in range(NS)]; rsD = [R(f"sD{i}") for i in range(NS)]
    cnt = dict(sq=0, tmp=0, pp=0, acc=0, set=0, lh=0, ob=0)

    def stats_rstd(srcs, N, div, ors, rr):
        nk = len(srcs)
        for i, (ap, rs_) in enumerate(srcs):
            sb_ = cnt["sq"] % 2; cnt["sq"] += 1
            k.op("act", lambda e, ap=ap, sb_=sb_: e.activation(SQ[sb_][:, 0:N], ap, AF.Square), [rs_], [rSQ[sb_]])
            k.op("pe", lambda e, i=i, sb_=sb_: e.matmul(PMs[:, 0:N], lhsT=ONES[:], rhs=SQ[sb_][:, 0:N], start=(i == 0), stop=(i == nk - 1)),
                 [rONES, rSQ[sb_]], [rPMs])
        k.op("dve", lambda e: e.tensor_scalar(ors, PMs[:, 0:N], 1.0 / div, EPS, op0=ALU.mult, op1=ALU.add), [rPMs], [rr])
        k.op("act", lambda e: e.activation(ors, ors, AF.Ln), [rr], [rr])
        k.op("act", lambda e: e.activation(ors, ors, AF.Exp, scale=-0.5), [rr], [rr])

    def ssd_dir(d, s, first_chunk):
        BT = BC[s][:, 0:128]; CT = BC[s][:, 128:256]; BK = BC[s][:, 256:384]
        dts = DT[s][:, 8 * d:8 * d + 8]
        k.op("dve", lambda e: e.tensor_tensor(DTA[:], dts, AN[:, 8 * d:8 * d + 8], op=ALU.mult), [rDT[s], rAN], [rDTA])
        k.op("pe", lambda e: e.matmul(PSm[:, 0:8], lhsT=TRIv[d], rhs=DTA[:], start=True, stop=True), [rTRI, rDTA], [rPScs])
        k.op("pe", lambda e: e.matmul(PSm[:, 8:16], lhsT=ONES[:], rhs=DTA[:], start=True, stop=True), [rONES, rDTA], [rPScs])
        k.op("act", lambda e: e.copy(CSS[:], PSm[:, 0:16]), [rPScs], [rCSS])
        k.op("act", lambda e: e.activation(ECS[:], CSS[:, 0:8], AF.Exp), [rCSS], [rECS])
        k.op("act", lambda e: e.activation(DEC[:], CSS[:, 8:16], AF.Exp), [rCSS], [rDEC])
        k.op("dve", lambda e: e.tensor_tensor(TE[:], CSS[:, 8:16], CSS[:, 0:8], op=ALU.subtract), [rCSS], [rTE])
        k.op("act", lambda e: e.activation(TE[:], TE[:], AF.Exp), [rTE], [rTE])
        k.op("dve", lambda e: e.tensor_tensor(TE[:], TE[:], dts, op=ALU.mult), [rTE, rDT[s]], [rTE])
        k.op("pe", lambda e: e.matmul(PSm[:, 128:256], lhsT=BT, rhs=CT, start=True, stop=True), [rBC[s]], [rPScb])
        k.op("dve", lambda e: e.tensor_tensor(CBM[:], PSm[:, 128:256], TRIv[d], op=ALU.mult), [rPScb, rTRI], [rCBM])
        def head_a(h):
            lb = cnt["lh"] % 2; cnt["lh"] += 1
            k.op("dve", lambda e: e.tensor_scalar(LH[lb][:], STRv[d], DTA[:, h:h + 1], None, op0=ALU.mult), [rTRI, rDTA], [rLH[lb]])
            k.op("pe", lambda e: e.matmul(PSg[lb][:, 0:128], lhsT=LH[lb][:], rhs=TRIv[d], start=True, stop=True), [rLH[lb], rTRI], [rPSg[lb]])
            k.op("act", lambda e: e.activation(EX[lb][:], PSg[lb][:, 0:128], AF.Exp), [rPSg[lb]], [rEX[lb]])
            return lb

        def head_b(h, lb):
            k.op("dve", lambda e: e.scalar_tensor_tensor(WH[lb][:], EX[lb][:], DT[s][:, 8 * d + h:8 * d + h + 1], CBM[:], op0=ALU.mult, op1=ALU.mult),
                 [rEX[lb], rDT[s], rCBM], [rWH[lb]])
            k.op("pe", lambda e: e.matmul(PY[:, h * 64:(h + 1) * 64], lhsT=WH[lb][:], rhs=XT[s][:, h * 64:(h + 1) * 64], start=True, stop=True),
                 [rWH[lb], rXT[s]], [rPY])

        lbs = {0: head_a(0)}
        for h in range(8):
            if h + 1 < 8:
                lbs[h + 1] = head_a(h + 1)
            head_b(h, lbs[h])
        if not first_chunk:
            k.op("pe", lambda e: e.matmul(PYO[:], lhsT=CT, rhs=ST[d][:], start=True, stop=True), [rBC[s], rST[d]], [rPYO])
            k.op("dve", lambda e: e.tensor_tensor(T1[:].rearrange("p (h q) -> p h q", h=8), PYO[:].rearrange("p (h q) -> p h q", h=8),
                                                  ECS[:].unsqueeze(2).to_broadcast([128, 8, 64]), op=ALU.mult), [rPYO, rECS], [rT1])
        k.op("dve", lambda e: e.tensor_tensor(XS[:].rearrange("p (h q) -> p h q", h=8), XT[s][:].rearrange("p (h q) -> p h q", h=8),
                                              TE[:].unsqueeze(2).to_broadcast([128, 8, 64]), op=ALU.mult), [rXT[s], rTE], [rXS])
        k.op("pe", lambda e: e.matmul(PYO[:], lhsT=BK, rhs=XS[:], start=True, stop=True), [rBC[s], rXS], [rPYO])
        if first_chunk:
            k.op("act", lambda e: e.copy(ST[d][:], PYO[:]), [rPYO], [rST[d]])
        else:
            k.op("dve", lambda e: e.tensor_tensor(ST[d][:].rearrange("p (h q) -> p h q", h=8), ST[d][:].rearrange("p (h q) -> p h q", h=8),
                                                  DEC[:].unsqueeze(2).to_broadcast([128, 8, 64]), op=ALU.mult), [rST[d], rDEC], [rST[d]])
            k.op("dve", lambda e: e.tensor_tensor(ST[d][:], ST[d][:], PYO[:], op=ALU.add), [rST[d], rPYO], [rST[d]])

    def group_pass1(b, seg0, seg1, g0, ms, first_group):
        N = GN
        lo = max(g0 - HL, seg0); hi = min(g0 + N + HL, seg1)
        c_lo = lo - (g0 - HL); c_hi = hi - (g0 - HL); NV = c_hi - c_lo
        col0 = b * tb
        k.dma("sp", X[:, :, c_lo:c_hi], xT_h[:, col0 + lo:col0 + hi].rearrange("(kc p) t -> p kc t", p=128), writes=rXk, dres=rXk[0])
        stats_rstd([(X[:, kc, c_lo:c_hi], rXk[kc]) for kc in range(16)], NV, 2048.0, RS[:, 0:NV], rRS)
        for kc in range(16):
            tb_ = cnt["tmp"] % 2; cnt["tmp"] += 1
            k.op("dve", lambda e, kc=kc, tb_=tb_: e.scalar_tensor_tensor(TMP[tb_][:, 0:NV], X[:, kc, c_lo:c_hi], G1[:, ms, kc:kc + 1], RS[:, 0:NV],
                                                                        op0=ALU.mult, op1=ALU.mult), [rXk[kc], rG1, rRS], [rTMP[tb_]])
            k.op("act", lambda e, kc=kc, tb_=tb_: e.activation(H[:, kc, c_lo:c_hi], TMP[tb_][:, 0:NV], AF.Identity,
                                                               bias=MODS[:, ms, 0, kc:kc + 1], scale=1.0), [rTMP[tb_], rMODS], [rXk[kc]])
        for blk in range(6):
            pp = cnt["pp"] % 2; cnt["pp"] += 1
            for kc in range(16):
                k.op("pe", lambda e, kc=kc, pp=pp, blk=blk: e.matmul(PP[pp][:, c_lo:c_hi], lhsT=W[:, kc, blk * 128:(blk + 1) * 128], rhs=H[:, kc, c_lo:c_hi],
                                                                     start=(kc == 0), stop=(kc == 15)), [rW, rXk[kc]], [rPP[pp]])
            ab = cnt["acc"] % 2; cnt["acc"] += 1
            k.op("dve", lambda e, pp=pp, blk=blk, ab=ab: e.tensor_scalar(ACC[ab][:, 0:N], PP[pp][:, HL:HL + N], CW[:, blk * 5 + 2:blk * 5 + 3], None, op0=ALU.mult),
                 [rPP[pp], rCW], [rACC[ab]])
            for tap in (0, 1, 3, 4):
                off = tap - 2
                t_lo = max(g0, lo - off); t_hi = min(g0 + N, hi - off)
                o_lo = t_lo - g0; o_hi = t_hi - g0
                i_lo = o_lo + HL + off; i_hi = o_hi + HL + off
                k.op("dve", lambda e, pp=pp, blk=blk, ab=ab, tap=tap, o_lo=o_lo, o_hi=o_hi, i_lo=i_lo, i_hi=i_hi: e.scalar_tensor_tensor(
                    ACC[ab][:, o_lo:o_hi], PP[pp][:, i_lo:i_hi], CW[:, blk * 5 + tap:blk * 5 + tap + 1], ACC[ab][:, o_lo:o_hi], op0=ALU.mult, op1=ALU.add),
                    [rPP[pp], rCW, rACC[ab]], [rACC[ab]])
            k.op("act", lambda e, blk=blk, ab=ab: e.activation(XC[:, blk, :], ACC[ab][:, 0:N], AF.Silu, bias=CW[:, 30 + blk:31 + blk], scale=1.0),
                 [rACC[ab], rCW], [rXC[blk]])
        for ch in range(N // 128):
            c = (g0 + ch * 128) // 128
            sc = b * nkt + c
            s = cnt["set"] % NS; cnt["set"] += 1
            hc0 = HL + ch * 128
            pp = cnt["pp"] % 2; cnt["pp"] += 1
            for kc in range(16):
                k.op("pe", lambda e, kc=kc, pp=pp, hc0=hc0: e.matmul(PP[pp][:], lhsT=H[:, kc, hc0:hc0 + 128], rhs=W[:, kc, 768:1280], start=(kc == 0), stop=(kc == 15)),
                     [rW, rXk[kc]], [rPP[pp]])
            k.op("act", lambda e, pp=pp, s=s: e.activation(ZS[s][:], PP[pp][:], AF.Silu), [rPP[pp]], [rZS[s]])
            k.dma("sp", sZ[sc], ZS[s][:], reads=[rZS[s]], writes=[rsZ[s]], nowaw=True)
            for kc in range(16):
                k.op("pe", lambda e, kc=kc, hc0=hc0: e.matmul(PSm[:, 256:272], lhsT=H[:, kc, hc0:hc0 + 128], rhs=W[:, kc, 1280:1296], start=(kc == 0), stop=(kc == 15)),
                     [rW, rXk[kc]], [rPSdt])
            k.op("dve", lambda e, s=s: e.tensor_tensor(DT[s][:], PSm[:, 256:272], DTB, op=ALU.add), [rPSdt, rSM], [rDT[s]])
            k.op("act", lambda e, s=s: e.activation(DT[s][:], DT[s][:], AF.Exp), [rDT[s]], [rDT[s]])
            k.op("dve", lambda e, s=s: e.tensor_scalar(DT[s][:], DT[s][:], 1.0, None, op0=ALU.add), [rDT[s]], [rDT[s]])
            k.op("act", lambda e, s=s: e.activation(DT[s][:], DT[s][:], AF.Ln), [rDT[s]], [rDT[s]])
            k.dma("sp", sD[sc], DT[s][:], reads=[rDT[s]], writes=[rsD[s]], nowaw=True)
            pp = cnt["pp"] % 2; cnt["pp"] += 1
            for q in range(4):
                k.op("pe", lambda e, q=q, ch=ch, pp=pp: e.transpose(PP[pp][:, q * 128:(q + 1) * 128], XC[:, q, ch * 128:(ch + 1) * 128], ID), [rXC[q], rTRI], [rPP[pp]])
            k.op("act", lambda e, s=s, pp=pp: e.copy(XT[s][:], PP[pp][:]), [rPP[pp]], [rXT[s]])
            k.dma("sp", sX[sc], XT[s][:], reads=[rXT[s]], writes=[rsX[s]], nowaw=True)
            k.op("dve", lambda e, s=s, ch=ch: e.tensor_copy(BC[s][:, 0:256].rearrange("p (a t) -> p a t", a=2), XC[:, 4:6, ch * 128:(ch + 1) * 128]), [rXC[4], rXC[5]], [rBC[s]])
            k.op("pe", lambda e, ch=ch: e.transpose(PSm[:, 384:512], XC[:, 4, ch * 128:(ch + 1) * 128], ID), [rXC[4], rTRI], [rPSbk])
            k.op("act", lambda e, s=s: e.copy(BC[s][:, 256:384], PSm[:, 384:512]), [rPSbk], [rBC[s]])
            k.dma("sp", sB[sc], BC[s][:], reads=[rBC[s]], writes=[rsB[s]], nowaw=True)
            fc = first_group and ch == 0
            ssd_dir(0, s, fc)
            k.op("dve", lambda e, s=s: e.tensor_tensor(YF[s][:], XT[s][:], DSK, op=ALU.mult), [rXT[s], rDN], [rYF[s]])
            if not fc:
                k.op("dve", lambda e, s=s: e.tensor_tensor(YF[s][:], YF[s][:], T1[:], op=ALU.add), [rYF[s], rT1], [rYF[s]])
            k.op("dve", lambda e, s=s: e.tensor_tensor(YF[s][:], PY[:], YF[s][:], op=ALU.add), [rPY, rYF[s]], [rYF[s]])
            k.dma("sp", sY[sc], YF[s][:], reads=[rYF[s]], writes=[rsY[s]], nowaw=True)

    def chunk_pass2(b, c, first_chunk, tok0):
        sc = b * nkt + c
        s = cnt["set"] % NS; cnt["set"] += 1
        k.dma("sp", XT[s][:], sX[sc], reads=rsX, writes=[rXT[s]])
        k.dma("sp", BC[s][:], sB[sc], reads=rsB, writes=[rBC[s]])
        k.dma("sp", DT[s][:], sD[sc], reads=rsD, writes=[rDT[s]])
        k.dma("sp", ZS[s][:], sZ[sc], reads=rsZ, writes=[rZS[s]])
        k.dma("sp", YF[s][:], sY[sc], reads=rsY, writes=[rYF[s]])
        ssd_dir(1, s, first_chunk)
        if not first_chunk:
            k.op("dve", lambda e, s=s: e.tensor_tensor(YF[s][:], YF[s][:], T1[:], op=ALU.add), [rYF[s], rT1], [rYF[s]])
        k.op("dve", lambda e, s=s: e.tensor_tensor(YF[s][:], PY[:], YF[s][:], op=ALU.add), [rPY, rYF[s]], [rYF[s]])
        k.op("dve", lambda e, s=s: e.tensor_tensor(GB[:], YF[s][:], ZS[s][:], op=ALU.mult), [rYF[s], rZS[s]], [rGB])
        k.op("act", lambda e: e.activation(GSQ[:], GB[:], AF.Square, accum_out=SS[:]), [rGB], [rGSQ, rSS])
        k.op("dve", lambda e: e.tensor_scalar(SS[:], SS[:], 1.0 / 512, EPS, op0=ALU.mult, op1=ALU.add), [rSS], [rSS])
        k.op("act", lambda e: e.activation(SS[:], SS[:], AF.Ln), [rSS], [rSS])
        k.op("act", lambda e: e.activation(SS[:], SS[:], AF.Exp, scale=-0.5), [rSS], [rSS])
        ob = cnt["ob"] % 2; cnt["ob"] += 1
        k.op("dve", lambda e, ob=ob: e.scalar_tensor_tensor(OB[ob][:], GB[:], SS[:, 0:1], NWO, op0=ALU.mult, op1=ALU.mult), [rGB, rSS, rDN], [rOB[ob]])
        k.dma("sp", out_h[b, tok0:tok0 + 128, :], OB[ob][:], reads=[rOB[ob]], writes=[rOUT])

    for b in range(nbatch):
        first = True
        for (seg0, seg1, ms) in ((0, nctx, 2), (nctx, tb, b)):
            g0 = seg0
            while g0 < seg1:
                group_pass1(b, seg0, seg1, g0, ms, first)
                first = False
                g0 += GN
        order = list(range(nctx // 128 - 1, -1, -1)) + list(range(nkt - 1, nctx // 128 - 1, -1))
        for i, c in enumerate(order):
            chunk_pass2(b, c, i == 0, c * 128)
    counts = k.emit()
    return nc, counts


_CACHE = {}


def _prog(key, fn):
    if key not in _CACHE:
        _CACHE[key] = fn()[0]
    return _CACHE[key]


def _pk(v):
    return np.ascontiguousarray(np.asarray(v, np.float32).reshape(16, 128).T)


def _rope_tabs(nloc, grid_w=64, theta=10000.0):
    rows = nloc // grid_w
    row = np.repeat(np.arange(rows, dtype=np.float32), grid_w)
    col = np.tile(np.arange(grid_w, dtype=np.float32), rows)
    inv = (np.float32(theta) ** (-np.arange(0, 64, 2, dtype=np.float32) / np.float32(64))).astype(np.float32)
    ar = row[:, None] * inv[None]
    ac = col[:, None] * inv[None]
    cosT = np.zeros((128, nloc), np.float32)
    sinT = np.zeros((128, nloc), np.float32)
    for blk, ang in ((0, ar), (1, ac)):
        c = np.cos(ang).T.astype(np.float32)
        s = np.sin(ang).T.astype(np.float32)
        cosT[blk * 64:blk * 64 + 32] = c
        cosT[blk * 64 + 32:blk * 64 + 64] = c
        sinT[blk * 64:blk * 64 + 32] = -s
        sinT[blk * 64 + 32:blk * 64 + 64] = s
    return cosT, sinT


_PERM = np.array([d + 32 if (d % 64) < 32 else d - 32 for d in range(128)])


def _tri_consts():
    j = np.arange(128)[:, None]
    i = np.arange(128)[None, :]
    return np.concatenate([(j <= i), (j >= i), (j > i), (j < i), np.eye(128, dtype=bool)], 1).astype(np.float32)


def _e_inputs(h, xT, w_in, rd, qw, kw, mods, nw, cosT, sinT):
    def blk(c0):
        return w_in[:, c0:c0 + 128]
    rq = blk(h * 128); rk = blk(1024 + h * 128); rv = blk(2048 + h * 128); rg = blk(3072 + h * 128)
    aq = blk(4096 + h * 128); ak = blk(5120 + (h // 4) * 128); av = blk(5376 + (h // 4) * 128)
    w = np.concatenate([rq, rq[:, _PERM], rk, rk[:, _PERM], rg, aq, aq[:, _PERM], ak, ak[:, _PERM], rv, av], 1)
    small = np.zeros((128, 8), np.float32)
    small[:, 0] = rd[0, h]; small[:, 1] = rd[1, h]
    small[:, 2] = qw; small[:, 3] = qw[_PERM]; small[:, 4] = kw; small[:, 5] = kw[_PERM]
    return dict(xT=xT, w=np.ascontiguousarray(w), mods=mods, nw=nw, small=small, cosT=cosT, sinT=sinT)


def _o_inputs(g, xT, w_in, conv_w, conv_b, dt_bias, a_log, d_skip, norm_w, mods, nw, tri):
    xs_c = 4096 + 512 * g; b_c = 8192 + 128 * g; c_c = 8192 + 1024 + 128 * g; z_c = 512 * g
    dt_cols = [10240 + d * 64 + 8 * g + j for d in range(2) for j in range(8)]
    w = np.concatenate([w_in[:, xs_c:xs_c + 512], w_in[:, b_c:b_c + 128], w_in[:, c_c:c_c + 128], w_in[:, z_c:z_c + 512], w_in[:, dt_cols]], 1)
    ch = np.concatenate([512 * g + np.arange(512), 4096 + 128 * g + np.arange(128), 5120 + 128 * g + np.arange(128)])
    cw = np.zeros((128, 36), np.float32)
    for blk in range(6):
        cc = ch[blk * 128:(blk + 1) * 128]
        cw[:, blk * 5:blk * 5 + 5] = conv_w[:, cc].T
        cw[:, 30 + blk] = conv_b[cc]
    small = np.zeros((128, 32), np.float32)
    small[:, 0:16] = np.concatenate([dt_bias[0, 8 * g:8 * g + 8], dt_bias[1, 8 * g:8 * g + 8]])[None]
    small[:, 16:32] = np.concatenate([a_log[0, 8 * g:8 * g + 8], a_log[1, 8 * g:8 * g + 8]])[None]
    dn = np.zeros((128, 1024), np.float32)
    dn[:, 0:512] = np.repeat(d_skip[8 * g:8 * g + 8], 64)[None]
    dn[:, 512:1024] = norm_w[512 * g:512 * g + 512][None]
    return dict(xT=xT, w=np.ascontiguousarray(w), mods=mods, nw=nw, cw=cw, small=small, dn=dn, tri=tri)


def kernel(x, c, ctx, c_ctx, ada_w, ada_b, norm1_w, norm2_w, ev_w_in, ev_w_out, ev_ret_decay,
           ev_q_norm, ev_k_norm, od_w_in, od_conv_w, od_conv_b, od_dt_bias, od_a_log, od_d,
           od_norm_w, od_w_out, peer_wq, peer_keys, peer_u, peer_v, final_norm_w):
    f32 = lambda a: np.asarray(a, dtype=np.float32)
    x = f32(x); ctx = f32(ctx); c = f32(c); c_ctx = f32(c_ctx)
    NCORE = 8
    cores = list(range(NCORE))
    DEPTH = 4
    cs = np.stack([c[0], c[1], c_ctx])
    cT = np.ascontiguousarray(cs.reshape(3, 16, 128).transpose(2, 1, 0)).reshape(128, 48)
    ncA = _prog("A", lambda: build_A(DEPTH, 1536))
    ada_w = f32(ada_w); ada_b = f32(ada_b)
    in_maps = [dict(cT=cT, w=np.ascontiguousarray(ada_w[:, :, i * 1536:(i + 1) * 1536]),
                    b=np.ascontiguousarray(ada_b[:, i * 1536:(i + 1) * 1536]).reshape(1, -1)) for i in cores]
    res = run_bass_kernel_spmd(ncA, in_maps, core_ids=cores)
    mods_all = np.zeros((DEPTH, 3, 12288), np.float32)
    for i in cores:
        m = res.results[i]["mod"].reshape(3, DEPTH, 1536)
        for l in range(DEPTH):
            mods_all[l, :, i * 1536:(i + 1) * 1536] = m[:, l]
    cosT, sinT = _rope_tabs(8192)
    tri = _tri_consts()
    ident = np.eye(128, dtype=np.float32)
    xc = ctx
    TBK = 256 + 8192
    for layer in range(DEPTH):
        last = layer == DEPTH - 1
        j = layer // 2
        mv = mods_all[layer].reshape(3, 6, 2048)
        mods_m = np.stack([np.stack([_pk(mv[r, 0]), _pk(mv[r, 1])]) for r in range(3)])
        mods_m = np.ascontiguousarray(mods_m.transpose(2, 0, 1, 3)).reshape(128, 96)
        nw1 = _pk(f32(norm1_w)[layer])
        xcat = np.concatenate([np.concatenate([xc[b], x[b]], 0) for b in range(2)], 0)
        xT_all = np.ascontiguousarray(xcat.T)
        if layer % 2 == 0:
            ncE = _prog("E", lambda: build_E(2))
            in_maps = [_e_inputs(h, xT_all, f32(ev_w_in)[j], f32(ev_ret_decay)[j], f32(ev_q_norm)[j], f32(ev_k_norm)[j], mods_m, nw1, cosT, sinT)
                       for h in cores]
            res = run_bass_kernel_spmd(ncE, in_maps, core_ids=cores)
            KO = 2048
            o_all = np.zeros((2, TBK, KO), np.float32)
            for h in cores:
                r = res.results[h]["oT"]
                for b in range(2):
                    o_all[b, :, h * 128:(h + 1) * 128] = r[b, 0:128].T
                    o_all[b, :, 1024 + h * 128:1024 + (h + 1) * 128] = r[b, 128:256].T
            w_out = f32(ev_w_out)[j]
        else:
            ncO = _prog("O", lambda: build_O(2))
            in_maps = [_o_inputs(g, xT_all, f32(od_w_in)[j], f32(od_conv_w)[j], f32(od_conv_b)[j], f32(od_dt_bias)[j], f32(od_a_log)[j],
                                 f32(od_d)[j], f32(od_norm_w)[j], mods_m, nw1, tri) for g in cores]
            res = run_bass_kernel_spmd(ncO, in_maps, core_ids=cores)
            KO = 4096
            o_all = np.zeros((2, TBK, KO), np.float32)
            for g in cores:
                r = res.results[g]["o"]
                for b in range(2):
                    o_all[b, :, g * 512:(g + 1) * 512] = r[b]
            w_out = f32(od_w_out)[j]
        del res, in_maps, xT_all, xcat
        if last:
            groups = [(0, 512, 0), (512, 512, 0), (1024, 512, 0), (1536, 512, 0)]
        else:
            groups = [(0, 512, 0), (512, 512, 0), (1024, 512, 0), (1536, 512, 0), (2048, 128, 1)]
        T = groups[-1][0] + groups[-1][1]
        KOc = KO // 128
        ncD = _prog(("D", KOc, last), lambda: build_D(KOc, groups, last))
        nw2 = np.ascontiguousarray(np.concatenate([_pk(f32(norm2_w)[layer]), _pk(f32(final_norm_w))], 1))
        keysT = np.ascontiguousarray(f32(peer_keys)[layer].transpose(3, 0, 1, 2)).reshape(128, 2048)
        uT = np.ascontiguousarray(f32(peer_u)[layer].T)
        vv = np.ascontiguousarray(f32(peer_v)[layer])
        wq = np.ascontiguousarray(f32(peer_wq)[layer])
        in_maps = []
        for ci in cores:
            b = ci // 4; q = ci % 4
            parts_x = [x[b, q * 2048:(q + 1) * 2048]]
            parts_o = [o_all[b, 256 + q * 2048:256 + (q + 1) * 2048]]
            if not last:
                parts_x += [xc[b, q * 64:(q + 1) * 64], np.zeros((64, 2048), np.float32)]
                parts_o += [o_all[b, q * 64:(q + 1) * 64], np.zeros((64, KO), np.float32)]
            xT = np.ascontiguousarray(np.concatenate(parts_x, 0).T)
            oT = np.ascontiguousarray(np.concatenate(parts_o, 0).T)
            md = np.stack([np.stack([_pk(mv[b, i]) for i in range(6)]), np.stack([_pk(mv[2, i]) for i in range(6)])])
            md = np.ascontiguousarray(md.transpose(2, 0, 1, 3)).reshape(128, 192)
            in_maps.append(dict(xT=xT, oT=oT, w_out=w_out, mods=md, nw=nw2, wq=wq, keysT=keysT, uT=uT, v=vv, ident=ident))
        res = run_bass_kernel_spmd(ncD, in_maps, core_ids=cores)
        x_new = np.zeros_like(x)
        xc_new = np.zeros_like(xc)
        for ci in cores:
            b = ci // 4; q = ci % 4
            r = res.results[ci]["outT"]
            x_new[b, q * 2048:(q + 1) * 2048] = r[:, 0:2048].T
            if not last:
                xc_new[b, q * 64:(q + 1) * 64] = r[:, 2048:2112].T
        x = x_new
        xc = xc_new
        del res, in_maps, o_all, uT, vv
    return x
```

```python
import numpy as np
from contextlib import ExitStack
import concourse.bass as bass
import concourse.mybir as mybir
from concourse.bass_utils import run_bass_kernel_spmd

F32 = mybir.dt.float32
BF16 = mybir.dt.bfloat16
AF = mybir.ActivationFunctionType
ALU = mybir.AluOpType
AX = mybir.AxisListType


class Res:
    __slots__ = ("name", "w", "rd", "dsem", "dcnt")

    def __init__(self, name):
        self.name = name
        self.w = None
        self.rd = []
        self.dsem = None
        self.dcnt = 0


class Op:
    __slots__ = ("eng", "fn", "deps", "isdma", "dres", "dval", "inc", "incval")

    def __init__(self, eng, fn, isdma=False):
        self.eng = eng
        self.fn = fn
        self.deps = []
        self.isdma = isdma
        self.dres = None
        self.dval = 0
        self.inc = False
        self.incval = 0


ENGS = ("pe", "act", "dve", "pool", "sp")


class KB:
    def __init__(self, nc):
        self.nc = nc
        self.ops = {e: [] for e in ENGS}
        self.es = ExitStack()
        self.dma_res = []
        self.nres = 0

    def sb(self, name, shape, dt=F32):
        return self.es.enter_context(self.nc.sbuf_tensor(name, list(shape), dt))

    def ps(self, name, shape=(128, 512), dt=F32):
        return self.es.enter_context(self.nc.psum_tensor(name, list(shape), dt))

    def res(self, name=None):
        self.nres += 1
        return Res(name or f"r{self.nres}")

    def handoff(self, frm, to):
        acc = []
        for f in frm:
            if f.w is not None:
                acc.append(f.w)
            acc.extend(f.rd)
        for t in to:
            t.rd = list(t.rd) + acc

    def _track(self, op, reads, writes, nowaw=False):
        deps = op.deps
        for r in reads:
            if r.w is not None:
                deps.append((r.w, "raw"))
            if not op.isdma:
                r.rd = [o for o in r.rd if o.isdma or o.eng != op.eng]
            r.rd.append(op)
        for w in writes:
            if w.w is not None and not nowaw:
                deps.append((w.w, "waw"))
            for o in w.rd:
                if o is not op:
                    deps.append((o, "war"))
            w.w = op
            w.rd = []

    def op(self, eng, fn, reads=(), writes=()):
        o = Op(eng, fn)
        self._track(o, reads, writes)
        self.ops[eng].append(o)
        return o

    def dma(self, eng, out, in_, reads=(), writes=(), dres=None, nowaw=False, **kw):
        def fn(e):
            return e.dma_start(out=out, in_=in_, **kw)
        o = Op(eng, fn, isdma=True)
        d = dres if dres is not None else writes[0]
        if d.dsem is None:
            d.dsem = self.es.enter_context(self.nc.semaphore(f"d_{d.name}_{len(self.dma_res)}"))
            self.dma_res.append(d)
        d.dcnt += 1
        o.dres = d
        o.dval = 16 * d.dcnt
        self._track(o, reads, writes, nowaw)
        self.ops[eng].append(o)
        return o

    def emit(self, final_wait_eng="sp"):
        nc = self.nc
        sems = {e: self.es.enter_context(nc.semaphore(f"s_{e}")) for e in ENGS}
        for e in ENGS:
            for o in self.ops[e]:
                for (d, kind) in o.deps:
                    if d.isdma:
                        continue
                    if d.eng == o.eng and kind != "raw":
                        continue
                    d.inc = True
        for e in ENGS:
            c = 0
            for o in self.ops[e]:
                if o.inc and not o.isdma:
                    c += 1
                    o.incval = c
        fin = [(d.dsem, 16 * d.dcnt) for d in self.dma_res]
        ops = self.ops

        def run(engname, e):
            waited = {}
            for o in ops[engname]:
                need = {}
                for (d, kind) in o.deps:
                    if d.isdma:
                        key = d.dres.dsem
                        val = d.dval
                    else:
                        if d.eng == o.eng and kind != "raw":
                            continue
                        key = sems[d.eng]
                        val = d.incval
                    if need.get(key, 0) < val:
                        need[key] = val
                for key, val in need.items():
                    if waited.get(key, 0) < val:
                        e.wait_ge(key, val)
                        waited[key] = val
                ins = o.fn(e)
                if o.isdma:
                    ins.then_inc(o.dres.dsem, 16)
                elif o.inc:
                    ins.then_inc(sems[engname], 1)
            if engname == final_wait_eng:
                for s, v in fin:
                    if waited.get(s, 0) < v:
                        e.wait_ge(s, v)

        with nc.Block() as block:
            @block.tensor
            def _(e):
                run("pe", e)

            @block.scalar
            def _(e):
                run("act", e)

            @block.vector
            def _(e):
                run("dve", e)

            @block.gpsimd
            def _(e):
                run("pool", e)

            @block.sync
            def _(e):
                run("sp", e)
        self.es.close()
        return {e: len(self.ops[e]) for e in ENGS}


def build_A(nlayer=4, ncols=1536):
    nc = bass.Bass("TRN2", target_bir_lowering=False)
    cT_h = nc.dram_tensor("cT", [128, 48], F32, kind="ExternalInput").ap()
    w_h = nc.dram_tensor("w", [nlayer, 2048, ncols], F32, kind="ExternalInput").ap()
    b_h = nc.dram_tensor("b", [1, nlayer * ncols], F32, kind="ExternalInput").ap()
    out_h = nc.dram_tensor("mod", [3, nlayer * ncols], F32, kind="ExternalOutput").ap()
    k = KB(nc)
    R = k.res
    CT = k.sb("CT", [128, 48]); rCT = R("CT")
    SCT = k.sb("SCT", [128, 48]); rSCT = R("SCT")
    BI = k.sb("BI", [1, nlayer * ncols]); rBI = R("BI")
    ON = k.sb("ON", [1, 4]); rON = R("ON")
    OUT = k.sb("OUT", [3, nlayer * ncols]); rO = R("O")
    WB = [k.sb(f"WB{i}", [128, 16, 512]) for i in range(2)]; rWB = [R(f"WB{i}") for i in range(2)]
    PS = [k.ps(f"PS{i}") for i in range(2)]; rPS = [R(f"PS{i}") for i in range(2)]
    rOUT = R("OUT")
    k.dma("sp", CT[:], cT_h, writes=[rCT])
    k.dma("sp", BI[:], b_h, writes=[rBI])
    k.op("dve", lambda e: e.memset(ON[:], 1.0), [], [rON])
    k.op("act", lambda e: e.activation(SCT[:], CT[:], AF.Silu), [rCT], [rSCT])
    i = 0
    for l in range(nlayer):
        for cc in range(ncols // 512):
            b = i % 2; i += 1
            k.dma("sp", WB[b][:], w_h[l, :, cc * 512:(cc + 1) * 512].rearrange("(kc p) j -> p kc j", p=128), writes=[rWB[b]])
            o0 = l * ncols + cc * 512
            for kc in range(16):
                k.op("pe", lambda e, b=b, kc=kc: e.matmul(PS[b][0:3, :], lhsT=SCT[:, kc * 3:kc * 3 + 3], rhs=WB[b][:, kc, :], start=(kc == 0), stop=False),
                     [rSCT, rWB[b]], [rPS[b]])
            k.op("pe", lambda e, b=b, o0=o0: e.matmul(PS[b][0:3, :], lhsT=ON[0:1, 0:3], rhs=BI[0:1, o0:o0 + 512], start=False, stop=True), [rON, rBI], [rPS[b]])
            k.op("act", lambda e, b=b, o0=o0: e.copy(OUT[0:3, o0:o0 + 512], PS[b][0:3, :]), [rPS[b]], [rO])
    k.dma("sp", out_h, OUT[:], reads=[rO], writes=[rOUT])
    counts = k.emit()
    return nc, counts


EPS = 1e-6
NE = 16384
BLK = 1024
CPB = BLK // 512
I1B = BLK // 128


def build_D(KOc, tok_groups, final, n_chunks=32):
    nc = bass.Bass("TRN2", target_bir_lowering=False)
    T = max(t0 + gn for t0, gn, _ in tok_groups)
    KO = KOc * 128
    xT_h = nc.dram_tensor("xT", [2048, T], F32, kind="ExternalInput").ap()
    oT_h = nc.dram_tensor("oT", [KO, T], F32, kind="ExternalInput").ap()
    wo_h = nc.dram_tensor("w_out", [16, 128, KO], F32, kind="ExternalInput").ap()
    mods_h = nc.dram_tensor("mods", [128, 2 * 6 * 16], F32, kind="ExternalInput").ap()
    nw_h = nc.dram_tensor("nw", [128, 32], F32, kind="ExternalInput").ap()
    wq_h = nc.dram_tensor("wq", [16, 128, 2048], F32, kind="ExternalInput").ap()
    kT_h = nc.dram_tensor("keysT", [128, 2048], F32, kind="ExternalInput").ap()
    uT_h = nc.dram_tensor("uT", [NE // 512, 128, 16 * 512], F32, kind="ExternalInput").ap()
    v_h = nc.dram_tensor("v", [NE // 512, 128, 4 * 2048], F32, kind="ExternalInput").ap()
    id_h = nc.dram_tensor("ident", [128, 128], F32, kind="ExternalInput").ap()
    out_h = nc.dram_tensor("outT", [2048, T], F32, kind="ExternalOutput").ap()

    k = KB(nc)
    R = k.res
    MODS = k.sb("MODS", [128, 2, 6, 16]); rMODS = R("MODS")
    NW = k.sb("NW", [128, 32]); rNW = R("NW")
    G2 = k.sb("G2", [128, 2, 16]); rG2 = R("G2")
    ID = k.sb("ID", [128, 128]); rID = R("ID")
    IDb = k.sb("IDb", [128, 128], BF16); rIDb = R("IDb")
    ONES = k.sb("ONES", [128, 128]); rONES = R("ONES")
    KT = k.sb("KT", [128, 2048], BF16); rKT = R("KT")
    k.dma("sp", MODS[:].rearrange("p a b c -> p (a b c)"), mods_h, writes=[rMODS])
    k.dma("sp", NW[:], nw_h, writes=[rNW])
    k.dma("sp", ID[:], id_h, writes=[rID])
    k.dma("pool", IDb[:], id_h, writes=[rIDb])
    k.dma("pool", KT[:], kT_h, writes=[rKT])
    k.op("dve", lambda e: e.memset(ONES[:], 1.0), [], [rONES])
    for ms in range(2):
        k.op("dve", lambda e, ms=ms: e.scalar_tensor_tensor(G2[:, ms, :], MODS[:, ms, 4, :], 1.0, NW[:, 0:16],
                                                          op0=ALU.add, op1=ALU.mult), [rMODS, rNW], [rG2])
    M1 = k.sb("M1", [128, 8192])
    MA = k.sb("MA", [128, 8192])
    MB = k.sb("MB", [128, 8192])
    MAb = MA.bitcast(BF16)
    MBb = MB.bitcast(BF16)
    X = M1[:].rearrange("p (kc t) -> p kc t", kc=16); rX = R("X")
    PFv = M1[:].rearrange("p (t d) -> p t d", t=4); rPF = [[R(f"PF{t}_{q}") for q in range(4)] for t in range(4)]
    rPFall = [r for rr in rPF for r in rr]
    OBv = MAb[:].rearrange("p (kc t) -> p kc t", t=512); rOB = R("OB")
    QT = MAb[:, 0:8192].rearrange("p (hs t) -> p hs t", hs=16); rQT = R("QT")
    UT = [MAb[:, i * 8192:(i + 1) * 8192].rearrange("p (kc e) -> p kc e", kc=16) for i in range(2)]; rUT = [R(f"UT{i}") for i in range(2)]
    X1R = MA[:].rearrange("p (kc t) -> p kc t", kc=16); rX1R = R("X1R")
    WO = [MBb[:, i * 4096:i * 4096 + KOc * 128].rearrange("p (kc j) -> p kc j", kc=KOc) for i in range(2)]; rWO = [R(f"WO{i}") for i in range(2)]
    WQ = [MBb[:, 8192 + i * 2048:8192 + (i + 1) * 2048].rearrange("p (kc j) -> p kc j", kc=16) for i in range(2)]; rWQ = [R(f"WQ{i}") for i in range(2)]
    VV = [MBb[:, i * 8192:(i + 1) * 8192].rearrange("p (j d) -> p j d", j=4) for i in range(2)]; rVV = [R(f"VV{i}") for i in range(2)]
    CAND = [k.sb(f"CAND{i}", [128, BLK]) for i in range(2)]; rCAND = [R(f"CAND{i}") for i in range(2)]
    EE = [k.sb(f"EE{i}", [128, BLK], BF16) for i in range(2)]; rEE = [R(f"EE{i}") for i in range(2)]
    SQ = [CAND[i] for i in range(2)]; rSQ = rCAND
    TMP = [EE[i].bitcast(F32) for i in range(2)]; rTMP = rEE
    RS = k.sb("RS", [128, 512]); rRS = R("RS")
    H2 = k.sb("H2", [128, 16, 512], BF16); rH2 = R("H2")
    S = [k.sb(f"S{i}", [128, 2048]) for i in range(4)]; rS = [R(f"S{i}") for i in range(4)]
    MR = k.sb("MR", [128, 256]); rMR = R("MR")
    MR2 = k.sb("MR2", [128, 256]); rMR2 = R("MR2")
    TS = k.sb("TS", [128, 2, 16]); rTS = R("TS")
    C256 = k.sb("C256", [128, 256]); rC256 = R("C256")
    VAL = [k.sb(f"VAL{i}", [128, 8, 24]) for i in range(4)]; rVAL = [R(f"VAL{i}") for i in range(4)]
    D16 = k.sb("D16", [128, 8, 16]); rD16 = R("D16")
    Z = k.sb("Z", [128, 8]); rZ = R("Z")
    BIAS = [k.sb(f"BIAS{i}", [128, 8]) for i in range(4)]; rBIAS = [R(f"BIAS{i}") for i in range(4)]
    THR = [k.sb(f"THR{i}", [128, 8]) for i in range(4)]; rTHR = [R(f"THR{i}") for i in range(4)]
    MH = [k.sb(f"MH{i}", [128, BLK], BF16) for i in range(2)]; rMH = [R(f"MH{i}") for i in range(2)]
    GG = [[k.sb(f"GG{i}_{j}", [128, BLK], BF16) for j in range(2)] for i in range(4)]; rGG = [[R(f"GG{i}_{j}") for j in range(2)] for i in range(4)]
    GA = [k.sb(f"GA{i}", [128, 512], BF16) for i in range(2)]; rGA = [R(f"GA{i}") for i in range(2)]
    WT_ = [k.sb(f"Wt{i}", [128, 512], BF16) for i in range(2)]; rWt = [R(f"Wt{i}") for i in range(2)]
    WTT = [k.sb(f"WTT{i}", [128, 4, 128], BF16) for i in range(2)]; rWTT = [R(f"WTT{i}") for i in range(2)]
    PO = [k.ps(f"PO{i}") for i in range(4)]; rPO = [R(f"PO{i}") for i in range(4)]
    PA = [k.ps(f"PA{i}") for i in range(2)]; rPA = [R(f"PA{i}") for i in range(2)]
    PT = k.ps("PT", (128, 512), BF16); rPT = R("PT")
    PM = k.ps("PM"); rPM = R("PM")
    rOUT = R("OUT")
    cnt = dict(wo=0, sq=0, tmp=0, wq=0, cand=0, uv=0, ga=0, po=0)
    NB = NE // BLK

    def do_group(t0, gn, ms):
        nt = gn // 128
        out_g = out_h[:, t0:t0 + gn].rearrange("(kc p) t -> p kc t", p=128)
        k.dma("sp", X[:, :, 0:gn], xT_h[:, t0:t0 + gn].rearrange("(kc p) t -> p kc t", p=128), writes=[rX])
        k.dma("pool", OBv[:, 0:KOc, 0:gn], oT_h[:, t0:t0 + gn].rearrange("(kc p) t -> p kc t", p=128), writes=[rOB])
        for dc in range(16):
            b = cnt["wo"] % 2; cnt["wo"] += 1
            k.dma("pool", WO[b], wo_h[dc].rearrange("p (kc j) -> p kc j", kc=KOc), writes=[rWO[b]])
            for kc in range(KOc):
                k.op("pe", lambda e, b=b, kc=kc: e.matmul(PM[:, 0:gn], lhsT=WO[b][:, kc, :], rhs=OBv[:, kc, 0:gn],
                                                          start=(kc == 0), stop=(kc == KOc - 1)), [rWO[b], rOB], [rPM])
            k.op("dve", lambda e, dc=dc: e.scalar_tensor_tensor(X[:, dc, 0:gn], PM[:, 0:gn], MODS[:, ms, 2, dc:dc + 1], X[:, dc, 0:gn],
                                                               op0=ALU.mult, op1=ALU.add), [rPM, rMODS, rX], [rX])
            sb_ = cnt["sq"] % 2; cnt["sq"] += 1
            k.op("act", lambda e, dc=dc, sb_=sb_: e.activation(SQ[sb_][:, 0:gn], X[:, dc, 0:gn], AF.Square), [rX], [rSQ[sb_]])
            k.op("pe", lambda e, dc=dc, sb_=sb_: e.matmul(PA[0][:, 0:gn], lhsT=ONES[:], rhs=SQ[sb_][:, 0:gn],
                                                          start=(dc == 0), stop=(dc == 15)), [rONES, rSQ[sb_]], [rPA[0]])
        k.dma("sp", out_g, X[:, :, 0:gn], reads=[rX], writes=[rOUT])
        k.op("dve", lambda e: e.tensor_scalar(RS[:, 0:gn], PA[0][:, 0:gn], 1.0 / 2048, EPS, op0=ALU.mult, op1=ALU.add), [rPA[0]], [rRS])
        k.op("act", lambda e: e.activation(RS[:, 0:gn], RS[:, 0:gn], AF.Ln), [rRS], [rRS])
        k.op("act", lambda e: e.activation(RS[:, 0:gn], RS[:, 0:gn], AF.Exp, scale=-0.5), [rRS], [rRS])
        for kc in range(16):
            tb = cnt["tmp"] % 2; cnt["tmp"] += 1
            k.op("dve", lambda e, kc=kc, tb=tb: e.scalar_tensor_tensor(TMP[tb][:, 0:gn], X[:, kc, 0:gn], G2[:, ms, kc:kc + 1], RS[:, 0:gn],
                                                                      op0=ALU.mult, op1=ALU.mult), [rX, rG2, rRS], [rTMP[tb]])
            k.op("act", lambda e, kc=kc, tb=tb: e.activation(H2[:, kc, 0:gn], TMP[tb][:, 0:gn], AF.Identity,
                                                             bias=MODS[:, ms, 3, kc:kc + 1], scale=1.0), [rTMP[tb], rMODS], [rH2])
        k.handoff([rOB], [rQT])
        k.handoff(rWO, rWQ)
        for hs in range(16):
            b = cnt["wq"] % 2; cnt["wq"] += 1
            k.dma("pool", WQ[b], wq_h[hs].rearrange("p (kc j) -> p kc j", kc=16), writes=[rWQ[b]])
            for kc in range(16):
                k.op("pe", lambda e, b=b, kc=kc: e.matmul(PM[:, 0:gn], lhsT=WQ[b][:, kc, :], rhs=H2[:, kc, 0:gn],
                                                          start=(kc == 0), stop=(kc == 15)), [rWQ[b], rH2], [rPM])
            k.op("act", lambda e, hs=hs: e.copy(QT[:, hs, 0:gn], PM[:, 0:gn]), [rPM], [rQT])
        for t in range(nt):
            for q in range(4):
                for j in range(4):
                    hs = q * 4 + j
                    k.op("pe", lambda e, t=t, q=q, j=j, hs=hs: e.matmul(PO[q][:, j * 128:(j + 1) * 128], lhsT=QT[:, hs, t * 128:(t + 1) * 128],
                                                                        rhs=KT[:, hs * 128:(hs + 1) * 128], start=True, stop=True),
                         [rQT, rKT], [rPO[q]])
                k.op("act", lambda e, t=t, q=q: e.copy(S[t][:, q * 512:(q + 1) * 512], PO[q][:]), [rPO[q]], [rS[t]])
            for h in range(8):
                for sd in range(2):
                    o0 = (h * 2 + sd) * 128
                    k.op("dve", lambda e, t=t, o0=o0, sd=sd: e.max(TS[:, sd, 0:8], S[t][:, o0:o0 + 128]), [rS[t]], [rTS])
                    k.op("dve", lambda e, t=t, o0=o0, sd=sd: e.match_replace(MR[:, 0:128], TS[:, sd, 0:8], S[t][:, o0:o0 + 128], -1e30), [rS[t], rTS], [rMR])
                    k.op("dve", lambda e, sd=sd: e.max(TS[:, sd, 8:16], MR[:, 0:128]), [rMR], [rTS])
                k.op("dve", lambda e: e.tensor_tensor(C256[:].rearrange("p (a b) -> p a b", a=16),
                                                      TS[:, 0, :].unsqueeze(2).to_broadcast([128, 16, 16]),
                                                      TS[:, 1, :].unsqueeze(1).to_broadcast([128, 16, 16]), op=ALU.add), [rTS], [rC256])
                k.op("dve", lambda e, t=t, h=h: e.max(VAL[t][:, h, 0:8], C256[:]), [rC256], [rVAL[t]])
                k.op("dve", lambda e, t=t, h=h: e.match_replace(MR[:], VAL[t][:, h, 0:8], C256[:], -1e30), [rC256, rVAL[t]], [rMR])
                k.op("dve", lambda e, t=t, h=h: e.max(VAL[t][:, h, 8:16], MR[:]), [rMR], [rVAL[t]])
                k.op("dve", lambda e, t=t, h=h: e.match_replace(MR2[:], VAL[t][:, h, 8:16], MR[:], -1e30), [rMR, rVAL[t]], [rMR2])
                k.op("dve", lambda e, t=t, h=h: e.max(VAL[t][:, h, 16:24], MR2[:]), [rMR2], [rVAL[t]])
            k.op("dve", lambda e, t=t: e.tensor_tensor(D16[:], VAL[t][:, :, 0:16], VAL[t][:, :, 0:1].to_broadcast([128, 8, 16]), op=ALU.subtract), [rVAL[t]], [rD16])
            k.op("act", lambda e: e.activation(D16[:], D16[:], AF.Exp), [rD16], [rD16])
            k.op("dve", lambda e: e.tensor_reduce(Z[:], D16[:], axis=AX.X, op=ALU.add), [rD16], [rZ])
            k.op("act", lambda e: e.activation(Z[:], Z[:], AF.Ln), [rZ], [rZ])
            k.op("dve", lambda e, t=t: e.scalar_tensor_tensor(BIAS[t][:], VAL[t][:, :, 0], -1.0, Z[:], op0=ALU.mult, op1=ALU.subtract), [rVAL[t], rZ], [rBIAS[t]])
            k.op("dve", lambda e, t=t: e.tensor_tensor(THR[t][:], VAL[t][:, :, 15], VAL[t][:, :, 16], op=ALU.add), [rVAL[t]], [rTHR[t]])
            k.op("dve", lambda e, t=t: e.tensor_scalar(THR[t][:], THR[t][:], 0.5, None, op0=ALU.mult), [rTHR[t]], [rTHR[t]])
        k.handoff([rQT], rUT)
        k.handoff(rWQ + rWO, rVV)
        k.handoff([rX], rPFall)
        blocks = [bb for bb in range(NB) if bb * CPB < n_chunks]

        def g_cand(blk, t, h):
            cb = cnt["cand"] % 2; cnt["cand"] += 1
            s1 = S[t][:, (h * 2) * 128 + blk * I1B:(h * 2) * 128 + (blk + 1) * I1B]
            s2 = S[t][:, (h * 2 + 1) * 128:(h * 2 + 2) * 128]
            k.op("dve", lambda e: e.tensor_tensor(CAND[cb][:].rearrange("p (a b) -> p a b", a=I1B),
                                                  s1.unsqueeze(2).to_broadcast([128, I1B, 128]),
                                                  s2.unsqueeze(1).to_broadcast([128, I1B, 128]), op=ALU.add), [rS[t]], [rCAND[cb]])
            k.op("act", lambda e: e.activation(EE[cb][:], CAND[cb][:], AF.Exp, bias=BIAS[t][:, h:h + 1], scale=1.0),
                 [rCAND[cb], rBIAS[t]], [rEE[cb]])
            return cb

        def g_mask(blk, t, h, cb):
            gj = blk % 2
            if h == 0:
                k.op("dve", lambda e: e.scalar_tensor_tensor(GG[t][gj][:], CAND[cb][:], THR[t][:, h:h + 1], EE[cb][:], op0=ALU.is_ge, op1=ALU.mult),
                     [rCAND[cb], rTHR[t], rEE[cb]], [rGG[t][gj]])
            else:
                k.op("dve", lambda e: e.scalar_tensor_tensor(MH[cb][:], CAND[cb][:], THR[t][:, h:h + 1], EE[cb][:], op0=ALU.is_ge, op1=ALU.mult),
                     [rCAND[cb], rTHR[t], rEE[cb]], [rMH[cb]])
                k.op("pool", lambda e: e.tensor_tensor(GG[t][gj][:], GG[t][gj][:], MH[cb][:], op=ALU.add), [rGG[t][gj], rMH[cb]], [rGG[t][gj]])

        def g_items(blk, items):
            if not items:
                return
            cbs = [None] * len(items)
            cbs[0] = g_cand(blk, *items[0])
            for i_, (t, h) in enumerate(items):
                if i_ + 1 < len(items):
                    cbs[i_ + 1] = g_cand(blk, *items[i_ + 1])
                g_mask(blk, t, h, cbs[i_])

        def chunk_load(c):
            ub = cnt["uv"] % 2; cnt["uv"] += 1
            k.dma("pool", UT[ub], uT_h[c].rearrange("p (kc e) -> p kc e", kc=16), writes=[rUT[ub]])
            k.dma("pool", VV[ub], v_h[c].rearrange("p (j d) -> p j d", j=4), writes=[rVV[ub]])
            return ub

        def chunk_act(ub, t):
            gb = cnt["ga"] % 2; cnt["ga"] += 1
            for kc in range(16):
                k.op("pe", lambda e, kc=kc: e.matmul(PA[gb][:], lhsT=H2[:, kc, t * 128:(t + 1) * 128], rhs=UT[ub][:, kc, :],
                                                     start=(kc == 0), stop=(kc == 15)), [rH2, rUT[ub]], [rPA[gb]])
            k.op("act", lambda e: e.activation(GA[gb][:], PA[gb][:], AF.Gelu), [rPA[gb]], [rGA[gb]])
            return gb

        def chunk_prod(c, ec, blk, ub, t, gb):
            gj = blk % 2
            k.op("dve", lambda e: e.tensor_tensor(WT_[gb][:], GA[gb][:], GG[t][gj][:, ec * 512:(ec + 1) * 512], op=ALU.mult),
                 [rGA[gb], rGG[t][gj]], [rWt[gb]])
            for j in range(4):
                k.op("pe", lambda e, j=j: e.transpose(PT[:, j * 128:(j + 1) * 128], WT_[gb][:, j * 128:(j + 1) * 128], IDb[:]), [rWt[gb], rIDb], [rPT])
            k.op("act", lambda e: e.copy(WTT[gb][:].rearrange("p a b -> p (a b)"), PT[:]), [rPT], [rWTT[gb]])
            pbs = []
            for dq in range(4):
                pb = cnt["po"] % 4; cnt["po"] += 1
                pbs.append(pb)
                for j in range(4):
                    k.op("pe", lambda e, j=j, dq=dq, pb=pb: e.matmul(PO[pb][:], lhsT=WTT[gb][:, j, :], rhs=VV[ub][:, j, dq * 512:(dq + 1) * 512],
                                                                     start=(j == 0), stop=(j == 3)), [rWTT[gb], rVV[ub]], [rPO[pb]])
            return pbs

        def chunk_pf(c, t, pbs):
            for dq in range(4):
                pb = pbs[dq]
                if c == 0:
                    k.op("dve", lambda e, dq=dq, pb=pb: e.tensor_copy(PFv[:, t, dq * 512:(dq + 1) * 512], PO[pb][:]), [rPO[pb]], [rPF[t][dq]])
                else:
                    k.op("dve", lambda e, dq=dq, pb=pb: e.tensor_tensor(PFv[:, t, dq * 512:(dq + 1) * 512], PO[pb][:], PFv[:, t, dq * 512:(dq + 1) * 512], op=ALU.add),
                         [rPO[pb], rPF[t][dq]], [rPF[t][dq]])

        all_g = [(t, h) for t in range(nt) for h in range(8)]
        g_items(blocks[0], all_g)
        for bi, blk in enumerate(blocks):
            nxt = blocks[bi + 1] if bi + 1 < len(blocks) else None
            chs = [(blk * CPB + ec, ec) for ec in range(CPB) if blk * CPB + ec < n_chunks]
            ubs = {c: chunk_load(c) for (c, ec) in chs}
            items = [(c, ec, t) for (c, ec) in chs for t in range(nt)]
            per = -(-len(all_g) // len(items))
            gbs = {}
            gbs[0] = chunk_act(ubs[items[0][0]], items[0][2])
            for i_, (c, ec, t) in enumerate(items):
                if i_ + 1 < len(items):
                    gbs[i_ + 1] = chunk_act(ubs[items[i_ + 1][0]], items[i_ + 1][2])
                pbs = chunk_prod(c, ec, blk, ubs[c], t, gbs[i_])
                if nxt is not None:
                    g_items(nxt, all_g[i_ * per:(i_ + 1) * per])
                chunk_pf(c, t, pbs)
        k.handoff(rUT, [rX1R])
        k.dma("sp", X1R[:, :, 0:gn], out_g, reads=[rOUT], writes=[rX1R])
        for t in range(nt):
            for dq in range(4):
                for j in range(4):
                    k.op("pe", lambda e, t=t, dq=dq, j=j: e.transpose(PM[:, j * 128:(j + 1) * 128], PFv[:, t, dq * 512 + j * 128:dq * 512 + (j + 1) * 128], ID[:]),
                         [rPF[t][dq], rID], [rPM])
                for j in range(4):
                    dc = dq * 4 + j
                    k.op("dve", lambda e, t=t, dc=dc, j=j: e.scalar_tensor_tensor(X1R[:, dc, t * 128:(t + 1) * 128], PM[:, j * 128:(j + 1) * 128], MODS[:, ms, 5, dc:dc + 1],
                                                                                 X1R[:, dc, t * 128:(t + 1) * 128], op0=ALU.mult, op1=ALU.add), [rPM, rMODS, rX1R], [rX1R])
        if final:
            for dc in range(16):
                sb_ = cnt["sq"] % 2; cnt["sq"] += 1
                k.op("act", lambda e, dc=dc, sb_=sb_: e.activation(SQ[sb_][:, 0:gn], X1R[:, dc, 0:gn], AF.Square), [rX1R], [rSQ[sb_]])
                k.op("pe", lambda e, dc=dc, sb_=sb_: e.matmul(PA[0][:, 0:gn], lhsT=ONES[:], rhs=SQ[sb_][:, 0:gn],
                                                              start=(dc == 0), stop=(dc == 15)), [rONES, rSQ[sb_]], [rPA[0]])
            k.op("dve", lambda e: e.tensor_scalar(RS[:, 0:gn], PA[0][:, 0:gn], 1.0 / 2048, EPS, op0=ALU.mult, op1=ALU.add), [rPA[0]], [rRS])
            k.op("act", lambda e: e.activation(RS[:, 0:gn], RS[:, 0:gn], AF.Ln), [rRS], [rRS])
            k.op("act", lambda e: e.activation(RS[:, 0:gn], RS[:, 0:gn], AF.Exp, scale=-0.5), [rRS], [rRS])
            for kc in range(16):
                k.op("dve", lambda e, kc=kc: e.scalar_tensor_tensor(X1R[:, kc, 0:gn], X1R[:, kc, 0:gn], NW[:, 16 + kc:17 + kc], RS[:, 0:gn],
                                                                   op0=ALU.mult, op1=ALU.mult), [rX1R, rNW, rRS], [rX1R])
        k.dma("sp", out_g, X1R[:, :, 0:gn], reads=[rX1R], writes=[rOUT])
        k.handoff([rX1R], [rOB])
        k.handoff(rVV, rWO)
        k.handoff(rPFall, [rX])

    for (t0_, gn_, ms_) in tok_groups:
        do_group(t0_, gn_, ms_)
    counts = k.emit()
    return nc, counts


EPS = 1e-6
NCTX = 256
NLOC = 8192
TB = NCTX + NLOC
NKT = TB // 128
GA_ = 256
NBLK = 11
C_RQ, C_RQP, C_RK, C_RKP, C_RG, C_AQ, C_AQP, C_AK, C_AKP, C_RV, C_AV = range(11)


def build_E(nbatch=2, tb=TB, nctx=NCTX):
    nc = bass.Bass("TRN2", target_bir_lowering=False)
    nkt = tb // 128
    nloc = tb - nctx
    xT_h = nc.dram_tensor("xT", [2048, nbatch * tb], F32, kind="ExternalInput").ap()
    w_h = nc.dram_tensor("w", [2048, NBLK * 128], F32, kind="ExternalInput").ap()
    mods_h = nc.dram_tensor("mods", [128, 3 * 2 * 16], F32, kind="ExternalInput").ap()
    nw_h = nc.dram_tensor("nw", [128, 16], F32, kind="ExternalInput").ap()
    sm_h = nc.dram_tensor("small", [128, 8], F32, kind="ExternalInput").ap()
    cos_h = nc.dram_tensor("cosT", [128, nloc], F32, kind="ExternalInput").ap()
    sin_h = nc.dram_tensor("sinT", [128, nloc], F32, kind="ExternalInput").ap()
    out_h = nc.dram_tensor("oT", [nbatch, 256, tb], F32, kind="ExternalOutput").ap()

    k = KB(nc)
    R = k.res
    MODS = k.sb("MODS", [128, 3, 2, 16]); rMODS = R("MODS")
    NW = k.sb("NW", [128, 16]); rNW = R("NW")
    SM = k.sb("SM", [128, 8]); rSM = R("SM")
    G1 = k.sb("G1", [128, 3, 16]); rG1 = R("G1")
    ONES = k.sb("ONES", [128, 128]); rONES = R("ONES")
    ONESb = k.sb("ONESb", [128, 128], BF16); rONESb = R("ONESb")
    LG = k.sb("LG", [128, 4]); rLG = R("LG")
    XR = k.sb("XR", [128, 4096])
    XRb = XR.bitcast(BF16)
    R2 = k.sb("R2", [128, 1536])
    R2b = R2.bitcast(BF16)
    IO1 = XR[:, 0:512]; rIO1 = R("IO1")
    IOM = k.sb("IOM", [128, 80]); rIOM = R("IOM")
    CF = k.sb("CF", [128, 80]); rCF = R("CF")
    CB = k.sb("CB", [128, 80]); rCB = R("CB")
    BF_ = k.sb("BF", [128, 512], BF16); rBF = R("BF")
    BB_ = k.sb("BB", [128, 512], BF16); rBB = R("BB")
    DD = [k.sb(f"DD{m}", [128, 512], BF16) for m in range(4)]; rDD = [R(f"DD{m}") for m in range(4)]
    DDf = XR[:, 1536:2048]; rDDf = R("DDf")
    T1 = XR[:, 512:1024]; rT1 = R("T1c")
    T2 = XR[:, 1024:1536]; rT2 = R("T2c")
    W = k.sb("W", [128, 16, NBLK * 128], BF16); rW = R("W")
    k.dma("sp", MODS[:].rearrange("p a b c -> p (a b c)"), mods_h, writes=[rMODS])
    k.dma("sp", NW[:], nw_h, writes=[rNW])
    k.dma("sp", SM[:], sm_h, writes=[rSM])
    for kc in range(16):
        k.dma("pool", W[:, kc, :], w_h[kc * 128:(kc + 1) * 128, :], writes=[rW])
    k.op("dve", lambda e: e.memset(ONES[:], 1.0), [], [rONES])
    k.op("dve", lambda e: e.memset(ONESb[:], 1.0), [], [rONESb])
    for s_ in range(3):
        k.op("dve", lambda e, s_=s_: e.scalar_tensor_tensor(G1[:, s_, :], MODS[:, s_, 1, :], 1.0, NW[:], op0=ALU.add, op1=ALU.mult), [rMODS, rNW], [rG1])
    k.op("act", lambda e: e.activation(LG[:, 2:4], SM[:, 0:2], AF.Exp), [rSM], [rLG])
    k.op("dve", lambda e: e.tensor_scalar(LG[:, 0:2], LG[:, 2:4], -1.0, None, op0=ALU.mult), [rLG], [rLG])
    k.op("pool", lambda e: e.iota(IO1, pattern=[[1, 512]], base=0, channel_multiplier=-1, allow_small_or_imprecise_dtypes=True), [], [rIO1])
    k.op("pool", lambda e: e.iota(IOM[:], pattern=[[128, 80]], base=0, channel_multiplier=0, allow_small_or_imprecise_dtypes=True), [], [rIOM])
    SC = 128.0 ** -0.5
    k.op("act", lambda e: e.activation(CF[:], IOM[:], AF.Exp, scale=LG[:, 0:1]), [rIOM, rLG], [rCF])
    k.op("act", lambda e: e.activation(CB[:], IOM[:], AF.Exp, scale=LG[:, 1:2]), [rIOM, rLG], [rCB])
    k.op("dve", lambda e: e.tensor_scalar(CF[:], CF[:], SC, None, op0=ALU.mult), [rCF], [rCF])
    k.op("dve", lambda e: e.tensor_scalar(CB[:], CB[:], SC, None, op0=ALU.mult), [rCB], [rCB])
    k.op("act", lambda e: e.activation(BF_[:], IO1, AF.Exp, scale=LG[:, 0:1]), [rIO1, rLG], [rBF])
    k.op("act", lambda e: e.activation(BB_[:], IO1, AF.Exp, scale=LG[:, 3:4]), [rIO1, rLG], [rBB])
    for m in range(4):
        k.op("dve", lambda e, m=m: e.tensor_scalar(T1, IO1, -128.0 * m, 0.0, op0=ALU.add, op1=ALU.max), [rIO1], [rT1])
        k.op("dve", lambda e, m=m: e.tensor_scalar(T2, IO1, -1.0, 128.0 * m, op0=ALU.mult, op1=ALU.add), [rIO1], [rT2])
        k.op("dve", lambda e: e.tensor_scalar(T2, T2, 0.0, LG[:, 1:2], op0=ALU.max, op1=ALU.mult), [rT2, rLG], [rT2])
        k.op("dve", lambda e: e.scalar_tensor_tensor(T1, T1, LG[:, 0:1], T2, op0=ALU.mult, op1=ALU.add), [rT1, rLG, rT2], [rT1])
        k.op("act", lambda e, m=m: e.activation(DDf, T1, AF.Exp), [rT1], [rDDf])
        k.op("dve", lambda e, m=m: e.tensor_scalar(T2, IO1, 128.0 * m, None, op0=ALU.is_equal), [rIO1], [rT2])
        k.op("dve", lambda e, m=m: e.tensor_tensor(DDf, DDf, T2, op=ALU.add), [rDDf, rT2], [rDDf])
        k.op("dve", lambda e, m=m: e.tensor_scalar(DD[m][:], DDf, SC, None, op0=ALU.mult), [rDDf], [rDD[m]])
    RQ = k.sb("RQ", [128, tb], BF16); rRQ = R("RQ")
    RK = k.sb("RK", [128, tb], BF16); rRK = R("RK")
    RG = k.sb("RG", [128, tb], BF16); rRG = R("RG")
    AQ = k.sb("AQ", [128, tb], BF16); rAQ = R("AQ")
    AK = k.sb("AK", [128, tb], BF16); rAK = R("AK")
    RV = k.sb("RV", [128, nkt, 128], BF16); rRV = R("RV")
    AV = k.sb("AV", [128, nkt, 128], BF16); rAV = R("AV")
    X = XR[:].rearrange("p (kc t) -> p kc t", kc=16)
    H = XRb[:].rearrange("p (kc t) -> p kc t", kc=16)[:, :, 0:GA_]
    rXk = [R(f"X{i}") for i in range(16)]
    SQ = [k.sb(f"SQ{i}", [128, GA_]) for i in range(2)]; rSQ = [R(f"SQ{i}") for i in range(2)]
    RS = k.sb("RS", [128, GA_]); rRS = R("RS")
    TMP = [k.sb(f"TMP{i}", [128, GA_]) for i in range(2)]; rTMP = [R(f"TMP{i}") for i in range(2)]
    COS = [k.sb(f"COS{i}", [128, GA_]) for i in range(1)]; rCOS = [R(f"COS{i}") for i in range(1)]
    SIN = [k.sb(f"SIN{i}", [128, GA_]) for i in range(1)]; rSIN = [R(f"SIN{i}") for i in range(1)]
    XN = R2[:, 0:512].rearrange("p (a t) -> p a t", a=2); rXN = R("XN")
    RA = R2[:, 512:768]; rRA = R("RA")
    RB = R2[:, 768:1024]; rRB = R("RB")
    RSQ = R2[:, 1024:1280]; rRSQ = R("RSQ")
    resA = rXk + [rXN, rRA, rRB, rRSQ]
    OS = [XR[:, 0:512], XR[:, 512:1024]]; rOS = [R(f"OS{i}") for i in range(2)]
    ORS = XR[:, 1024:1536]; rORS = R("ORS")
    CEN = XR[:, 1536:2048]; rCEN = R("CEN")
    SQB = XR[:, 2048:2560]; rSQB = R("SQB")
    RSB = XR[:, 2560:3072]; rRSB = R("RSB")
    RI = XR[:, 3072:3584]; rRI = R("RI")
    PTa = [XRb[:, 7168:7680], XRb[:, 7680:8192]]; rPTa = [R(f"PTa{i}") for i in range(2)]
    DC = [R2[:, 0:512], R2[:, 512:1024]]; rDC = [R(f"DC{i}") for i in range(2)]
    PTr = [R2b[:, 2048:2560], R2b[:, 2560:3072]]; rPTr = [R(f"PTr{i}") for i in range(2)]
    resB = rOS + [rORS, rCEN, rSQB, rRSB, rRI] + rPTa + rDC + rPTr
    k.handoff([rIO1, rT1, rT2, rDDf], resA)
    PP = [k.ps(f"PP{i}") for i in range(4)]; rPP = [R(f"PP{i}") for i in range(4)]
    PO = k.ps("PO"); rPO = R("PO")
    PR = k.ps("PR"); rPR = R("PR")
    PQ = k.ps("PQ"); rPQ = R("PQ")
    PM = k.ps("PM"); rPM = R("PM")
    rOUT = R("OUT")
    cnt = dict(sq=0, tmp=0, cs=0, pp=0, pta=0, ptr=0, dc=0, os=0)

    def stats_rstd(src_of_kc, nk, N, div, out_rs):
        for kc in range(nk):
            sb_ = cnt["sq"] % 2; cnt["sq"] += 1
            ap, rr = src_of_kc(kc)
            k.op("act", lambda e, ap=ap, sb_=sb_: e.activation(SQ[sb_][:, 0:N], ap, AF.Square), [rr], [rSQ[sb_]])
            k.op("pe", lambda e, kc=kc, sb_=sb_: e.matmul(PM[:, 0:N], lhsT=ONES[:], rhs=SQ[sb_][:, 0:N], start=(kc == 0), stop=(kc == nk - 1)),
                 [rONES, rSQ[sb_]], [rPM])
        ors, rr = out_rs
        k.op("dve", lambda e: e.tensor_scalar(ors, PM[:, 0:N], 1.0 / div, EPS, op0=ALU.mult, op1=ALU.add), [rPM], [rr])
        k.op("act", lambda e: e.activation(ors, ors, AF.Ln), [rr], [rr])
        k.op("act", lambda e: e.activation(ors, ors, AF.Exp, scale=-0.5), [rr], [rr])

    def phase_a_group(b, g0, N, ms, is_ctx, loc0):
        c0 = b * tb + g0
        k.dma("sp", X[:, :, 0:N], xT_h[:, c0:c0 + N].rearrange("(kc p) t -> p kc t", p=128), writes=rXk, dres=rXk[0])
        if not is_ctx:
            cb = 0
            k.dma("sp", COS[cb][:, 0:N], cos_h[:, loc0:loc0 + N], writes=[rCOS[cb]])
            k.dma("sp", SIN[cb][:, 0:N], sin_h[:, loc0:loc0 + N], writes=[rSIN[cb]])
        stats_rstd(lambda kc: (X[:, kc, 0:N], rXk[kc]), 16, N, 2048.0, (RS[:, 0:N], rRS))
        for kc in range(16):
            tb_ = cnt["tmp"] % 2; cnt["tmp"] += 1
            k.op("dve", lambda e, kc=kc, tb_=tb_: e.scalar_tensor_tensor(TMP[tb_][:, 0:N], X[:, kc, 0:N], G1[:, ms, kc:kc + 1], RS[:, 0:N],
                                                                        op0=ALU.mult, op1=ALU.mult), [rXk[kc], rG1, rRS], [rTMP[tb_]])
            k.op("act", lambda e, kc=kc, tb_=tb_: e.activation(H[:, kc, 0:N], TMP[tb_][:, 0:N], AF.Identity,
                                                               bias=MODS[:, ms, 0, kc:kc + 1], scale=1.0), [rTMP[tb_], rMODS], [rXk[kc]])

        def proj(cblk, pp, off):
            for kc in range(16):
                k.op("pe", lambda e, kc=kc: e.matmul(PP[pp][:, off:off + N], lhsT=W[:, kc, cblk * 128:(cblk + 1) * 128], rhs=H[:, kc, 0:N],
                                                     start=(kc == 0), stop=(kc == 15)), [rW, rXk[kc]], [rPP[pp]])

        def rope_store(pp, dst, rdst, srcA, srcB, rsrc):
            if is_ctx:
                k.op("act", lambda e: e.copy(dst[:, g0:g0 + N], srcA), rsrc, [rdst])
            else:
                k.op("dve", lambda e: e.tensor_tensor(RA[:, 0:N], srcA, COS[cb][:, 0:N], op=ALU.mult), rsrc + [rCOS[cb]], [rRA])
                k.op("dve", lambda e: e.tensor_tensor(RB[:, 0:N], srcB, SIN[cb][:, 0:N], op=ALU.mult), rsrc + [rSIN[cb]], [rRB])
                k.op("dve", lambda e: e.tensor_tensor(dst[:, g0:g0 + N], RA[:, 0:N], RB[:, 0:N], op=ALU.add), [rRA, rRB], [rdst])

        for (ca, cbk, dst, rdst) in ((C_RQ, C_RQP, RQ, rRQ), (C_RK, C_RKP, RK, rRK)):
            pp = cnt["pp"] % 4; cnt["pp"] += 1
            proj(ca, pp, 0)
            if not is_ctx:
                proj(cbk, pp, 256)
            rope_store(pp, dst, rdst, PP[pp][:, 0:N], PP[pp][:, 256:256 + N], [rPP[pp]])
        pp = cnt["pp"] % 4; cnt["pp"] += 1
        proj(C_RG, pp, 0)
        k.op("act", lambda e, pp=pp: e.activation(RG[:, g0:g0 + N], PP[pp][:, 0:N], AF.Silu), [rPP[pp]], [rRG])
        for (ca, cbk, dst, rdst, wc) in ((C_AQ, C_AQP, AQ, rAQ, 2), (C_AK, C_AKP, AK, rAK, 4)):
            pp = cnt["pp"] % 4; cnt["pp"] += 1
            proj(ca, pp, 0)
            if not is_ctx:
                proj(cbk, pp, 256)
            stats_rstd(lambda kc, pp=pp: (PP[pp][:, 0:N], rPP[pp]), 1, N, 128.0, (RSQ[:, 0:N], rRSQ))
            k.op("dve", lambda e, pp=pp, wc=wc: e.scalar_tensor_tensor(XN[:, 0, 0:N], PP[pp][:, 0:N], SM[:, wc:wc + 1], RSQ[:, 0:N], op0=ALU.mult, op1=ALU.mult),
                 [rPP[pp], rSM, rRSQ], [rXN])
            if not is_ctx:
                k.op("dve", lambda e, pp=pp, wc=wc: e.scalar_tensor_tensor(XN[:, 1, 0:N], PP[pp][:, 256:256 + N], SM[:, wc + 1:wc + 2], RSQ[:, 0:N], op0=ALU.mult, op1=ALU.mult),
                     [rPP[pp], rSM, rRSQ], [rXN])
            rope_store(pp, dst, rdst, XN[:, 0, 0:N], XN[:, 1, 0:N], [rXN])
        for (cv, dst, rdst) in ((C_RV, RV, rRV), (C_AV, AV, rAV)):
            pp = cnt["pp"] % 4; cnt["pp"] += 1
            for tt in range(N // 128):
                for kc in range(16):
                    k.op("pe", lambda e, kc=kc, tt=tt, pp=pp, cv=cv: e.matmul(PP[pp][:, tt * 128:(tt + 1) * 128], lhsT=H[:, kc, tt * 128:(tt + 1) * 128],
                                                                              rhs=W[:, kc, cv * 128:(cv + 1) * 128], start=(kc == 0), stop=(kc == 15)), [rW, rXk[kc]], [rPP[pp]])
            kt0 = g0 // 128
            k.op("act", lambda e, pp=pp, dst=dst, kt0=kt0: e.copy(dst[:, kt0:kt0 + N // 128, :].rearrange("p a b -> p (a b)"), PP[pp][:, 0:N]), [rPP[pp]], [rdst])

    def phase_b_group(b, q0, N, is_ctx, jloc):
        kts = list(range(nctx // 128)) if is_ctx else list(range(nkt))
        nk = len(kts)
        q0t = q0 // 128

        def scores(kt):
            pa = cnt["pta"] % 2; cnt["pta"] += 1
            k.op("pe", lambda e: e.matmul(PP[pa][:, 0:N], lhsT=AK[:, kt * 128:(kt + 1) * 128], rhs=AQ[:, q0:q0 + N], start=True, stop=True),
                 [rAK, rAQ], [rPP[pa]])
            k.op("act", lambda e: e.activation(PTa[pa][:, 0:N], PP[pa][:, 0:N], AF.Exp, scale=SC), [rPP[pa]], [rPTa[pa]])
            pr = cnt["ptr"] % 2; cnt["ptr"] += 1
            k.op("pe", lambda e: e.matmul(PP[2 + pr][:, 0:N], lhsT=RK[:, kt * 128:(kt + 1) * 128], rhs=RQ[:, q0:q0 + N], start=True, stop=True),
                 [rRK, rRQ], [rPP[2 + pr]])
            if is_ctx:
                m = kt
                k.op("dve", lambda e: e.tensor_tensor(PTr[pr][:, 0:N], PP[2 + pr][:, 0:N], DD[m][:, 0:N], op=ALU.mult), [rPP[2 + pr], rDD[m]], [rPTr[pr]])
            elif kt < nctx // 128:
                dcb = cnt["dc"] % 2; cnt["dc"] += 1
                mf = q0t - kt
                mb = nkt - q0t + kt
                k.op("dve", lambda e: e.tensor_scalar(DC[dcb][:, 0:N], BF_[:, 0:N], CF[:, mf:mf + 1], None, op0=ALU.mult), [rBF, rCF], [rDC[dcb]])
                k.op("dve", lambda e: e.scalar_tensor_tensor(DC[dcb][:, 0:N], BB_[:, 0:N], CB[:, mb:mb + 1], DC[dcb][:, 0:N], op0=ALU.mult, op1=ALU.add),
                     [rBB, rCB, rDC[dcb]], [rDC[dcb]])
                k.op("dve", lambda e: e.tensor_tensor(PTr[pr][:, 0:N], PP[2 + pr][:, 0:N], DC[dcb][:, 0:N], op=ALU.mult), [rPP[2 + pr], rDC[dcb]], [rPTr[pr]])
            elif kt < q0t:
                mf = q0t - kt
                k.op("dve", lambda e: e.scalar_tensor_tensor(PTr[pr][:, 0:N], PP[2 + pr][:, 0:N], CF[:, mf:mf + 1], BF_[:, 0:N], op0=ALU.mult, op1=ALU.mult),
                     [rPP[2 + pr], rCF, rBF], [rPTr[pr]])
            elif kt < q0t + N // 128:
                m = kt - q0t
                k.op("dve", lambda e: e.tensor_tensor(PTr[pr][:, 0:N], PP[2 + pr][:, 0:N], DD[m][:, 0:N], op=ALU.mult), [rPP[2 + pr], rDD[m]], [rPTr[pr]])
            else:
                mb = kt - q0t
                k.op("dve", lambda e: e.scalar_tensor_tensor(PTr[pr][:, 0:N], PP[2 + pr][:, 0:N], CB[:, mb:mb + 1], BB_[:, 0:N], op0=ALU.mult, op1=ALU.mult),
                     [rPP[2 + pr], rCB, rBB], [rPTr[pr]])
            return pa, pr

        def pv(kt, i, pa, pr):
            k.op("pe", lambda e: e.matmul(PO[:, 0:N], lhsT=AV[:, kt, :], rhs=PTa[pa][:, 0:N], start=(i == 0), stop=(i == nk - 1)),
                 [rAV, rPTa[pa]], [rPO])
            k.op("pe", lambda e: e.matmul(PR[:, 0:N], lhsT=ONESb[:], rhs=PTa[pa][:, 0:N], start=(i == 0), stop=(i == nk - 1)),
                 [rONESb, rPTa[pa]], [rPR])
            k.op("pe", lambda e: e.matmul(PQ[:, 0:N], lhsT=RV[:, kt, :], rhs=PTr[pr][:, 0:N], start=(i == 0), stop=(i == nk - 1)),
                 [rRV, rPTr[pr]], [rPQ])

        bufs = {0: scores(kts[0])}
        for i, kt in enumerate(kts):
            if i + 1 < nk:
                bufs[i + 1] = scores(kts[i + 1])
            pv(kt, i, *bufs[i])
        ob = cnt["os"] % 2; cnt["os"] += 1
        k.op("dve", lambda e: e.reciprocal(RI[:, 0:N], PR[:, 0:N]), [rPR], [rRI])
        k.op("dve", lambda e, ob=ob: e.tensor_tensor(OS[ob][:, 0:N], PO[:, 0:N], RI[:, 0:N], op=ALU.mult), [rPO, rRI], [rOS[ob]])
        k.dma("sp", out_h[b, 128:256, q0:q0 + N], OS[ob][:, 0:N], reads=[rOS[ob]], writes=[rOUT])
        k.op("act", lambda e: e.copy(ORS[:, 0:N], PQ[:, 0:N]), [rPQ], [rORS])
        k.op("pe", lambda e: e.matmul(PM[:, 0:N], lhsT=ONES[:], rhs=ORS[:, 0:N], start=True, stop=True), [rONES, rORS], [rPM])
        k.op("dve", lambda e: e.scalar_tensor_tensor(CEN[:, 0:N], PM[:, 0:N], -1.0 / 128, ORS[:, 0:N], op0=ALU.mult, op1=ALU.add), [rPM, rORS], [rCEN])
        k.op("act", lambda e: e.activation(SQB[:, 0:N], CEN[:, 0:N], AF.Square), [rCEN], [rSQB])
        k.op("pe", lambda e: e.matmul(PM[:, 0:N], lhsT=ONES[:], rhs=SQB[:, 0:N], start=True, stop=True), [rONES, rSQB], [rPM])
        k.op("dve", lambda e: e.tensor_scalar(RSB[:, 0:N], PM[:, 0:N], 1.0 / 128, EPS, op0=ALU.mult, op1=ALU.add), [rPM], [rRSB])
        k.op("act", lambda e: e.activation(RSB[:, 0:N], RSB[:, 0:N], AF.Ln), [rRSB], [rRSB])
        k.op("act", lambda e: e.activation(RSB[:, 0:N], RSB[:, 0:N], AF.Exp, scale=-0.5), [rRSB], [rRSB])
        k.op("dve", lambda e: e.tensor_tensor(CEN[:, 0:N], CEN[:, 0:N], RSB[:, 0:N], op=ALU.mult), [rCEN, rRSB], [rCEN])
        ob2 = cnt["os"] % 2; cnt["os"] += 1
        k.op("dve", lambda e, ob2=ob2: e.tensor_tensor(OS[ob2][:, 0:N], CEN[:, 0:N], RG[:, q0:q0 + N], op=ALU.mult), [rCEN, rRG], [rOS[ob2]])
        k.dma("sp", out_h[b, 0:128, q0:q0 + N], OS[ob2][:, 0:N], reads=[rOS[ob2]], writes=[rOUT])

    for b in range(nbatch):
        g0 = 0
        while g0 < tb:
            is_ctx = g0 < nctx
            N = min(GA_, (nctx - g0) if is_ctx else (tb - g0))
            phase_a_group(b, g0, N, 2 if is_ctx else b, is_ctx, g0 - nctx)
            g0 += N
        k.handoff(resA, resB)
        phase_b_group(b, 0, nctx, True, 0)
        q0 = nctx
        while q0 < tb:
            N = min(512, tb - q0)
            phase_b_group(b, q0, N, False, 0)
            q0 += N
        k.handoff(resB, resA)
    counts = k.emit()
    return nc, counts


EPS = 1e-6
NCTX = 256
NLOC = 8192
TB = NCTX + NLOC
GN = 256
HL = 2
NW_ = 1296


def build_O(nbatch=2, tb=TB, nctx=NCTX):
    nc = bass.Bass("TRN2", target_bir_lowering=False)
    nkt = tb // 128
    xT_h = nc.dram_tensor("xT", [2048, nbatch * tb], F32, kind="ExternalInput").ap()
    w_h = nc.dram_tensor("w", [2048, NW_], F32, kind="ExternalInput").ap()
    mods_h = nc.dram_tensor("mods", [128, 96], F32, kind="ExternalInput").ap()
    nw_h = nc.dram_tensor("nw", [128, 16], F32, kind="ExternalInput").ap()
    cw_h = nc.dram_tensor("cw", [128, 36], F32, kind="ExternalInput").ap()
    sm_h = nc.dram_tensor("small", [128, 32], F32, kind="ExternalInput").ap()
    dn_h = nc.dram_tensor("dn", [128, 1024], F32, kind="ExternalInput").ap()
    tri_h = nc.dram_tensor("tri", [128, 640], F32, kind="ExternalInput").ap()
    out_h = nc.dram_tensor("o", [nbatch, tb, 512], F32, kind="ExternalOutput").ap()
    sX = nc.dram_tensor("sX", [nbatch * nkt, 128, 512], F32, kind="ExternalOutput").ap()
    sY = nc.dram_tensor("sY", [nbatch * nkt, 128, 512], F32, kind="ExternalOutput").ap()
    sZ = nc.dram_tensor("sZ", [nbatch * nkt, 128, 512], F32, kind="ExternalOutput").ap()
    sB = nc.dram_tensor("sB", [nbatch * nkt, 128, 384], F32, kind="ExternalOutput").ap()
    sD = nc.dram_tensor("sD", [nbatch * nkt, 128, 16], F32, kind="ExternalOutput").ap()

    k = KB(nc)
    R = k.res
    MODS = k.sb("MODS", [128, 3, 2, 16]); rMODS = R("MODS")
    NW = k.sb("NW", [128, 16]); rNW = R("NW")
    G1 = k.sb("G1", [128, 3, 16]); rG1 = R("G1")
    CW = k.sb("CW", [128, 36]); rCW = R("CW")
    SM = k.sb("SM", [128, 32]); rSM = R("SM")
    AN = k.sb("AN", [128, 16]); rAN = R("AN")
    DN = k.sb("DN", [128, 1024]); rDN = R("DN")
    TRI = k.sb("TRI", [128, 640]); rTRI = R("TRI")
    ONES = k.sb("ONES", [128, 128]); rONES = R("ONES")
    W = k.sb("W", [128, 16, NW_], BF16); rW = R("W")
    k.dma("sp", MODS[:].rearrange("p a b c -> p (a b c)"), mods_h, writes=[rMODS])
    k.dma("sp", NW[:], nw_h, writes=[rNW])
    k.dma("sp", CW[:], cw_h, writes=[rCW])
    k.dma("sp", SM[:], sm_h, writes=[rSM])
    k.dma("sp", DN[:], dn_h, writes=[rDN])
    k.dma("sp", TRI[:], tri_h, writes=[rTRI])
    for kc in range(16):
        k.dma("pool", W[:, kc, :], w_h[kc * 128:(kc + 1) * 128, :], writes=[rW])
    k.op("dve", lambda e: e.memset(ONES[:], 1.0), [], [rONES])
    for s_ in range(3):
        k.op("dve", lambda e, s_=s_: e.scalar_tensor_tensor(G1[:, s_, :], MODS[:, s_, 1, :], 1.0, NW[:], op0=ALU.add, op1=ALU.mult), [rMODS, rNW], [rG1])
    k.op("act", lambda e: e.activation(AN[:], SM[:, 16:32], AF.Exp), [rSM], [rAN])
    k.op("dve", lambda e: e.tensor_scalar(AN[:], AN[:], -1.0, None, op0=ALU.mult), [rAN], [rAN])
    TRIv = [TRI[:, 0:128], TRI[:, 128:256]]
    STRv = [TRI[:, 256:384], TRI[:, 384:512]]
    ID = TRI[:, 512:640]
    DSK = DN[:, 0:512]
    NWO = DN[:, 512:1024]
    DTB = SM[:, 0:16]
    XW = GN + 2 * HL
    XRs = [k.sb(f"XR{i}", [128, 16 * XW]) for i in range(2)]
    Xs = [XRs[i][:].rearrange("p (kc t) -> p kc t", kc=16) for i in range(2)]
    Hs = [XRs[i].bitcast(BF16)[:].rearrange("p (kc t) -> p kc t", kc=16)[:, :, 0:XW] for i in range(2)]
    rXks = [[R(f"X{j}_{i}") for i in range(16)] for j in range(2)]
    SQ = [k.sb(f"SQ{i}", [128, XW]) for i in range(2)]; rSQ = [R(f"SQ{i}") for i in range(2)]
    RS = k.sb("RS", [128, XW]); rRS = R("RS")
    TMP = [k.sb(f"TMP{i}", [128, XW]) for i in range(2)]; rTMP = [R(f"TMP{i}") for i in range(2)]
    ACC = [k.sb(f"ACC{i}", [128, GN]) for i in range(2)]; rACC = [R(f"ACC{i}") for i in range(2)]
    XC = k.sb("XC", [128, 6, GN]); rXC = [R(f"XC{i}") for i in range(6)]
    NS = 2
    XT = [k.sb(f"XT{i}", [128, 512]) for i in range(NS)]; rXT = [R(f"XT{i}") for i in range(NS)]
    YF = [k.sb(f"YF{i}", [128, 512]) for i in range(NS)]; rYF = [R(f"YF{i}") for i in range(NS)]
    ZS = [k.sb(f"ZS{i}", [128, 512]) for i in range(NS)]; rZS = [R(f"ZS{i}") for i in range(NS)]
    BC = [k.sb(f"BC{i}", [128, 384]) for i in range(NS)]; rBC = [R(f"BC{i}") for i in range(NS)]
    DT = [k.sb(f"DT{i}", [128, 16]) for i in range(NS)]; rDT = [R(f"DT{i}") for i in range(NS)]
    ST = [k.sb(f"ST{d}", [128, 512]) for d in range(2)]; rST = [R(f"ST{d}") for d in range(2)]
    DTA = k.sb("DTA", [128, 8]); rDTA = R("DTA")
    CSS = k.sb("CSS", [128, 16]); rCSS = R("CSS")
    ECS = k.sb("ECS", [128, 8]); rECS = R("ECS")
    TE = k.sb("TE", [128, 8]); rTE = R("TE")
    DEC = k.sb("DEC", [128, 8]); rDEC = R("DEC")
    CBM = k.sb("CBM", [128, 128]); rCBM = R("CBM")
    LH = [k.sb(f"LH{i}", [128, 128]) for i in range(2)]; rLH = [R(f"LH{i}") for i in range(2)]
    EX = [k.sb(f"EX{i}", [128, 128]) for i in range(2)]; rEX = [R(f"EX{i}") for i in range(2)]
    WH = [k.sb(f"WH{i}", [128, 128]) for i in range(2)]; rWH = [R(f"WH{i}") for i in range(2)]
    T1 = k.sb("T1", [128, 512]); rT1 = R("T1")
    XS = k.sb("XS", [128, 512]); rXS = R("XS")
    GB = k.sb("GB", [128, 512]); rGB = R("GB")
    GSQ = k.sb("GSQ", [128, 512]); rGSQ = R("GSQ")
    SS = k.sb("SS", [128, 1]); rSS = R("SS")
    OB = [k.sb(f"OB{i}", [128, 512]) for i in range(2)]; rOB = [R(f"OB{i}") for i in range(2)]
    PP = [k.ps(f"PP{i}") for i in range(2)]; rPP = [R(f"PP{i}") for i in range(2)]
    PMs = k.ps("PMs"); rPMs = R("PMs")
    PSm = k.ps("PSm"); rPScs = rPScb = rPSdt = rPSbk = R("PSm")
    PSg = [k.ps(f"PSg{i}") for i in range(2)]; rPSg = [R(f"PSg{i}") for i in range(2)]
    PY = k.ps("PY"); rPY = R("PY")
    PYO = k.ps("PYO"); rPYO = R("PYO")
    rOUT = R("OUT")
    rsX = [R(f"sX{i}") for i in range(NS)]; rsY = [R(f"sY{i}") for i in range(NS)]; rsZ = [R(f"sZ{i}") for i in range(NS)]; rsB = [R(f"sB{i}") for i in range(NS)]; rsD = [R(f"sD{i}") for i in range(NS)]
    cnt = dict(sq=0, tmp=0, pp=0, acc=0, set=0, lh=0, ob=0, xb=0)

    def stats_rstd(srcs, N, div, ors, rr):
        nk = len(srcs)
        for i, (ap, rs_) in enumerate(srcs):
            sb_ = cnt["sq"] % 2; cnt["sq"] += 1
            k.op("act", lambda e, ap=ap, sb_=sb_: e.activation(SQ[sb_][:, 0:N], ap, AF.Square), [rs_], [rSQ[sb_]])
            k.op("pe", lambda e, i=i, sb_=sb_: e.matmul(PMs[:, 0:N], lhsT=ONES[:], rhs=SQ[sb_][:, 0:N], start=(i == 0), stop=(i == nk - 1)),
                 [rONES, rSQ[sb_]], [rPMs])
        k.op("dve", lambda e: e.tensor_scalar(ors, PMs[:, 0:N], 1.0 / div, EPS, op0=ALU.mult, op1=ALU.add), [rPMs], [rr])
        k.op("act", lambda e: e.activation(ors, ors, AF.Ln), [rr], [rr])
        k.op("act", lambda e: e.activation(ors, ors, AF.Exp, scale=-0.5), [rr], [rr])

    def ssd_dir(d, s, first_chunk):
        BT = BC[s][:, 0:128]; CT = BC[s][:, 128:256]; BK = BC[s][:, 256:384]
        dts = DT[s][:, 8 * d:8 * d + 8]
        k.op("dve", lambda e: e.tensor_tensor(DTA[:], dts, AN[:, 8 * d:8 * d + 8], op=ALU.mult), [rDT[s], rAN], [rDTA])
        k.op("pe", lambda e: e.matmul(PSm[:, 0:8], lhsT=TRIv[d], rhs=DTA[:], start=True, stop=True), [rTRI, rDTA], [rPScs])
        k.op("pe", lambda e: e.matmul(PSm[:, 8:16], lhsT=ONES[:], rhs=DTA[:], start=True, stop=True), [rONES, rDTA], [rPScs])
        k.op("act", lambda e: e.copy(CSS[:], PSm[:, 0:16]), [rPScs], [rCSS])
        k.op("act", lambda e: e.activation(ECS[:], CSS[:, 0:8], AF.Exp), [rCSS], [rECS])
        k.op("act", lambda e: e.activation(DEC[:], CSS[:, 8:16], AF.Exp), [rCSS], [rDEC])
        k.op("dve", lambda e: e.tensor_tensor(TE[:], CSS[:, 8:16], CSS[:, 0:8], op=ALU.subtract), [rCSS], [rTE])
        k.op("act", lambda e: e.activation(TE[:], TE[:], AF.Exp), [rTE], [rTE])
        k.op("dve", lambda e: e.tensor_tensor(TE[:], TE[:], dts, op=ALU.mult), [rTE, rDT[s]], [rTE])
        k.op("pe", lambda e: e.matmul(PSm[:, 128:256], lhsT=BT, rhs=CT, start=True, stop=True), [rBC[s]], [rPScb])
        k.op("dve", lambda e: e.tensor_tensor(CBM[:], PSm[:, 128:256], TRIv[d], op=ALU.mult), [rPScb, rTRI], [rCBM])
        def head_a(h):
            lb = cnt["lh"] % 2; cnt["lh"] += 1
            k.op("dve", lambda e: e.tensor_scalar(LH[lb][:], STRv[d], DTA[:, h:h + 1], None, op0=ALU.mult), [rTRI, rDTA], [rLH[lb]])
            k.op("pe", lambda e: e.matmul(PSg[lb][:, 0:128], lhsT=LH[lb][:], rhs=TRIv[d], start=True, stop=True), [rLH[lb], rTRI], [rPSg[lb]])
            k.op("act", lambda e: e.activation(EX[lb][:], PSg[lb][:, 0:128], AF.Exp), [rPSg[lb]], [rEX[lb]])
            return lb

        def head_b(h, lb):
            k.op("dve", lambda e: e.scalar_tensor_tensor(WH[lb][:], EX[lb][:], DT[s][:, 8 * d + h:8 * d + h + 1], CBM[:], op0=ALU.mult, op1=ALU.mult),
                 [rEX[lb], rDT[s], rCBM], [rWH[lb]])
            k.op("pe", lambda e: e.matmul(PY[:, h * 64:(h + 1) * 64], lhsT=WH[lb][:], rhs=XT[s][:, h * 64:(h + 1) * 64], start=True, stop=True),
                 [rWH[lb], rXT[s]], [rPY])

        lbs = {0: head_a(0)}
        for h in range(8):
            if h + 1 < 8:
                lbs[h + 1] = head_a(h + 1)
            head_b(h, lbs[h])
        if not first_chunk:
            k.op("pe", lambda e: e.matmul(PYO[:], lhsT=CT, rhs=ST[d][:], start=True, stop=True), [rBC[s], rST[d]], [rPYO])
            k.op("dve", lambda e: e.tensor_tensor(T1[:].rearrange("p (h q) -> p h q", h=8), PYO[:].rearrange("p (h q) -> p h q", h=8),
                                                  ECS[:].unsqueeze(2).to_broadcast([128, 8, 64]), op=ALU.mult), [rPYO, rECS], [rT1])
        k.op("dve", lambda e: e.tensor_tensor(XS[:].rearrange("p (h q) -> p h q", h=8), XT[s][:].rearrange("p (h q) -> p h q", h=8),
                                              TE[:].unsqueeze(2).to_broadcast([128, 8, 64]), op=ALU.mult), [rXT[s], rTE], [rXS])
        k.op("pe", lambda e: e.matmul(PYO[:], lhsT=BK, rhs=XS[:], start=True, stop=True), [rBC[s], rXS], [rPYO])
        if first_chunk:
            k.op("act", lambda e: e.copy(ST[d][:], PYO[:]), [rPYO], [rST[d]])
        else:
            k.op("dve", lambda e: e.tensor_tensor(ST[d][:].rearrange("p (h q) -> p h q", h=8), ST[d][:].rearrange("p (h q) -> p h q", h=8),
                                                  DEC[:].unsqueeze(2).to_broadcast([128, 8, 64]), op=ALU.mult), [rST[d], rDEC], [rST[d]])
            k.op("dve", lambda e: e.tensor_tensor(ST[d][:], ST[d][:], PYO[:], op=ALU.add), [rST[d], rPYO], [rST[d]])

    def group_pass1(b, seg0, seg1, g0, ms, first_group):
        N = GN
        xb = cnt["xb"] % 2; cnt["xb"] += 1
        X = Xs[xb]; H = Hs[xb]; rXk = rXks[xb]
        lo = max(g0 - HL, seg0); hi = min(g0 + N + HL, seg1)
        c_lo = lo - (g0 - HL); c_hi = hi - (g0 - HL); NV = c_hi - c_lo
        col0 = b * tb
        k.dma("sp", X[:, :, c_lo:c_hi], xT_h[:, col0 + lo:col0 + hi].rearrange("(kc p) t -> p kc t", p=128), writes=rXk, dres=rXk[0])
        stats_rstd([(X[:, kc, c_lo:c_hi], rXk[kc]) for kc in range(16)], NV, 2048.0, RS[:, 0:NV], rRS)
        for kc in range(16):
            tb_ = cnt["tmp"] % 2; cnt["tmp"] += 1
            k.op("dve", lambda e, kc=kc, tb_=tb_: e.scalar_tensor_tensor(TMP[tb_][:, 0:NV], X[:, kc, c_lo:c_hi], G1[:, ms, kc:kc + 1], RS[:, 0:NV],
                                                                        op0=ALU.mult, op1=ALU.mult), [rXk[kc], rG1, rRS], [rTMP[tb_]])
            k.op("act", lambda e, kc=kc, tb_=tb_: e.activation(H[:, kc, c_lo:c_hi], TMP[tb_][:, 0:NV], AF.Identity,
                                                               bias=MODS[:, ms, 0, kc:kc + 1], scale=1.0), [rTMP[tb_], rMODS], [rXk[kc]])
        for blk in range(6):
            pp = cnt["pp"] % 2; cnt["pp"] += 1
            for kc in range(16):
                k.op("pe", lambda e, kc=kc, pp=pp, blk=blk: e.matmul(PP[pp][:, c_lo:c_hi], lhsT=W[:, kc, blk * 128:(blk + 1) * 128], rhs=H[:, kc, c_lo:c_hi],
                                                                     start=(kc == 0), stop=(kc == 15)), [rW, rXk[kc]], [rPP[pp]])
            ab = cnt["acc"] % 2; cnt["acc"] += 1
            k.op("dve", lambda e, pp=pp, blk=blk, ab=ab: e.tensor_scalar(ACC[ab][:, 0:N], PP[pp][:, HL:HL + N], CW[:, blk * 5 + 2:blk * 5 + 3], None, op0=ALU.mult),
                 [rPP[pp], rCW], [rACC[ab]])
            for tap in (0, 1, 3, 4):
                off = tap - 2
                t_lo = max(g0, lo - off); t_hi = min(g0 + N, hi - off)
                o_lo = t_lo - g0; o_hi = t_hi - g0
                i_lo = o_lo + HL + off; i_hi = o_hi + HL + off
                k.op("dve", lambda e, pp=pp, blk=blk, ab=ab, tap=tap, o_lo=o_lo, o_hi=o_hi, i_lo=i_lo, i_hi=i_hi: e.scalar_tensor_tensor(
                    ACC[ab][:, o_lo:o_hi], PP[pp][:, i_lo:i_hi], CW[:, blk * 5 + tap:blk * 5 + tap + 1], ACC[ab][:, o_lo:o_hi], op0=ALU.mult, op1=ALU.add),
                    [rPP[pp], rCW, rACC[ab]], [rACC[ab]])
            k.op("act", lambda e, blk=blk, ab=ab: e.activation(XC[:, blk, :], ACC[ab][:, 0:N], AF.Silu, bias=CW[:, 30 + blk:31 + blk], scale=1.0),
                 [rACC[ab], rCW], [rXC[blk]])
        for ch in range(N // 128):
            c = (g0 + ch * 128) // 128
            sc = b * nkt + c
            s = cnt["set"] % NS; cnt["set"] += 1
            hc0 = HL + ch * 128
            pp = cnt["pp"] % 2; cnt["pp"] += 1
            for kc in range(16):
                k.op("pe", lambda e, kc=kc, pp=pp, hc0=hc0: e.matmul(PP[pp][:], lhsT=H[:, kc, hc0:hc0 + 128], rhs=W[:, kc, 768:1280], start=(kc == 0), stop=(kc == 15)),
                     [rW, rXk[kc]], [rPP[pp]])
            k.op("act", lambda e, pp=pp, s=s: e.activation(ZS[s][:], PP[pp][:], AF.Silu), [rPP[pp]], [rZS[s]])
            k.dma("sp", sZ[sc], ZS[s][:], reads=[rZS[s]], writes=[rsZ[s]], nowaw=True)
            for kc in range(16):
                k.op("pe", lambda e, kc=kc, hc0=hc0: e.matmul(PSm[:, 256:272], lhsT=H[:, kc, hc0:hc0 + 128], rhs=W[:, kc, 1280:1296], start=(kc == 0), stop=(kc == 15)),
                     [rW, rXk[kc]], [rPSdt])
            k.op("dve", lambda e, s=s: e.tensor_tensor(DT[s][:], PSm[:, 256:272], DTB, op=ALU.add), [rPSdt, rSM], [rDT[s]])
            k.op("act", lambda e, s=s: e.activation(DT[s][:], DT[s][:], AF.Exp), [rDT[s]], [rDT[s]])
            k.op("dve", lambda e, s=s: e.tensor_scalar(DT[s][:], DT[s][:], 1.0, None, op0=ALU.add), [rDT[s]], [rDT[s]])
            k.op("act", lambda e, s=s: e.activation(DT[s][:], DT[s][:], AF.Ln), [rDT[s]], [rDT[s]])
            k.dma("sp", sD[sc], DT[s][:], reads=[rDT[s]], writes=[rsD[s]], nowaw=True)
            pp = cnt["pp"] % 2; cnt["pp"] += 1
            for q in range(4):
                k.op("pe", lambda e, q=q, ch=ch, pp=pp: e.transpose(PP[pp][:, q * 128:(q + 1) * 128], XC[:, q, ch * 128:(ch + 1) * 128], ID), [rXC[q], rTRI], [rPP[pp]])
            k.op("act", lambda e, s=s, pp=pp: e.copy(XT[s][:], PP[pp][:]), [rPP[pp]], [rXT[s]])
            k.dma("sp", sX[sc], XT[s][:], reads=[rXT[s]], writes=[rsX[s]], nowaw=True)
            k.op("dve", lambda e, s=s, ch=ch: e.tensor_copy(BC[s][:, 0:256].rearrange("p (a t) -> p a t", a=2), XC[:, 4:6, ch * 128:(ch + 1) * 128]), [rXC[4], rXC[5]], [rBC[s]])
            k.op("pe", lambda e, ch=ch: e.transpose(PSm[:, 384:512], XC[:, 4, ch * 128:(ch + 1) * 128], ID), [rXC[4], rTRI], [rPSbk])
            k.op("act", lambda e, s=s: e.copy(BC[s][:, 256:384], PSm[:, 384:512]), [rPSbk], [rBC[s]])
            k.dma("sp", sB[sc], BC[s][:], reads=[rBC[s]], writes=[rsB[s]], nowaw=True)
            fc = first_group and ch == 0
            ssd_dir(0, s, fc)
            k.op("dve", lambda e, s=s: e.tensor_tensor(YF[s][:], XT[s][:], DSK, op=ALU.mult), [rXT[s], rDN], [rYF[s]])
            if not fc:
                k.op("dve", lambda e, s=s: e.tensor_tensor(YF[s][:], YF[s][:], T1[:], op=ALU.add), [rYF[s], rT1], [rYF[s]])
            k.op("dve", lambda e, s=s: e.tensor_tensor(YF[s][:], PY[:], YF[s][:], op=ALU.add), [rPY, rYF[s]], [rYF[s]])
            k.dma("sp", sY[sc], YF[s][:], reads=[rYF[s]], writes=[rsY[s]], nowaw=True)

    def chunk_pass2(b, c, first_chunk, tok0):
        sc = b * nkt + c
        s = cnt["set"] % NS; cnt["set"] += 1
        k.dma("sp", XT[s][:], sX[sc], reads=rsX, writes=[rXT[s]])
        k.dma("sp", BC[s][:], sB[sc], reads=rsB, writes=[rBC[s]])
        k.dma("sp", DT[s][:], sD[sc], reads=rsD, writes=[rDT[s]])
        k.dma("sp", ZS[s][:], sZ[sc], reads=rsZ, writes=[rZS[s]])
        k.dma("sp", YF[s][:], sY[sc], reads=rsY, writes=[rYF[s]])
        ssd_dir(1, s, first_chunk)
        if not first_chunk:
            k.op("dve", lambda e, s=s: e.tensor_tensor(YF[s][:], YF[s][:], T1[:], op=ALU.add), [rYF[s], rT1], [rYF[s]])
        k.op("dve", lambda e, s=s: e.tensor_tensor(YF[s][:], PY[:], YF[s][:], op=ALU.add), [rPY, rYF[s]], [rYF[s]])
        k.op("dve", lambda e, s=s: e.tensor_tensor(GB[:], YF[s][:], ZS[s][:], op=ALU.mult), [rYF[s], rZS[s]], [rGB])
        k.op("act", lambda e: e.activation(GSQ[:], GB[:], AF.Square, accum_out=SS[:]), [rGB], [rGSQ, rSS])
        k.op("dve", lambda e: e.tensor_scalar(SS[:], SS[:], 1.0 / 512, EPS, op0=ALU.mult, op1=ALU.add), [rSS], [rSS])
        k.op("act", lambda e: e.activation(SS[:], SS[:], AF.Ln), [rSS], [rSS])
        k.op("act", lambda e: e.activation(SS[:], SS[:], AF.Exp, scale=-0.5), [rSS], [rSS])
        ob = cnt["ob"] % 2; cnt["ob"] += 1
        k.op("dve", lambda e, ob=ob: e.scalar_tensor_tensor(OB[ob][:], GB[:], SS[:, 0:1], NWO, op0=ALU.mult, op1=ALU.mult), [rGB, rSS, rDN], [rOB[ob]])
        k.dma("sp", out_h[b, tok0:tok0 + 128, :], OB[ob][:], reads=[rOB[ob]], writes=[rOUT])

    for b in range(nbatch):
        first = True
        for (seg0, seg1, ms) in ((0, nctx, 2), (nctx, tb, b)):
            g0 = seg0
            while g0 < seg1:
                group_pass1(b, seg0, seg1, g0, ms, first)
                first = False
                g0 += GN
        order = list(range(nctx // 128 - 1, -1, -1)) + list(range(nkt - 1, nctx // 128 - 1, -1))
        for i, c in enumerate(order):
            chunk_pass2(b, c, i == 0, c * 128)
    counts = k.emit()
    return nc, counts


_CACHE = {}


def _prog(key, fn):
    if key not in _CACHE:
        _CACHE[key] = fn()[0]
    return _CACHE[key]


def _pk(v):
    return np.ascontiguousarray(np.asarray(v, np.float32).reshape(16, 128).T)


def _rope_tabs(nloc, grid_w=64, theta=10000.0):
    rows = nloc // grid_w
    row = np.repeat(np.arange(rows, dtype=np.float32), grid_w)
    col = np.tile(np.arange(grid_w, dtype=np.float32), rows)
    inv = (np.float32(theta) ** (-np.arange(0, 64, 2, dtype=np.float32) / np.float32(64))).astype(np.float32)
    ar = row[:, None] * inv[None]
    ac = col[:, None] * inv[None]
    cosT = np.zeros((128, nloc), np.float32)
    sinT = np.zeros((128, nloc), np.float32)
    for blk, ang in ((0, ar), (1, ac)):
        c = np.cos(ang).T.astype(np.float32)
        s = np.sin(ang).T.astype(np.float32)
        cosT[blk * 64:blk * 64 + 32] = c
        cosT[blk * 64 + 32:blk * 64 + 64] = c
        sinT[blk * 64:blk * 64 + 32] = -s
        sinT[blk * 64 + 32:blk * 64 + 64] = s
    return cosT, sinT


_PERM = np.array([d + 32 if (d % 64) < 32 else d - 32 for d in range(128)])


def _tri_consts():
    j = np.arange(128)[:, None]
    i = np.arange(128)[None, :]
    return np.concatenate([(j <= i), (j >= i), (j > i), (j < i), np.eye(128, dtype=bool)], 1).astype(np.float32)


def _e_inputs(h, xT, w_in, rd, qw, kw, mods, nw, cosT, sinT):
    def blk(c0):
        return w_in[:, c0:c0 + 128]
    rq = blk(h * 128); rk = blk(1024 + h * 128); rv = blk(2048 + h * 128); rg = blk(3072 + h * 128)
    aq = blk(4096 + h * 128); ak = blk(5120 + (h // 4) * 128); av = blk(5376 + (h // 4) * 128)
    w = np.concatenate([rq, rq[:, _PERM], rk, rk[:, _PERM], rg, aq, aq[:, _PERM], ak, ak[:, _PERM], rv, av], 1)
    small = np.zeros((128, 8), np.float32)
    small[:, 0] = rd[0, h]; small[:, 1] = rd[1, h]
    small[:, 2] = qw; small[:, 3] = qw[_PERM]; small[:, 4] = kw; small[:, 5] = kw[_PERM]
    return dict(xT=xT, w=np.ascontiguousarray(w), mods=mods, nw=nw, small=small, cosT=cosT, sinT=sinT)


def _o_inputs(g, xT, w_in, conv_w, conv_b, dt_bias, a_log, d_skip, norm_w, mods, nw, tri):
    xs_c = 4096 + 512 * g; b_c = 8192 + 128 * g; c_c = 8192 + 1024 + 128 * g; z_c = 512 * g
    dt_cols = [10240 + d * 64 + 8 * g + j for d in range(2) for j in range(8)]
    w = np.concatenate([w_in[:, xs_c:xs_c + 512], w_in[:, b_c:b_c + 128], w_in[:, c_c:c_c + 128], w_in[:, z_c:z_c + 512], w_in[:, dt_cols]], 1)
    ch = np.concatenate([512 * g + np.arange(512), 4096 + 128 * g + np.arange(128), 5120 + 128 * g + np.arange(128)])
    cw = np.zeros((128, 36), np.float32)
    for blk in range(6):
        cc = ch[blk * 128:(blk + 1) * 128]
        cw[:, blk * 5:blk * 5 + 5] = conv_w[:, cc].T
        cw[:, 30 + blk] = conv_b[cc]
    small = np.zeros((128, 32), np.float32)
    small[:, 0:16] = np.concatenate([dt_bias[0, 8 * g:8 * g + 8], dt_bias[1, 8 * g:8 * g + 8]])[None]
    small[:, 16:32] = np.concatenate([a_log[0, 8 * g:8 * g + 8], a_log[1, 8 * g:8 * g + 8]])[None]
    dn = np.zeros((128, 1024), np.float32)
    dn[:, 0:512] = np.repeat(d_skip[8 * g:8 * g + 8], 64)[None]
    dn[:, 512:1024] = norm_w[512 * g:512 * g + 512][None]
    return dict(xT=xT, w=np.ascontiguousarray(w), mods=mods, nw=nw, cw=cw, small=small, dn=dn, tri=tri)


def kernel(x, c, ctx, c_ctx, ada_w, ada_b, norm1_w, norm2_w, ev_w_in, ev_w_out, ev_ret_decay,
           ev_q_norm, ev_k_norm, od_w_in, od_conv_w, od_conv_b, od_dt_bias, od_a_log, od_d,
           od_norm_w, od_w_out, peer_wq, peer_keys, peer_u, peer_v, final_norm_w):
    f32 = lambda a: np.asarray(a, dtype=np.float32)
    x = f32(x); ctx = f32(ctx); c = f32(c); c_ctx = f32(c_ctx)
    NCORE = 8
    cores = list(range(NCORE))
    DEPTH = 4
    cs = np.stack([c[0], c[1], c_ctx])
    cT = np.ascontiguousarray(cs.reshape(3, 16, 128).transpose(2, 1, 0)).reshape(128, 48)
    ncA = _prog("A", lambda: build_A(DEPTH, 1536))
    ada_w = f32(ada_w); ada_b = f32(ada_b)
    in_maps = [dict(cT=cT, w=np.ascontiguousarray(ada_w[:, :, i * 1536:(i + 1) * 1536]),
                    b=np.ascontiguousarray(ada_b[:, i * 1536:(i + 1) * 1536]).reshape(1, -1)) for i in cores]
    res = run_bass_kernel_spmd(ncA, in_maps, core_ids=cores)
    mods_all = np.zeros((DEPTH, 3, 12288), np.float32)
    for i in cores:
        m = res.results[i]["mod"].reshape(3, DEPTH, 1536)
        for l in range(DEPTH):
            mods_all[l, :, i * 1536:(i + 1) * 1536] = m[:, l]
    cosT, sinT = _rope_tabs(8192)
    tri = _tri_consts()
    ident = np.eye(128, dtype=np.float32)
    xc = ctx
    TBK = 256 + 8192
    for layer in range(DEPTH):
        last = layer == DEPTH - 1
        j = layer // 2
        mv = mods_all[layer].reshape(3, 6, 2048)
        mods_m = np.stack([np.stack([_pk(mv[r, 0]), _pk(mv[r, 1])]) for r in range(3)])
        mods_m = np.ascontiguousarray(mods_m.transpose(2, 0, 1, 3)).reshape(128, 96)
        nw1 = _pk(f32(norm1_w)[layer])
        xcat = np.concatenate([np.concatenate([xc[b], x[b]], 0) for b in range(2)], 0)
        xT_all = np.ascontiguousarray(xcat.T)
        if layer % 2 == 0:
            ncE = _prog("E", lambda: build_E(2))
            in_maps = [_e_inputs(h, xT_all, f32(ev_w_in)[j], f32(ev_ret_decay)[j], f32(ev_q_norm)[j], f32(ev_k_norm)[j], mods_m, nw1, cosT, sinT)
                       for h in cores]
            res = run_bass_kernel_spmd(ncE, in_maps, core_ids=cores)
            KO = 2048
            o_all = np.zeros((2, TBK, KO), np.float32)
            for h in cores:
                r = res.results[h]["oT"]
                for b in range(2):
                    o_all[b, :, h * 128:(h + 1) * 128] = r[b, 0:128].T
                    o_all[b, :, 1024 + h * 128:1024 + (h + 1) * 128] = r[b, 128:256].T
            w_out = f32(ev_w_out)[j]
        else:
            ncO = _prog("O", lambda: build_O(2))
            in_maps = [_o_inputs(g, xT_all, f32(od_w_in)[j], f32(od_conv_w)[j], f32(od_conv_b)[j], f32(od_dt_bias)[j], f32(od_a_log)[j],
                                 f32(od_d)[j], f32(od_norm_w)[j], mods_m, nw1, tri) for g in cores]
            res = run_bass_kernel_spmd(ncO, in_maps, core_ids=cores)
            KO = 4096
            o_all = np.zeros((2, TBK, KO), np.float32)
            for g in cores:
                r = res.results[g]["o"]
                for b in range(2):
                    o_all[b, :, g * 512:(g + 1) * 512] = r[b]
            w_out = f32(od_w_out)[j]
        del res, in_maps, xT_all, xcat
        if last:
            groups = [(0, 512, 0), (512, 512, 0), (1024, 512, 0), (1536, 512, 0)]
        else:
            groups = [(0, 512, 0), (512, 512, 0), (1024, 512, 0), (1536, 512, 0), (2048, 128, 1)]
        T = groups[-1][0] + groups[-1][1]
        KOc = KO // 128
        ncD = _prog(("D", KOc, last), lambda: build_D(KOc, groups, last))
        nw2 = np.ascontiguousarray(np.concatenate([_pk(f32(norm2_w)[layer]), _pk(f32(final_norm_w))], 1))
        keysT = np.ascontiguousarray(f32(peer_keys)[layer].transpose(3, 0, 1, 2)).reshape(128, 2048)
        uT = np.ascontiguousarray(f32(peer_u)[layer].reshape(32, 512, 16, 128).transpose(0, 3, 2, 1)).reshape(32, 128, 8192)
        vv = np.ascontiguousarray(f32(peer_v)[layer].reshape(32, 4, 128, 2048).transpose(0, 2, 1, 3)).reshape(32, 128, 8192)
        wq = np.ascontiguousarray(f32(peer_wq)[layer].reshape(16, 128, 16, 128).transpose(2, 1, 0, 3)).reshape(16, 128, 2048)
        w_out = np.ascontiguousarray(w_out.reshape(KOc, 128, 16, 128).transpose(2, 1, 0, 3)).reshape(16, 128, KO)
        in_maps = []
        for ci in cores:
            b = ci // 4; q = ci % 4
            parts_x = [x[b, q * 2048:(q + 1) * 2048]]
            parts_o = [o_all[b, 256 + q * 2048:256 + (q + 1) * 2048]]
            if not last:
                parts_x += [xc[b, q * 64:(q + 1) * 64], np.zeros((64, 2048), np.float32)]
                parts_o += [o_all[b, q * 64:(q + 1) * 64], np.zeros((64, KO), np.float32)]
            xT = np.ascontiguousarray(np.concatenate(parts_x, 0).T)
            oT = np.ascontiguousarray(np.concatenate(parts_o, 0).T)
            md = np.stack([np.stack([_pk(mv[b, i]) for i in range(6)]), np.stack([_pk(mv[2, i]) for i in range(6)])])
            md = np.ascontiguousarray(md.transpose(2, 0, 1, 3)).reshape(128, 192)
            in_maps.append(dict(xT=xT, oT=oT, w_out=w_out, mods=md, nw=nw2, wq=wq, keysT=keysT, uT=uT, v=vv, ident=ident))
        res = run_bass_kernel_spmd(ncD, in_maps, core_ids=cores)
        x_new = np.zeros_like(x)
        xc_new = np.zeros_like(xc)
        for ci in cores:
            b = ci // 4; q = ci % 4
            r = res.results[ci]["outT"]
            x_new[b, q * 2048:(q + 1) * 2048] = r[:, 0:2048].T
            if not last:
                xc_new[b, q * 64:(q + 1) * 64] = r[:, 2048:2112].T
        x = x_new
        xc = xc_new
        del res, in_maps, o_all, uT, vv
    return x
```
